# Optimizing a Trainium2 kernel written in Bass

```python
import math
import jax, jax.numpy as jnp
from jax import lax
import numpy as np

D_MODEL = 1024
BATCH = 8
SEQ = 8192
DEPTH = 1

D_MIX = D_MODEL
RWKV_W = D_MIX // 2
RWKV_N = 64
RWKV_H = RWKV_W // RWKV_N
R_DECAY = 32
R_AAA = 32
R_GATE = 96
DIFF_W = D_MIX - RWKV_W
DIFF_H = 4
DIFF_D = DIFF_W // DIFF_H // 2
N_SHIFT = 3 * RWKV_W + R_DECAY + R_AAA + R_GATE
N_IN = N_SHIFT + 3 * DIFF_W
MEM_LEN = 256
CROSS_H = 4
CROSS_D = D_MODEL // CROSS_H
D_FF = 2816
CONV_W = 3
ROPE_THETA = 10000.0
Q_BLOCK = 128
NORM_EPS = 1e-6
LNX_EPS = 64e-5
SUBLN_EPS = 1e-5

kernel_name = "hybrid_rwkv7_diffattn_memxattn_convffn"

F32 = jnp.float32


def rms_norm(x, g, eps=NORM_EPS):
    xf = x.astype(F32)
    y = xf * lax.rsqrt(jnp.mean(xf * xf, axis=-1, keepdims=True) + eps)
    return (y * g.astype(F32)).astype(x.dtype)


def rope(x, positions):
    d = x.shape[-1]
    inv = ROPE_THETA ** (-jnp.arange(0, d, 2, dtype=F32) / d)
    ang = positions.astype(F32)[..., None] * inv
    ang = ang.reshape(ang.shape[:2] + (1,) * (x.ndim - 3) + ang.shape[-1:])
    cos, sin = jnp.cos(ang), jnp.sin(ang)
    xf = x.astype(F32)
    x1, x2 = xf[..., : d // 2], xf[..., d // 2:]
    return jnp.concatenate([x1 * cos - x2 * sin, x2 * cos + x1 * sin], axis=-1).astype(x.dtype)


def token_shift(z):
    return jnp.pad(z, ((0, 0), (1, 0), (0, 0)))[:, :-1]


def rwkv7_mix(z, shift_mix, w0, w_lora_up, a0, a_lora_up, g_lora_up,
              k_k, k_a, r_k, lnx_gain, lnx_bias):
    B, T, _ = z.shape
    z = z + (token_shift(z) - z) * shift_mix
    r, k, v, zw, za, zg = jnp.split(
        z, [RWKV_W, 2 * RWKV_W, 3 * RWKV_W, 3 * RWKV_W + R_DECAY,
            3 * RWKV_W + R_DECAY + R_AAA], axis=-1)
    w_log = -jax.nn.softplus(-(w0 + jnp.tanh(zw) @ w_lora_up)) - 0.5
    decay = jnp.exp(-jnp.exp(w_log.astype(F32)))
    a = jax.nn.sigmoid(a0 + za @ a_lora_up)
    g = jax.nn.sigmoid(zg) @ g_lora_up
    heads = lambda t: t.reshape(B, T, RWKV_H, RWKV_N).astype(F32)
    kk = heads(k * k_k)
    kk = kk / jnp.maximum(jnp.sqrt(jnp.sum(kk * kk, -1, keepdims=True)), 1e-12)
    k = heads(k * (1.0 + (a - 1.0) * k_a))
    r_h, v_h, a_h, w_h = heads(r), heads(v), heads(a), heads(decay)

    def step(S, inp):
        r_t, w_t, k_t, v_t, kk_t, a_t = inp
        sa = jnp.einsum('bhvk,bhk->bhv', S, kk_t)
        S = (S * w_t[:, :, None, :]
             - sa[..., None] * (kk_t * a_t)[:, :, None, :]
             + v_t[..., None] * k_t[:, :, None, :])
        return S, jnp.einsum('bhvk,bhk->bhv', S, r_t)

    tm = lambda t: jnp.moveaxis(t, 1, 0)
    S0 = jnp.zeros((B, RWKV_H, RWKV_N, RWKV_N), F32)
    _, y = lax.scan(step, S0, (tm(r_h), tm(w_h), tm(k), tm(v_h), tm(kk), tm(a_h)))
    y = jnp.moveaxis(y, 0, 1)
    mu = jnp.mean(y, -1, keepdims=True)
    var = jnp.mean(jnp.square(y - mu), -1, keepdims=True)
    y = ((y - mu) * lax.rsqrt(var + LNX_EPS)).reshape(B, T, RWKV_W)
    y = y * lnx_gain.astype(F32) + lnx_bias.astype(F32)
    bonus = jnp.sum(r_h * k * r_k.astype(F32), -1, keepdims=True) * v_h
    out = (y + bonus.reshape(B, T, RWKV_W)) * g.astype(F32)
    return out.astype(z.dtype)


def diff_attn(z, positions, lam_q1, lam_k1, lam_q2, lam_k2, subln_gain, lambda_init):
    B, T, _ = z.shape
    q, k, v = jnp.split(z, 3, axis=-1)
    q = rope(q.reshape(B, T, DIFF_H, 2, DIFF_D), positions)
    k = rope(k.reshape(B, T, DIFF_H, 2, DIFF_D), positions)
    v = v.reshape(B, T, DIFF_H, 2 * DIFF_D)
    lam = (jnp.exp(jnp.sum(lam_q1.astype(F32) * lam_k1.astype(F32)))
           - jnp.exp(jnp.sum(lam_q2.astype(F32) * lam_k2.astype(F32))) + lambda_init)
    nb = T // Q_BLOCK
    qb_all = (q * (DIFF_D ** -0.5)).reshape(B, nb, Q_BLOCK, DIFF_H, 2, DIFF_D)
    qb_all = jnp.moveaxis(qb_all, 1, 0)
    key_pos = jnp.arange(T)

    def block(args):
        qb, bi = args
        s = jnp.einsum('bqhcd,bkhcd->bhcqk', qb, k, preferred_element_type=F32)
        q_pos = bi * Q_BLOCK + jnp.arange(Q_BLOCK)
        s = jnp.where((key_pos[None, :] <= q_pos[:, None]), s, -jnp.inf)
        p = jax.nn.softmax(s, axis=-1)
        attn = p[:, :, 0] - lam * p[:, :, 1]
        return jnp.einsum('bhqk,bkhe->bqhe', attn.astype(v.dtype), v)

    o = lax.map(block, (qb_all, jnp.arange(nb)))
    o = jnp.moveaxis(o, 0, 1).reshape(B, T, DIFF_H, 2 * DIFF_D)
    o = rms_norm(o, subln_gain, SUBLN_EPS) * (1.0 - lambda_init)
    return o.reshape(B, T, DIFF_W)


def memory_cross_attn(h, mem_n, wq, wkv, wo):
    B, T, _ = h.shape
    q = (h @ wq).reshape(B, T, CROSS_H, CROSS_D)
    k, v = jnp.split(mem_n @ wkv, 2, axis=-1)
    k = k.reshape(B, -1, CROSS_H, CROSS_D)
    v = v.reshape(B, -1, CROSS_H, CROSS_D)
    s = jnp.einsum('bthd,bmhd->bhtm', q, k, preferred_element_type=F32) * (CROSS_D ** -0.5)
    p = jax.nn.softmax(s, axis=-1)
    o = jnp.einsum('bhtm,bmhd->bthd', p.astype(v.dtype), v).reshape(B, T, D_MODEL)
    return o @ wo


def conv_ffn(h, w_up, conv_w, conv_b, w_down):
    T = h.shape[1]
    gate, val = jnp.split(h @ w_up, 2, axis=-1)
    gp = jnp.pad(gate, ((0, 0), (CONV_W - 1, 0), (0, 0)))
    c = sum(conv_w[j] * gp[:, j:j + T] for j in range(CONV_W)) + conv_b
    return (jax.nn.silu(c) * val) @ w_down


def setup_inputs(seed: int = 0) -> dict:
    key = jax.random.key(seed)
    ks = iter(jax.random.split(key, 40))
    nrm = lambda shape, scale: scale * jax.random.normal(next(ks), shape, F32)
    near1 = lambda shape: 1.0 + 0.02 * jax.random.normal(next(ks), shape, F32)
    L = DEPTH
    x = jax.random.normal(next(ks), (BATCH, SEQ, D_MODEL), F32)
    mem = jax.random.normal(next(ks), (BATCH, MEM_LEN, D_MODEL), F32)
    offset = jax.random.randint(next(ks), (BATCH, 1), 0, 4096, jnp.int32)
    positions = (offset + jnp.arange(SEQ, dtype=jnp.int32)[None, :]).astype(jnp.int32)
    return {
        "x": x, "mem": mem, "positions": positions,
        "norm_mix": near1((L, D_MODEL)),
        "w_in": nrm((L, D_MODEL, N_IN), D_MODEL ** -0.5),
        "shift_mix": jax.random.uniform(next(ks), (L, N_SHIFT), F32),
        "w0": jax.random.uniform(next(ks), (L, RWKV_W), F32, -6.0, 1.0),
        "w_lora_up": nrm((L, R_DECAY, RWKV_W), R_DECAY ** -0.5),
        "a0": nrm((L, RWKV_W), 0.5),
        "a_lora_up": nrm((L, R_AAA, RWKV_W), R_AAA ** -0.5),
        "g_lora_up": nrm((L, R_GATE, RWKV_W), R_GATE ** -0.5),
        "k_k": 0.85 + nrm((L, RWKV_W), 0.1),
        "k_a": 1.0 + nrm((L, RWKV_W), 0.1),
        "r_k": nrm((L, RWKV_H, RWKV_N), 0.1),
        "lnx_gain": near1((L, RWKV_W)),
        "lnx_bias": nrm((L, RWKV_W), 0.02),
        "lam_q1": nrm((L, DIFF_D), 0.1),
        "lam_k1": nrm((L, DIFF_D), 0.1),
        "lam_q2": nrm((L, DIFF_D), 0.1),
        "lam_k2": nrm((L, DIFF_D), 0.1),
        "subln_gain": near1((L, 2 * DIFF_D)),
        "w_out": nrm((L, D_MIX, D_MODEL), D_MIX ** -0.5),
        "norm_cross": near1((L, D_MODEL)),
        "norm_mem": near1((L, D_MODEL)),
        "wq_c": nrm((L, D_MODEL, D_MODEL), D_MODEL ** -0.5),
        "wkv_c": nrm((L, D_MODEL, 2 * D_MODEL), D_MODEL ** -0.5),
        "wo_c": nrm((L, D_MODEL, D_MODEL), D_MODEL ** -0.5),
        "norm_ffn": near1((L, D_MODEL)),
        "w_up": nrm((L, D_MODEL, 2 * D_FF), D_MODEL ** -0.5),
        "conv_w": nrm((L, CONV_W, D_FF), CONV_W ** -0.5),
        "conv_b": nrm((L, D_FF), 0.02),
        "w_down": nrm((L, D_FF, D_MODEL), D_FF ** -0.5),
        "norm_final": near1((D_MODEL,)),
    }


def reference(x, mem, positions, norm_mix, w_in, shift_mix, w0, w_lora_up, a0, a_lora_up,
              g_lora_up, k_k, k_a, r_k, lnx_gain, lnx_bias, lam_q1, lam_k1, lam_q2, lam_k2,
              subln_gain, w_out, norm_cross, norm_mem, wq_c, wkv_c, wo_c, norm_ffn, w_up,
              conv_w, conv_b, w_down, norm_final):
    for l in range(DEPTH):
        lambda_init = 0.8 - 0.6 * math.exp(-0.3 * l)
        z = rms_norm(x, norm_mix[l]) @ w_in[l]
        y_a = rwkv7_mix(z[..., :N_SHIFT], shift_mix[l], w0[l], w_lora_up[l], a0[l],
                        a_lora_up[l], g_lora_up[l], k_k[l], k_a[l], r_k[l],
                        lnx_gain[l], lnx_bias[l])
        y_b = diff_attn(z[..., N_SHIFT:], positions, lam_q1[l], lam_k1[l], lam_q2[l],
                        lam_k2[l], subln_gain[l], lambda_init)
        x = x + jnp.concatenate([y_a, y_b], axis=-1) @ w_out[l]
        x = x + memory_cross_attn(rms_norm(x, norm_cross[l]), rms_norm(mem, norm_mem[l]),
                                  wq_c[l], wkv_c[l], wo_c[l])
        x = x + conv_ffn(rms_norm(x, norm_ffn[l]), w_up[l], conv_w[l], conv_b[l], w_down[l])
    return rms_norm(x, norm_final)
```

```python
import contextlib
import math
import numpy as np
import concourse.bass as bass
import concourse.mybir as mybir
from concourse.bass_utils import run_bass_kernel_spmd

F32 = mybir.dt.float32
BF16 = mybir.dt.bfloat16
I32 = mybir.dt.int32
AF = mybir.ActivationFunctionType
ALU = mybir.AluOpType
AX = mybir.AxisListType

D = 1024
NIN = 3232
DFF = 2816
MEM = 256
LNX_EPS = 64e-5
SUBLN_EPS = 1e-5
NORM_EPS = 1e-6
LAMBDA_INIT = 0.2
SEQ_ONLY = False


class Buf:
    __slots__ = ("w", "r", "name")

    def __init__(self, name=""):
        self.w = None
        self.r = []
        self.name = name


class Tile:
    def __init__(self, t, name):
        self.t = t
        self.buf = Buf(name)
        self.name = name

    def __getitem__(self, idx):
        return self.t[idx]


def _b(x):
    return x.buf if isinstance(x, Tile) else x


class Ctx:
    ENG = ("pe", "act", "dve", "pool", "sp")

    def __init__(self, nc, stack):
        self.nc = nc
        self.stack = stack
        self.sem = {}
        for e in self.ENG:
            self.sem[e] = stack.enter_context(nc.semaphore("s_" + e))
        self.cnt = {e: 0 for e in self.ENG}
        self.known = {e: {} for e in self.ENG}
        self.ops = {e: [] for e in self.ENG}
        self.dsem = {}
        self.nops = 0

    def _collect(self, e, reads, writes):
        waits = {}

        def add(tok):
            if tok is None:
                return
            k, v = tok
            if waits.get(k, 0) < v:
                waits[k] = v
        for b in reads:
            add(b.w)
        for b in writes:
            add(b.w)
            for t in b.r:
                add(t)
        kn = self.known[e]
        out = []
        for k, v in waits.items():
            if e == "pe" and k == ("E", "pe"):
                continue
            if kn.get(k, 0) >= v:
                continue
            kn[k] = v
            out.append((k, v))
        return out

    def op(self, e, fn, R=(), W=()):
        R = [_b(x) for x in R]
        W = [_b(x) for x in W]
        waits = self._collect(e, R, W)
        self.cnt[e] += 1
        tok = (("E", e), self.cnt[e])
        for b in R:
            b.r.append(tok)
        for b in W:
            b.w = tok
            b.r = []
        self.ops[e].append((waits, fn, None))
        self.nops += 1

    def dma(self, q, out, in_, slot, R=(), W=()):
        R = [_b(x) for x in R]
        W = [_b(x) for x in W]
        if slot not in self.dsem:
            s = self.stack.enter_context(self.nc.semaphore("d_" + slot))
            self.dsem[slot] = [s, 0]
        waits = self._collect(q, R, W)
        self.dsem[slot][1] += 16
        tok = (("D", slot), self.dsem[slot][1])
        for b in R:
            b.r.append(tok)
        for b in W:
            b.w = tok
            b.r = []

        def fn(eng, out=out, in_=in_):
            return eng.dma_start(out=out, in_=in_)
        self.ops[q].append((waits, fn, slot))
        self.nops += 1

    def _semof(self, k):
        return self.sem[k[1]] if k[0] == "E" else self.dsem[k[1]][0]

    def emit(self):
        nc = self.nc
        waits = []
        for name, (s, cn) in self.dsem.items():
            k = ("D", name)
            if cn > 0 and self.known["sp"].get(k, 0) < cn:
                self.known["sp"][k] = cn
                waits.append((k, cn))
        if waits:
            self.ops["sp"].append((waits, None, None))
        ops = self.ops
        self.ops = {e: [] for e in self.ENG}
        with nc.Block() as block:
            def mk(e):
                def body(eng):
                    for waits, fn, slot in ops[e]:
                        for k, v in waits:
                            eng.wait_ge(self._semof(k), v)
                        if fn is None:
                            continue
                        inst = fn(eng)
                        if slot is None:
                            inst.then_inc(self.sem[e], 1)
                        else:
                            inst.then_inc(self.dsem[slot][0], 16)
                return body
            block.tensor(mk("pe"))
            block.scalar(mk("act"))
            block.vector(mk("dve"))
            block.gpsimd(mk("pool"))
            block.sync(mk("sp"))
        for e in self.ENG:
            for e2 in self.ENG:
                self.known[e][("E", e2)] = self.cnt[e2]
            for name, (s, cn) in self.dsem.items():
                self.known[e][("D", name)] = cn

    def ACT(self, out, in_, func, R, W, scale=1.0, bias=None):
        if bias is None:
            self.op("act", lambda e: e.activation(out=out, in_=in_, func=func, scale=scale), R, W)
        else:
            self.op("act", lambda e: e.activation(out=out, in_=in_, func=func, scale=scale, bias=bias), R, W)

    def RPOW(self, out, in_, R, W, power, scale=1.0, bias=None, tmp=None, RB=()):
        t = out if tmp is None else tmp
        self.ACT(t, in_, AF.Ln, list(R) + list(RB), W, scale=scale, bias=bias)
        self.ACT(out, t, AF.Exp, W, W, scale=power)

    def TT(self, eng, out, in0, in1, op, R, W):
        self.op(eng, lambda e: e.tensor_tensor(out=out, in0=in0, in1=in1, op=op), R, W)

    def TS(self, eng, out, in0, s1, op0, R, W, s2=None, op1=None):
        if op1 is None:
            self.op(eng, lambda e: e.tensor_scalar(out=out, in0=in0, scalar1=s1, scalar2=None, op0=op0), R, W)
        else:
            self.op(eng, lambda e: e.tensor_scalar(out=out, in0=in0, scalar1=s1, scalar2=s2, op0=op0, op1=op1), R, W)

    def STT(self, out, in0, scalar, in1, op0, op1, R, W):
        self.op("dve", lambda e: e.scalar_tensor_tensor(out=out, in0=in0, scalar=scalar, in1=in1, op0=op0, op1=op1), R, W)

    def CP(self, eng, out, in_, R, W):
        if eng == "act":
            self.op("act", lambda e: e.activation(out=out, in_=in_, func=AF.Copy), R, W)
        else:
            self.op(eng, lambda e: e.tensor_copy(out=out, in_=in_), R, W)

    def MM(self, out, lhsT, rhs, start, stop, R, W):
        self.op("pe", lambda e: e.matmul(out, lhsT=lhsT, rhs=rhs, start=start, stop=stop), R, W)

    def TR(self, out, in_, ident, R, W):
        self.op("pe", lambda e: e.transpose(out=out, in_=in_, identity=ident), R, W)


class Pool_:
    CNT = [0]

    def __init__(self, nc, st):
        self.nc = nc
        self.st = st

    def sb(self, name, shape, dt):
        Pool_.CNT[0] += 1
        nm = "%s_%d" % (name, Pool_.CNT[0])
        return Tile(self.st.enter_context(self.nc.sbuf_tensor(nm, shape, dt)), nm)

    def ps(self, name, shape, dt=F32):
        Pool_.CNT[0] += 1
        nm = "%s_%d" % (name, Pool_.CNT[0])
        return Tile(self.st.enter_context(self.nc.psum_tensor(nm, shape, dt)), nm)


VEC_SPEC = [
    ("mix_r", 4), ("mix_k", 4), ("mix_v", 4), ("mix_wa", 1), ("mix_g", 1),
    ("w0", 4), ("a0", 4), ("k_k", 4), ("k_a", 4), ("r_k", 4),
    ("norm_mix", 8), ("norm_cross", 8), ("norm_mem", 8), ("norm_ffn", 8), ("norm_final", 8),
    ("conv_w0", 22), ("conv_w1", 22), ("conv_w2", 22), ("conv_b", 22),
    ("inv_freq", 1), ("sin_scale", 1), ("subln", 1),
]
VEC_OFF = {}
_o = 0
for _n, _k in VEC_SPEC:
    VEC_OFF[_n] = _o
    _o += _k
NVEC = _o


def _cols(v, n):
    v = np.asarray(v, np.float32).reshape(-1)
    out = np.zeros((n * 128,), np.float32)
    out[: v.shape[0]] = v
    return np.ascontiguousarray(out.reshape(n, 128).T)


def cb(h):
    return (h % 2) * 4 + h // 2


def build(T, dbg=False):
    NB = T // 512
    NCH = T // 128
    nc = bass.Bass("TRN2", target_bir_lowering=False)
    dr = {}

    def din(name, shape, dt=F32):
        dr[name] = nc.dram_tensor(name, shape, dt, kind="ExternalInput").ap()
        return dr[name]

    def dscr(name, shape, dt):
        dr[name] = nc.dram_tensor(name, shape, dt, kind="ExternalOutput" if dbg else "Internal").ap()
        return dr[name]

    din("xT", [8, 128, T])
    din("memT", [8, 128, MEM])
    din("pos", [128, T], I32)
    din("vecs", [128, NVEC])
    din("w_in", [8, 128, NIN])
    din("lup", [64, 512])
    din("gup", [96, 512])
    din("tmb", [128, 3, 512])
    din("lam", [128, 4, 64])
    din("w_out", [8, 128, D])
    din("wq_c", [8, 128, D])
    din("wkv_c", [8, 128, 2 * D])
    din("wo_c", [8, 128, D])
    din("w_up", [8, 128, 2 * DFF])
    din("w_down", [22, 128, D])
    din("consts", [128, 6, 128])
    din("cmask", [128, 4, 512])
    outT = nc.dram_tensor("outT", [8, 128, T], F32, kind="ExternalOutput").ap()

    for nm in ("kt_fm", "bt_fm", "kk_fm", "rt_fm", "qh_fm", "kh_fm"):
        dscr(nm, [4, 128, T], BF16)
    for nm in ("bt_tm", "kk_tm", "v_tm", "va_tm"):
        dscr(nm, [T, 512], BF16)
    dscr("g_tm", [T, 512], F32)
    dscr("c_tm", [T, 8], F32)
    dscr("dm_fm", [128, 4, NCH], F32)
    dscr("ee_fm", [128, 4, NCH], F32)
    dscr("ymix_fm", [8, 128, T], BF16)
    dscr("x2_fm", [8, 128, T], F32)
    dscr("a_fm", [22, 128, T], BF16)

    with contextlib.ExitStack() as st0:
        c = Ctx(nc, st0)
        phase_A(nc, c, dr, T)
        phase_B(nc, c, dr, T)
        phase_C(nc, c, dr, T)
        phase_D1(nc, c, dr, T)
        phase_D2(nc, c, dr, T)
        phase_D3(nc, c, dr, T, outT)
    return nc


def load_consts(c, P, dr, q="sp"):
    cf = P.sb("cf", [128, 6, 128], F32)
    cbf = P.sb("cbf", [128, 6, 128], BF16)
    c.dma(q, cf[:], dr["consts"], "cf", W=[cf])
    c.CP("dve", cbf[:], cf[:], [cf], [cbf])
    return cf, cbf


def load_weight_bf16(c, P, dr, name, kc, n, wb, stg, gcol=None, vec=None, eng="dve"):
    for k in range(kc):
        s = stg[k % len(stg)]
        c.dma("sp", s[:, 0:n], dr[name][k], s.name, W=[s])
        if gcol is None:
            c.CP("dve" if k % 2 == 0 else "act", wb[:, k, :], s[:, 0:n], [s], [wb])
        else:
            c.TS("dve", wb[:, k, :], s[:, 0:n], vec[:, gcol + k:gcol + k + 1], ALU.mult, [s, vec], [wb])


def phase_A(nc, c, dr, T):
    NB = T // 512
    with contextlib.ExitStack() as st:
        P = Pool_(nc, st)
        cf, cbf = load_consts(c, P, dr)
        ident_b = cbf[:, 0, :]
        bones_f = cf[:, 4, :]
        bones_b = cbf[:, 4, :]
        swap_f = cf[:, 5, :]
        vec = P.sb("vec", [128, NVEC], F32)
        c.dma("sp", vec[:], dr["vecs"], "vec", W=[vec])
        V = lambda n, j=0: vec[:, VEC_OFF[n] + j: VEC_OFF[n] + j + 1]
        ones_f = P.sb("ones_f", [128, 128], BF16)
        c.op("pool", lambda e: e.memset(ones_f[:], 1.0), W=[ones_f])
        epsc = P.sb("epsc", [128, 2], F32)
        c.op("pool", lambda e: e.memset(epsc[:, 0:1], NORM_EPS), W=[epsc])
        c.op("pool", lambda e: e.memset(epsc[:, 1:2], 1e-18), W=[epsc])
        ones512 = P.sb("ones512", [128, 512], F32)
        c.op("pool", lambda e: e.memset(ones512[:], 1.0), W=[ones512])
        Wb = P.sb("Wb", [128, 8, NIN], BF16)
        xs = P.sb("xs", [128, 4096], F32)
        xs3 = xs[:].rearrange("p (k t) -> p k t", t=512)
        stg = [xs]
        load_weight_bf16(c, P, dr, "w_in", 8, NIN, Wb, stg, VEC_OFF["norm_mix"], vec)
        lupf = P.sb("lupf", [64, 512], F32)
        lupb = P.sb("lupb", [64, 512], BF16)
        gupf = P.sb("gupf", [96, 512], F32)
        gupb = P.sb("gupb", [96, 512], BF16)
        c.dma("sp", lupf[:], dr["lup"], "lupf", W=[lupf])
        c.dma("sp", gupf[:], dr["gup"], "gupf", W=[gupf])
        c.CP("dve", lupb[:], lupf[:], [lupf], [lupb])
        c.CP("dve", gupb[:], gupf[:], [gupf], [gupb])

        sq = P.sb("sq", [128, 8, 512], BF16)
        xb = P.sb("xb", [128, 8, 512], BF16)
        rstd = P.sb("rstd", [128, 512], F32)
        pp = [P.ps("pp", [128, 512]) for _ in range(6)]
        ptr = [P.ps("ptr", [128, 512], BF16) for _ in range(2)]
        ppi = [0]

        def nextp():
            ppi[0] += 1
            return pp[ppi[0] % 6]
        tri = [0]

        def nexttr():
            tri[0] += 1
            return ptr[tri[0] % 2]

        Hz = P.sb("Hz", [128, 14], F32)
        c.op("pool", lambda e: e.memset(Hz[:], 0.0), W=[Hz])
        zs_wa = P.sb("zs_wa", [128, 512], F32)
        zs_g = P.sb("zs_g", [128, 512], F32)
        th_b = P.sb("th_b", [64, 512], BF16)
        sg_b = P.sb("sg_b", [96, 512], BF16)
        NSET = 2
        sets = []
        for si in range(NSET):
            sets.append({
                "tmp": [P.sb("tmpA", [128, 512], F32) for _ in range(11)],
                "z": [P.sb("zt", [128, 513], F32) for _ in range(3)],
                "bft": [P.sb("bfA", [128, 512], BF16) for _ in range(6)],
                "trs": [P.sb("trs", [128, 512], BF16) for _ in range(2)],
            })
        tmp = sets[0]["tmp"]
        bft2 = [P.sb("bfQ", [128, 512], BF16) for _ in range(2)]
        trsA = bft2
        cts = P.sb("cts", [128, 4, 8], F32)
        dmt = P.sb("dmt", [128, 4, 4], F32)
        eet = P.sb("eet", [128, 4, 4], F32)
        posi = P.sb("posi", [128, 512], I32)
        ropei = P.sb("ropei", [128, 512], I32)
        ropeT = [P.sb("ropeT", [128, 512], F32) for _ in range(8)]
        cosT, sinT = ropeT[0], ropeT[1]
        gts = zs_wa

        def proj_fm(cols, ncol):
            p = nextp()
            for k in range(8):
                c.MM(p[0:ncol, :], Wb[:, k, cols:cols + ncol], xb[:, k, :], k == 0, k == 7, [Wb, xb], [p])
            return p

        def zproj(cols, ncol, dst, hidx):
            p = proj_fm(cols, ncol)
            c.CP("act", dst[0:ncol, 0:1], Hz[0:ncol, hidx:hidx + 1], [Hz], [dst])
            c.CP("act", dst[0:ncol, 1:513], p[0:ncol, :], [p], [dst])
            c.CP("act", Hz[0:ncol, hidx:hidx + 1], dst[0:ncol, 512:513], [dst], [Hz])

        def shift(dst_t, dst, src, ncol, mixcol, d):
            c.TT("pool", d[0:ncol, :], src[0:ncol, 0:512], src[0:ncol, 1:513], ALU.subtract, [src], [d])
            c.STT(dst, d[0:ncol, :], mixcol, src[0:ncol, 1:513], ALU.mult, ALU.add, [d, src, vec], [dst_t])

        def jchain(j, S, b):
            tsl = slice(b * 512, (b + 1) * 512)
            jsl = slice(j * 128, (j + 1) * 128)
            tmp = S["tmp"]
            zr, zk, zv = S["z"]
            lw, cl, rel, epos, eneg, eprev, av, kpr, t1, t2 = tmp[1:11]
            rsh, vsh, kap = lw, cl, rel
            p = nextp()
            c.MM(p[:, :], lupb[0:32, jsl], th_b[0:32, :], True, True, [lupb, th_b], [p])
            c.ACT(lw[:], p[:, :], AF.Sigmoid, [p, vec], [lw], bias=V("w0", j))
            p = nextp()
            c.MM(p[:, :], lupb[32:64, jsl], th_b[32:64, :], True, True, [lupb, th_b], [p])
            c.ACT(av[:], p[:, :], AF.Sigmoid, [p, vec], [av], bias=V("a0", j))
            yield
            c.TS("dve", lw[:], lw[:], -0.6065306597126334, ALU.mult, [lw], [lw])
            c.op("dve", lambda e, cl=cl, lw=lw: e.tensor_tensor_scan(out=cl[:], data0=ones512[:], data1=lw[:], initial=0.0, op0=ALU.mult, op1=ALU.add), [ones512, lw], [cl])
            yield
            for cc in range(4):
                s_ = slice(cc * 128, (cc + 1) * 128)
                c.TS("dve", rel[:, s_], cl[:, s_], cl[:, cc * 128 + 63: cc * 128 + 64], ALU.subtract, [cl], [rel])
            yield
            c.ACT(epos[:], rel[:], AF.Exp, [rel], [epos])
            c.ACT(eneg[:], rel[:], AF.Exp, [rel], [eneg], scale=-1.0)
            c.TT("dve", t1[:], rel[:], lw[:], ALU.subtract, [rel, lw], [t1])
            yield
            c.ACT(eprev[:], t1[:], AF.Exp, [t1], [eprev])
            c.ACT(dmt[:, j, :], t1[:].rearrange("p (c t) -> p c t", t=128)[:, :, 0], AF.Exp, [t1], [dmt], scale=-1.0)
            c.CP("act", eet[:, j, :], epos[:].rearrange("p (c t) -> p c t", t=128)[:, :, 127], [epos], [eet])
            yield
            zproj(0 + j * 128, 128, zr, 2 + j)
            yield
            shift(rsh, rsh[:], zr, 128, V("mix_r", j), tmp[0])
            yield
            zproj(512 + j * 128, 128, zk, 6 + j)
            yield
            ksh = t2
            shift(ksh, ksh[:], zk, 128, V("mix_k", j), tmp[0])
            yield
            zproj(1024 + j * 128, 128, zv, 10 + j)
            yield
            shift(vsh, vsh[:], zv, 128, V("mix_v", j), tmp[0])
            yield
            kr = tmp[0]
            c.ACT(kr[:], ksh[:], AF.Copy, [ksh, vec], [kr], scale=V("k_k", j))
            c.ACT(t1[:], kr[:], AF.Square, [kr], [t1])
            yield
            p = nextp()
            c.MM(p[:, :], bones_f, t1[:], True, True, [cf, t1], [p])
            c.RPOW(t1[:], p[:, :], [p], [t1], -0.5, bias=epsc[:, 1:2], RB=[epsc])
            yield
            c.TT("dve", kap[:], kr[:], t1[:], ALU.mult, [kr, t1], [kap])
            yield
            c.TS("dve", t1[:], av[:], -1.0, ALU.add, [av, vec], [t1], s2=V("k_a", j), op1=ALU.mult)
            c.STT(kpr[:], t1[:], 1.0, ksh[:], ALU.add, ALU.mult, [t1, ksh], [kpr])
            yield
            o_kt, o_bt, o_kk, o_rt, o_v, o_rk = S["bft"]
            c.TT("dve", o_kt[:], kap[:], eprev[:], ALU.mult, [kap, eprev], [o_kt])
            c.TT("pool", t1[:], kap[:], av[:], ALU.mult, [kap, av], [t1])
            yield
            c.TT("dve", o_bt[:], t1[:], eneg[:], ALU.mult, [t1, eneg], [o_bt])
            c.TT("dve", o_kk[:], kpr[:], eneg[:], ALU.mult, [kpr, eneg], [o_kk])
            c.TT("pool", o_rt[:], rsh[:], epos[:], ALU.mult, [rsh, epos], [o_rt])
            c.CP("act", o_v[:], vsh[:], [vsh], [o_v])
            yield
            c.TT("pool", t1[:], rsh[:], kpr[:], ALU.mult, [rsh, kpr], [t1])
            yield
            c.ACT(o_rk[:], t1[:], AF.Copy, [t1, vec], [o_rk], scale=V("r_k", j))
            for nm, tl in (("kt_fm", o_kt), ("bt_fm", o_bt), ("kk_fm", o_kk), ("rt_fm", o_rt)):
                c.dma("sp", dr[nm][j, :, tsl], tl[:], "st_" + tl.name, R=[tl])
            yield
            for ti_, (nm, tl) in enumerate((("bt_tm", o_bt), ("kk_tm", o_kk), ("v_tm", o_v))):
                pt = nexttr()
                for tt in range(4):
                    c.TR(pt[:, tt * 128:(tt + 1) * 128], tl[:, tt * 128:(tt + 1) * 128], ident_b, [tl, cbf], [pt])
                ts_ = S["trs"][ti_ % 2]
                c.CP("act" if ti_ % 2 == 0 else "dve", ts_[:], pt[:, :], [pt], [ts_])
                c.dma("sp", dr[nm].rearrange("(n p) f -> p n f", p=128)[:, b * 4:(b + 1) * 4, jsl],
                      ts_[:].rearrange("p (n f) -> p n f", f=128), "st_" + ts_.name, R=[ts_])
                yield
            p = nextp()
            for tt in range(4):
                c.MM(p[:, tt * 2:tt * 2 + 2], o_rk[:, tt * 128:(tt + 1) * 128],
                     cbf[:, 4, :].rearrange("p (i k) -> p i k", k=64)[:, :, 0], True, True, [o_rk, cbf], [p])
            c.CP("act", cts[:, :, 2 * j:2 * j + 2], p[:, 0:8].rearrange("p (t i) -> p t i", i=2), [p], [cts])
            yield


        def achain(b):
            tsl = slice(b * 512, (b + 1) * 512)
            c.dma("sp", posi[:], dr["pos"][:, tsl], "posi", W=[posi])
            u, uc, kf, f1 = ropeT[2:6]
            c.CP("dve", u[:], posi[:], [posi], [u])
            c.TS("dve", u[:], u[:], V("inv_freq"), ALU.mult, [u, vec], [u], s2=1.0 / (2 * math.pi), op1=ALU.mult)
            yield
            for which, dst in ((0, sinT), (1, cosT)):
                if which == 1:
                    c.TS("dve", uc[:], u[:], 0.25, ALU.add, [u], [uc])
                    src = uc
                else:
                    src = u
                c.CP("dve", ropei[:], src[:], [src], [ropei])
                yield
                c.CP("dve", kf[:], ropei[:], [ropei], [kf])
                yield
                c.TT("dve", f1[:], src[:], kf[:], ALU.subtract, [src, kf], [f1])
                yield
                c.STT(kf[:], f1[:], 0.5, f1[:], ALU.is_gt, ALU.subtract, [f1], [kf])
                yield
                if which == 0:
                    c.ACT(dst[:], kf[:], AF.Sin, [kf, vec], [dst], scale=V("sin_scale"))
                else:
                    c.ACT(dst[:], kf[:], AF.Sin, [kf], [dst], scale=-6.283185)
                yield
            qfs = [ropeT[2], ropeT[3]]
            t1s = [ropeT[4], ropeT[5]]
            t2s = [ropeT[6], ropeT[7]]
            it = 0
            for which, (c0, nm) in enumerate(((1696, "qh_fm"), (2208, "kh_fm"))):
                for j in range(4):
                    qf_, t1, t2 = qfs[it % 2], t1s[it % 2], t2s[it % 2]
                    p = proj_fm(c0 + j * 128, 128)
                    c.CP("act", qf_[:], p[:, :], [p], [qf_])
                    yield
                    p2 = nextp()
                    c.MM(p2[:, :], swap_f, qf_[:], True, True, [cf, qf_], [p2])
                    c.TT("dve", t2[:], p2[:, :], sinT[:], ALU.mult, [p2, sinT], [t2])
                    c.TT("pool", t1[:], qf_[:], cosT[:], ALU.mult, [qf_, cosT], [t1])
                    yield
                    ob = bft2[it % 2]
                    c.TT("dve", ob[:], t1[:], t2[:], ALU.add, [t1, t2], [ob])
                    c.dma("sp", dr[nm][j, :, tsl], ob[:], "st_" + ob.name, R=[ob])
                    it += 1
                    yield
            for tt in range(4):
                p = nextp()
                for k in range(8):
                    c.MM(p[:, :], xb[:, k, tt * 128:(tt + 1) * 128], Wb[:, k, 2720:3232], k == 0, k == 7, [xb, Wb], [p])
                ts_ = trsA[tt % 2]
                c.CP("act" if tt % 2 == 0 else "dve", ts_[:], p[:, :], [p], [ts_])
                c.dma("sp", dr["va_tm"][b * 512 + tt * 128: b * 512 + (tt + 1) * 128, :], ts_[:], "st_" + ts_.name, R=[ts_])
                yield

        def achain_head(gen, n):
            for _ in range(n):
                try:
                    next(gen)
                except StopIteration:
                    return
                yield

        def run_slots(slots):
            cur = [None] * len(slots)
            live = True
            while live:
                live = False
                for si, sl_ in enumerate(slots):
                    while True:
                        if cur[si] is None:
                            if not sl_:
                                break
                            cur[si] = sl_.pop(0)
                        try:
                            next(cur[si])
                            live = True
                            break
                        except StopIteration:
                            cur[si] = None

        for b in range(NB):
            tsl = slice(b * 512, (b + 1) * 512)
            if b == 0:
                c.dma("sp", xs3, dr["xT"].rearrange("k p t -> p k t")[:, :, tsl], "xs", W=[xs])
            c.ACT(sq[:], xs3, AF.Square, [xs], [sq])
            p = nextp()
            for k in range(8):
                c.MM(p[:, :], ones_f[:], sq[:, k, :], k == 0, k == 7, [ones_f, sq], [p])
            c.RPOW(rstd[:], p[:, :], [p], [rstd], -0.5, scale=1.0 / D, bias=epsc[:, 0:1], RB=[epsc])
            c.TT("dve", xb[:], xs3, rstd[:].unsqueeze(1).to_broadcast([128, 8, 512]), ALU.mult, [xs, rstd], [xb])
            if b + 1 < NB:
                c.dma("sp", xs3, dr["xT"].rearrange("k p t -> p k t")[:, :, slice((b + 1) * 512, (b + 2) * 512)], "xs", W=[xs])
            lora_done = [False]

            def lora(b=b):
                zw_ = sets[0]["z"][0]
                zproj(1536, 64, zw_, 0)
                yield
                shift(zs_wa, zs_wa[0:64, :], zw_, 64, V("mix_wa")[0:64, :], tmp[0])
                yield
                c.ACT(th_b[0:32, :], zs_wa[0:32, :], AF.Tanh, [zs_wa], [th_b])
                c.CP("act", th_b[32:64, :], zs_wa[32:64, :], [zs_wa], [th_b])
                yield
                zg_ = sets[1]["z"][0]
                zproj(1600, 96, zg_, 1)
                yield
                shift(zs_g, zs_g[0:96, :], zg_, 96, V("mix_g")[0:96, :], sets[1]["tmp"][0])
                yield
                c.ACT(sg_b[:, :], zs_g[0:96, :], AF.Sigmoid, [zs_g], [sg_b])
                yield
                for tt in range(4):
                    p = nextp()
                    c.MM(p[:, :], sg_b[:, tt * 128:(tt + 1) * 128], gupb[:, :], True, True, [sg_b, gupb], [p])
                    c.CP("dve", gts[:], p[:, :], [p], [gts])
                    c.dma("sp", dr["g_tm"][b * 512 + tt * 128: b * 512 + (tt + 1) * 128, :], gts[:], "st_gts", R=[gts])
                    yield
                lora_done[0] = True

            def guard(gen):
                assert lora_done[0], "jchain started before lora finished"
                yield from gen

            ach = achain(b)
            run_slots([[lora(), guard(jchain(0, sets[0], b)), guard(jchain(2, sets[0], b))],
                       [achain_head(ach, 12), guard(jchain(1, sets[1], b)), guard(jchain(3, sets[1], b))],
                       [ach]])
            c.dma("sp", dr["c_tm"].rearrange("(n p) h -> p n h", p=128)[:, b * 4:(b + 1) * 4, :], cts[:], "st_cts", R=[cts])
            c.dma("sp", dr["dm_fm"][:, :, b * 4:(b + 1) * 4], dmt[:], "st_dmt", R=[dmt])
            c.dma("sp", dr["ee_fm"][:, :, b * 4:(b + 1) * 4], eet[:], "st_eet", R=[eet])

        c.emit()


def make_consts():
    cm = np.zeros((128, 6, 128), np.float32)
    p = np.arange(128)[:, None]
    m = np.arange(128)[None, :]
    cm[:, 0, :] = (p == m)
    cm[:, 1, :] = (p < m)
    cm[:, 2, :] = (p <= m)
    cm[:, 3, :] = (p > m)
    cm[:, 4, :] = (p // 64 == m // 64)
    partner = np.where((np.arange(128) % 64) < 32, np.arange(128) + 32, np.arange(128) - 32)
    sw = np.zeros((128, 128), np.float32)
    sw[partner, np.arange(128)] = 1.0
    cm[:, 5, :] = sw
    cmask = np.zeros((128, 4, 512), np.float32)
    q = np.arange(512)[None, :]
    for i in range(4):
        cmask[:, i, :] = (128 * i + p <= q)
    return cm, cmask


def pack_shared(inp):
    g = lambda n: np.asarray(inp[n], np.float32)
    vec = np.zeros((128, NVEC), np.float32)

    def put(name, v, n):
        vec[:, VEC_OFF[name]:VEC_OFF[name] + n] = _cols(v, n)
    sm = g("shift_mix")[0]
    put("mix_r", sm[0:512], 4)
    put("mix_k", sm[512:1024], 4)
    put("mix_v", sm[1024:1536], 4)
    put("mix_wa", sm[1536:1600], 1)
    put("mix_g", sm[1600:1696], 1)
    for n in ("w0", "a0", "k_k", "k_a", "r_k"):
        put(n, g(n)[0].reshape(-1), 4)
    for n in ("norm_mix", "norm_cross", "norm_mem", "norm_ffn"):
        put(n, g(n)[0], 8)
    put("norm_final", g("norm_final"), 8)
    cw = g("conv_w")[0]
    for j in range(3):
        put("conv_w%d" % j, cw[j], 22)
    put("conv_b", g("conv_b")[0], 22)
    pidx = np.arange(128) % 32
    inv = (10000.0 ** (-(2.0 * pidx.astype(np.float32)) / 64.0)).astype(np.float32)
    vec[:, VEC_OFF["inv_freq"]] = inv
    sgn = np.where((np.arange(128) % 64) < 32, -1.0, 1.0).astype(np.float32)
    vec[:, VEC_OFF["sin_scale"]] = -6.283185 * sgn
    vec[:, VEC_OFF["subln"]] = g("subln_gain")[0]
    tmb = np.zeros((128, 3, 512), np.float32)
    tmb[:, 0, :] = g("lnx_gain")[0][None, :]
    tmb[:, 1, :] = g("lnx_bias")[0][None, :]
    tmb[:, 2, :] = np.tile(g("subln_gain")[0], 4)[None, :]
    lam = np.zeros((128, 4, 64), np.float32)
    for i, n in enumerate(("lam_q1", "lam_k1", "lam_q2", "lam_k2")):
        lam[:, i, :] = g(n)[0][None, :]
    cm, cmask = make_consts()
    sh = {
        "vecs": vec, "tmb": tmb, "lam": lam, "consts": cm, "cmask": cmask,
        "w_in": np.ascontiguousarray(g("w_in")[0].reshape(8, 128, NIN)),
        "lup": np.ascontiguousarray(np.concatenate([g("w_lora_up")[0], g("a_lora_up")[0]], 0)),
        "gup": np.ascontiguousarray(g("g_lora_up")[0]),
        "w_out": np.ascontiguousarray(g("w_out")[0].reshape(8, 128, D)),
        "wq_c": np.ascontiguousarray(g("wq_c")[0].reshape(8, 128, D)),
        "wkv_c": np.ascontiguousarray(g("wkv_c")[0].reshape(8, 128, 2 * D)),
        "wo_c": np.ascontiguousarray(g("wo_c")[0].reshape(8, 128, D)),
        "w_up": np.ascontiguousarray(g("w_up")[0].reshape(8, 128, 2 * DFF)),
        "w_down": np.ascontiguousarray(g("w_down")[0].reshape(22, 128, D)),
    }
    return sh


def pack_core(inp, b, T):
    x = np.asarray(inp["x"], np.float32)[b]
    mem = np.asarray(inp["mem"], np.float32)[b]
    pos = np.asarray(inp["positions"], np.int32)[b]
    return {
        "xT": np.ascontiguousarray(x.T.reshape(8, 128, T)),
        "memT": np.ascontiguousarray(mem.T.reshape(8, 128, MEM)),
        "pos": np.ascontiguousarray(np.broadcast_to(pos[None, :], (128, T))),
    }


_NC_CACHE = {}


def kernel(**inputs):
    x = np.asarray(inputs["x"])
    B, T, _ = x.shape
    if T not in _NC_CACHE:
        _NC_CACHE[T] = build(T)
    nc = _NC_CACHE[T]
    sh = pack_shared(inputs)
    in_maps = []
    for b in range(B):
        m = dict(sh)
        m.update(pack_core(inputs, b, T))
        in_maps.append(m)
    res = run_bass_kernel_spmd(nc, in_maps, core_ids=list(range(B)))
    out = np.stack([np.ascontiguousarray(res.results[b]["outT"].reshape(D, T).T) for b in range(B)], 0)
    return out.astype(np.float32)


def phase_B(nc, c, dr, T):
    NCH = T // 128
    with contextlib.ExitStack() as st:
        P = Pool_(nc, st)
        cf, cbf = load_consts(c, P, dr)
        ident_b = cbf[:, 0, :]
        mSU = P.sb("mSU", [128, 8, 128], F32)
        mUI = P.sb("mUI", [128, 8, 128], F32)
        mSL = P.sb("mSL", [128, 8, 128], F32)
        idr = P.sb("idr", [128, 8, 128], BF16)
        for h in range(8):
            c.CP("pool", mSU[:, h, :], cf[:, 1, :], [cf], [mSU])
            c.CP("pool", mUI[:, h, :], cf[:, 2, :], [cf], [mUI])
            c.CP("pool", mSL[:, h, :], cf[:, 3, :], [cf], [mSL])
            c.CP("pool", idr[:, h, :], cf[:, 0, :], [cf], [idr])
        tmb = P.sb("tmb", [128, 3, 512], F32)
        c.dma("sp", tmb[:], dr["tmb"], "tmb", W=[tmb])
        dm = P.sb("dm", [128, 4, NCH], F32)
        ee = P.sb("ee", [128, 4, NCH], F32)
        c.dma("sp", dm[:], dr["dm_fm"], "dm", W=[dm])
        c.dma("sp", ee[:], dr["ee_fm"], "ee", W=[ee])
        S0 = P.sb("S0", [128, 4, 64], F32)
        c.op("pool", lambda e: e.memset(S0[:], 0.0), W=[S0])
        Sm = P.sb("Sm", [128, 4, 64], F32)
        Smb = P.sb("Smb", [128, 4, 64], BF16)
        NBUF = 2
        fm = {n: [P.sb(n, [128, 4, 128], BF16) for _ in range(NBUF)] for n in ("kt", "bt", "kk", "rt")}
        tm = {n: [P.sb(n, [128, 512], BF16) for _ in range(NBUF)] for n in ("btT", "kkT", "vT")}
        gt = [P.sb("gt", [128, 512], F32) for _ in range(NBUF)]
        ct = [P.sb("ct", [128, 8], F32) for _ in range(NBUF)]
        Zb = P.sb("Zb", [128, 2, 256], BF16)
        nU = P.sb("nU", [128, 512], BF16)
        y32 = P.sb("y32", [128, 512], F32)
        ysq = P.sb("ysq", [128, 512], F32)
        vf = P.sb("vf", [128, 512], F32)
        st8 = [P.sb("st8", [128, 8], F32) for _ in range(4)]
        yo = P.sb("yo", [128, 512], BF16)
        yT = P.sb("yT", [128, 4, 128], BF16)
        QA = P.ps("QA", [128, 1024])
        SP = P.ps("SPs", [128, 512])
        TP = P.ps("TPs", [128, 512], BF16)
        dpi = [0]

        def nextd():
            dpi[0] += 1
            return DP[dpi[0] % 2]

        def HP(i):
            return slice(i * 64, (i + 1) * 64)

        def mk(name, n=2):
            return [[P.sb(name, [128, 4, 128], BF16) for _ in range(2)] for _ in range(n)]
        AakH, ArbH, ArkH, XTfH = mk("AakH"), mk("ArbH"), mk("ArkH"), mk("XTfH")
        NbH, MbH, XTH = mk("NbH"), mk("MbH"), mk("XTH")
        DPh = [[P.ps("DPh", [128, 512]) for _ in range(2)] for _ in range(2)]
        dph_i = [0, 0]

        def nexth(half):
            dph_i[half] += 1
            return DPh[half][dph_i[half] % 2]

        def gen_load(ch):
            bi = ch % NBUF
            tsl = slice(ch * 128, (ch + 1) * 128)
            for n, src in (("kt", "kt_fm"), ("bt", "bt_fm"), ("kk", "kk_fm"), ("rt", "rt_fm")):
                t_ = fm[n][bi]
                c.dma("sp", t_[:], dr[src].rearrange("j p t -> p j t")[:, :, tsl], t_.name, W=[t_])
            for n, src in (("btT", "bt_tm"), ("kkT", "kk_tm"), ("vT", "v_tm")):
                t_ = tm[n][bi]
                c.dma("sp", t_[:], dr[src][tsl, :], t_.name, W=[t_])
            c.dma("sp", gt[bi][:], dr["g_tm"][tsl, :], gt[bi].name, W=[gt[bi]])
            c.dma("sp", ct[bi][:], dr["c_tm"][tsl, :], ct[bi].name, W=[ct[bi]])

        def gen_pre(ch, half):
            bi = ch % NBUF
            kt, bt, kk, rt = fm["kt"][bi], fm["bt"][bi], fm["kk"][bi], fm["rt"][bi]
            Aak, Arb, Ark = AakH[ch % 2][half], ArbH[ch % 2][half], ArkH[ch % 2][half]
            Nb, Mb, XT = NbH[half], MbH[half], XTH[half]
            hp = slice(half * 64, (half + 1) * 64)
            fl = lambda t_: t_[:].rearrange("p h t -> p (h t)")
            mk4 = lambda m_: m_[:, 0:4, :].rearrange("p h t -> p (h t)")

            def headmm(dst, A, Bm):
                for j in range(4):
                    c.MM(dst[:, j * 128:(j + 1) * 128], A[hp, j, :], Bm[hp, j, :], True, True, [A, Bm], [dst])

            d = nexth(half); headmm(d, bt, kt)
            N0 = Nb[0]
            c.STT(fl(N0), d[:, :], -1.0, mk4(mSU), ALU.mult, ALU.mult, [d, mSU], [N0])
            yield
            d = nexth(half); headmm(d, kt, bt)
            M0 = Mb[0]
            c.STT(fl(M0), d[:, :], -1.0, mk4(mSL), ALU.mult, ALU.mult, [d, mSL], [M0])
            yield
            d = nexth(half); headmm(d, kk, kt)
            c.TT("dve", fl(Aak), d[:, :], mk4(mSU), ALU.mult, [d, mSU], [Aak])
            yield
            d = nexth(half); headmm(d, bt, rt)
            c.TT("dve", fl(Arb), d[:, :], mk4(mUI), ALU.mult, [d, mUI], [Arb])
            yield
            d = nexth(half); headmm(d, kk, rt)
            c.TT("dve", fl(Ark), d[:, :], mk4(mUI), ALU.mult, [d, mUI], [Ark])
            yield
            xc = XT[0]
            c.TT("dve", fl(xc), fl(N0), idr[:, 0:4, :].rearrange("p h t -> p (h t)"), ALU.add, [N0, idr], [xc])
            yield
            Nc, Mc = N0, M0
            for lv in range(1, 7):
                Nn, Mn = Nb[lv % 2], Mb[lv % 2]
                dM = nexth(half)
                for j in range(4):
                    c.MM(dM[:, j * 128:(j + 1) * 128], Nc[:, j, :], Mc[:, j, :], True, True, [Nc, Mc], [dM])
                c.CP("act", fl(Mn), dM[:, :], [dM], [Mn])
                yield
                if lv < 6:
                    dN = nexth(half)
                    for j in range(4):
                        c.MM(dN[:, j * 128:(j + 1) * 128], Mc[:, j, :], Nc[:, j, :], True, True, [Nc, Mc], [dN])
                    c.CP("act", fl(Nn), dN[:, :], [dN], [Nn])
                    yield
                dX = nexth(half)
                for j in range(4):
                    c.MM(dX[:, j * 128:(j + 1) * 128], Mn[:, j, :], xc[:, j, :], True, True, [Mn, xc], [dX])
                xn = XT[lv % 2] if lv < 6 else XTfH[ch % 2][half]
                c.TT("dve", fl(xn), dX[:, :], fl(xc), ALU.add, [dX, xc], [xn])
                yield
                xc, Nc, Mc = xn, Nn, Mn

        def gen_seq(ch):
            bi = ch % NBUF
            tsl = slice(ch * 128, (ch + 1) * 128)
            AakQ, ArbQ, ArkQ, XTQ = AakH[ch % 2], ArbH[ch % 2], ArkH[ch % 2], XTfH[ch % 2]
            kt, bt, kk, rt = fm["kt"][bi], fm["bt"][bi], fm["kk"][bi], fm["rt"][bi]
            btT, kkT, vT = tm["btT"][bi], tm["kkT"][bi], tm["vT"][bi]
            for j in range(4):
                c.TS("dve", Sm[:, j, :], S0[:, j, :], dm[:, j, ch:ch + 1], ALU.mult, [S0, dm], [Sm])
                yield
            c.CP("act", Smb[:], Sm[:], [Sm], [Smb])
            yield
            dZ = QA
            for h in range(8):
                j, i = h // 2, h % 2
                o = dZ[:, i * 512 + j * 64: i * 512 + (j + 1) * 64]
                c.MM(o, kt[HP(i), j, :], Smb[HP(i), j, :], True, False, [kt, Smb], [dZ])
                c.MM(o, AakQ[i][:, j, :], vT[:, h * 64:(h + 1) * 64], False, True, [AakQ[i], vT], [dZ])
            c.CP("act", Zb[:], dZ[:, :].rearrange("p (i x) -> p i x", i=2)[:, :, 0:256], [dZ], [Zb])
            yield
            for h in range(8):
                j, i = h // 2, h % 2
                c.MM(SP[:, h * 64:(h + 1) * 64], XTQ[i][:, j, :], Zb[:, i, j * 64:(j + 1) * 64], True, True, [XTQ[i], Zb], [SP])
            c.ACT(nU[:], SP[:, :], AF.Copy, [SP], [nU], scale=-1.0)
            yield
            dY = QA
            for h in range(8):
                j, i = h // 2, h % 2
                o = dY[:, i * 512 + j * 64: i * 512 + (j + 1) * 64]
                hs = slice(h * 64, (h + 1) * 64)
                c.MM(o, rt[HP(i), j, :], Smb[HP(i), j, :], True, False, [rt, Smb], [dY])
                c.MM(o, ArbQ[i][:, j, :], nU[:, hs], False, False, [ArbQ[i], nU], [dY])
                c.MM(o, ArkQ[i][:, j, :], vT[:, hs], False, True, [ArkQ[i], vT], [dY])
            dS = SP
            for h in range(8):
                j, i = h // 2, h % 2
                hs = slice(h * 64, (h + 1) * 64)
                o = dS[HP(i), j * 64:(j + 1) * 64]
                c.MM(o, btT[:, hs], nU[:, hs], True, False, [btT, nU], [dS])
                c.MM(o, kkT[:, hs], vT[:, hs], False, True, [kkT, vT], [dS])
            c.TT("dve", Sm[:].rearrange("p j v -> p (j v)"), dS[:, 0:256], Sm[:].rearrange("p j v -> p (j v)"), ALU.add, [dS, Sm], [Sm])
            yield
            for j in range(4):
                c.TS("dve", S0[:, j, :], Sm[:, j, :], ee[:, j, ch:ch + 1], ALU.mult, [Sm, ee], [S0])
                yield
            y4 = y32[:].rearrange("p (j i v) -> p i j v", i=2, v=64)
            for i in range(2):
                c.CP("act", y4[:, i, :, :], dY[:, i * 512:i * 512 + 256].rearrange("p (j v) -> p j v", v=64), [dY], [y32])
                yield
            y3 = y32[:].rearrange("p (h v) -> p h v", v=64)
            s1, s2, mean, rs = st8
            c.op("dve", lambda e, s1=s1, y3=y3: e.tensor_reduce(out=s1[:], in_=y3, axis=AX.X, op=ALU.add), [y32], [s1])
            c.ACT(ysq[:], y32[:], AF.Square, [y32], [ysq])
            yield
            c.op("dve", lambda e, s2=s2: e.tensor_reduce(out=s2[:], in_=ysq[:].rearrange("p (h v) -> p h v", v=64), axis=AX.X, op=ALU.add), [ysq], [s2])
            c.TS("dve", mean[:], s1[:], 1.0 / 64, ALU.mult, [s1], [mean])
            yield
            c.TT("dve", s1[:], mean[:], mean[:], ALU.mult, [mean], [s1])
            yield
            c.STT(s2[:], s2[:], 1.0 / 64, s1[:], ALU.mult, ALU.subtract, [s2, s1], [s2])
            yield
            c.ACT(rs[:], s2[:], AF.Sqrt, [s2], [rs], bias=LNX_EPS)
            yield
            c.op("dve", lambda e, rs=rs: e.reciprocal(out=rs[:], in_=rs[:]), [rs], [rs])
            c.TT("dve", y3, y3, mean[:].unsqueeze(2).to_broadcast([128, 8, 64]), ALU.subtract, [y32, mean], [y32])
            yield
            c.TT("dve", y3, y3, rs[:].unsqueeze(2).to_broadcast([128, 8, 64]), ALU.mult, [y32, rs], [y32])
            yield
            c.TT("pool", y32[:], y32[:], tmb[:, 0, :], ALU.mult, [y32, tmb], [y32])
            yield
            c.TT("pool", y32[:], y32[:], tmb[:, 1, :], ALU.add, [y32, tmb], [y32])
            yield
            c.CP("act", vf[:], vT[:], [vT], [vf])
            yield
            c.TT("dve", vf[:].rearrange("p (h v) -> p h v", v=64), vf[:].rearrange("p (h v) -> p h v", v=64),
                 ct[bi][:].unsqueeze(2).to_broadcast([128, 8, 64]), ALU.mult, [vf, ct[bi]], [vf])
            c.TT("pool", y32[:], y32[:], vf[:], ALU.add, [y32, vf], [y32])
            yield
            c.TT("dve", yo[:], y32[:], gt[bi][:], ALU.mult, [y32, gt[bi]], [yo])
            yield
            for q in range(4):
                c.TR(TP[:, q * 128:(q + 1) * 128], yo[:, q * 128:(q + 1) * 128], ident_b, [yo, cbf], [TP])
            c.CP("act", yT[:].rearrange("p q t -> p (q t)"), TP[:, :], [TP], [yT])
            yield
            c.dma("sp", dr["ymix_fm"].rearrange("k p t -> p k t")[:, 0:4, tsl], yT[:], "st_yT", R=[yT])
            yield

        for ch in range(NCH + 1):
            gens = []
            if ch < NCH:
                gen_load(ch)
            if ch >= 1:
                gens.append(gen_seq(ch - 1))
            if ch < NCH:
                gens.append(gen_pre(ch, 0))
                gens.append(gen_pre(ch, 1))
            while gens:
                for g_ in list(gens):
                    try:
                        next(g_)
                    except StopIteration:
                        gens.remove(g_)
        c.emit()


def phase_C(nc, c, dr, T):
    NCH = T // 128
    NB = T // 512
    with contextlib.ExitStack() as st:
        P = Pool_(nc, st)
        vec = P.sb("vec", [128, NVEC], F32)
        c.dma("sp", vec[:], dr["vecs"], "vec", W=[vec])
        cmf = P.sb("cmf", [128, 4, 512], F32)
        cmb = P.sb("cmb", [128, 4, 512], BF16)
        c.dma("sp", cmf[:], dr["cmask"], "cmf", W=[cmf])
        c.TS("dve", cmb[:], cmf[:], 30000.0, ALU.mult, [cmf], [cmb], s2=-30000.0, op1=ALU.add)
        idf = P.sb("idf", [128, 128], F32)
        idb = P.sb("idb", [128, 128], BF16)
        c.dma("sp", idf[:], dr["consts"][:, 0, :], "idf", W=[idf])
        c.CP("dve", idb[:], idf[:], [idf], [idb])
        lam = P.sb("lam", [128, 4, 64], F32)
        c.dma("sp", lam[:], dr["lam"], "lam", W=[lam])
        lp = P.sb("lp", [128, 2, 64], F32)
        ls = P.sb("ls", [128, 2], F32)
        nlam = P.sb("nlam", [128, 1], F32)
        c.TT("dve", lp[:, 0, :], lam[:, 0, :], lam[:, 1, :], ALU.mult, [lam], [lp])
        c.TT("dve", lp[:, 1, :], lam[:, 2, :], lam[:, 3, :], ALU.mult, [lam], [lp])
        c.op("dve", lambda e: e.tensor_reduce(out=ls[:], in_=lp[:], axis=AX.X, op=ALU.add), [lp], [ls])
        c.ACT(ls[:], ls[:], AF.Exp, [ls], [ls])
        c.TT("dve", nlam[:], ls[:, 1:2], ls[:, 0:1], ALU.subtract, [ls], [nlam])
        c.TS("dve", nlam[:], nlam[:], -LAMBDA_INIT, ALU.add, [nlam], [nlam])
        ones_f = P.sb("ones_f", [128, 128], F32)
        c.op("pool", lambda e: e.memset(ones_f[:], 1.0), W=[ones_f])
        ones_bb = P.sb("ones_bb", [128, 128], BF16)
        c.op("pool", lambda e: e.memset(ones_bb[:], 1.0), W=[ones_bb])
        osqb = P.sb("osqb", [128, 512], BF16)
        Kt = P.sb("Kt", [128, 4, T], BF16)
        Va = P.sb("Va", [128, NCH, 512], BF16)
        for h in range(4):
            c.dma("sp", Kt[:, h, :], dr["kh_fm"][h], "Kt%d" % h, W=[Kt])
        vsrc = dr["va_tm"].rearrange("(n p) f -> p n f", p=128)
        for n0 in range(0, NCH, 8):
            n1 = min(NCH, n0 + 8)
            c.dma("sp", Va[:, n0:n1, :], vsrc[:, n0:n1, :], "Va%d" % (n0 % 16), W=[Va])
        Qt = [P.sb("Qt", [128, 4, 512], BF16) for _ in range(2)]
        NPT = 4
        PT = [P.sb("PT", [128, 2, 512], BF16) for _ in range(NPT)]
        PS = [P.ps("PS", [128, 1024]) for _ in range(2)]
        ACC = [P.ps("ACC", [128, 512]) for _ in range(2)]
        EPt = P.ps("EPt", [128, 1024])
        EPs = EPt
        Pacc = [P.sb("Pacc", [128, 2, 512], F32) for _ in range(2)]
        oraw = [P.sb("oraw", [128, 2, 512], F32) for _ in range(2)]
        rden = P.sb("rden", [128, 2, 512], F32)
        o1 = P.sb("o1", [128, 512], F32)
        o2 = P.sb("o2", [128, 512], F32)
        osq = P.sb("osq", [128, 512], F32)
        rs = P.sb("rs", [128, 512], F32)
        yb = [P.sb("yb", [128, 512], BF16) for _ in range(2)]
        epsc = P.sb("epsc", [128, 1], F32)
        c.op("pool", lambda e: e.memset(epsc[:], SUBLN_EPS), W=[epsc])
        nlam8 = P.sb("nlam8", [128, 1], F32)

        tiles = []
        for qb in range(NB):
            for h in range(4):
                nkt = 4 * (qb + 1)
                for kt in range(nkt):
                    tiles.append((qb, h, kt, nkt))
        NT = len(tiles)

        def stage1(t):
            qb, h, kt, nkt = tiles[t]
            if qb == 0 and h == 0 and kt == 0:
                qt = Qt[0]
                c.dma("sp", qt[:], dr["qh_fm"].rearrange("j p t -> p j t")[:, :, 0:512], qt.name, W=[qt])
            if h == 3 and kt == 0 and qb + 1 < NB:
                qn = Qt[(qb + 1) % 2]
                c.dma("sp", qn[:], dr["qh_fm"].rearrange("j p t -> p j t")[:, :, (qb + 1) * 512:(qb + 2) * 512], qn.name, W=[qn])
            qt = Qt[qb % 2]
            i = kt - 4 * qb
            q0 = 128 * i if i > 0 else 0
            ps = PS[t % 2]
            pt = PT[t % NPT]
            for cm in range(2):
                hp = slice(cm * 64, (cm + 1) * 64)
                c.MM(ps[:, cm * 512 + q0:(cm + 1) * 512], Kt[hp, h, kt * 128:(kt + 1) * 128], qt[hp, h, q0:512], True, i < 0, [Kt, qt], [ps])
            if i >= 0:
                for cm in range(2):
                    c.MM(ps[:, cm * 512 + q0:(cm + 1) * 512], idb[:], cmb[:, i, q0:512], False, True, [idb, cmb], [ps])
            psv = ps[:, :].rearrange("p (m q) -> p m q", m=2)[:, :, q0:512]
            c.ACT(pt[:, :, q0:512], psv, AF.Exp, [ps], [pt], scale=0.125)

        def stage2(t):
            qb, h, kt, nkt = tiles[t]
            g = qb * 4 + h
            i = kt - 4 * qb
            q0 = 128 * i if i > 0 else 0
            pt = PT[t % NPT]
            pa = Pacc[g % 2]
            for cm in range(2):
                c.MM(ACC[cm][:, q0:512], Va[:, kt, h * 128:(h + 1) * 128], pt[:, cm, q0:512], kt == 0, kt == nkt - 1, [Va, pt], [ACC[cm]])
            if kt == 0:
                c.CP("dve", pa[:], pt[:], [pt], [pa])
            else:
                c.TT("dve", pa[:, :, q0:512], pa[:, :, q0:512], pt[:, :, q0:512], ALU.add, [pa, pt], [pa])
            if kt == nkt - 1:
                orw = oraw[g % 2]
                c.CP("act", orw[:, 0, :], ACC[0][:, :], [ACC[0]], [orw])
                c.CP("dve", orw[:, 1, :], ACC[1][:, :], [ACC[1]], [orw])
                return (qb, h, g)
            return None

        def ep1a(qb, h, g):
            pa = Pacc[g % 2]
            for cm in range(2):
                c.MM(EPt[:, cm * 512:(cm + 1) * 512], ones_f[:], pa[:, cm, :], True, True, [ones_f, pa], [EPt])

        def ep1b(qb, h, g):
            orw = oraw[g % 2]
            for cm in range(2):
                c.RPOW(rden[:, cm, :], EPt[:, cm * 512:(cm + 1) * 512], [EPt], [rden], -1.0)
            c.TT("pool", o1[:], orw[:, 0, :], rden[:, 0, :], ALU.mult, [orw, rden], [o1])
            c.TT("pool", o2[:], orw[:, 1, :], rden[:, 1, :], ALU.mult, [orw, rden], [o2])
            c.TS("pool", o2[:], o2[:], nlam[:, 0:1], ALU.mult, [o2, nlam], [o2], s2=0.0, op1=ALU.add)
            c.TT("pool", o1[:], o1[:], o2[:], ALU.add, [o1, o2], [o1])

        def ep2a(qb, h, g):
            c.ACT(osqb[:], o1[:], AF.Square, [o1], [osqb])

        def ep2b(qb, h, g):
            c.MM(EPt[:, 0:512], ones_bb[:], osqb[:], True, True, [ones_bb, osqb], [EPt])

        def ep3(qb, h, g):
            c.RPOW(rs[:], EPt[:, 0:512], [EPt], [rs], -0.5, scale=1.0 / 128, bias=epsc[:, 0:1], RB=[epsc])
            c.TT("pool", o1[:], o1[:], rs[:], ALU.mult, [o1, rs], [o1])
            y_ = yb[g % 2]
            c.TS("pool", y_[:], o1[:], vec[:, VEC_OFF["subln"]:VEC_OFF["subln"] + 1], ALU.mult, [o1, vec], [y_], s2=1.0 - LAMBDA_INIT, op1=ALU.mult)
            c.dma("sp", dr["ymix_fm"][4 + h, :, qb * 512:(qb + 1) * 512], y_[:], "st_" + y_.name, R=[y_])

        LA = 1
        pend = []

        def flush(upto_g=None, force=False):
            while pend:
                cd, fn_, r_ = pend[0]
                if force or cd <= 0 or (upto_g is not None and r_[2] <= upto_g):
                    pend.pop(0)
                    fn_(*r_)
                else:
                    break

        for t in range(NT + LA):
            if t < NT:
                stage1(t)
            if t - LA >= 0:
                qb_, h_, kt_, nkt_ = tiles[t - LA]
                if kt_ == 0:
                    flush(upto_g=qb_ * 4 + h_ - 2)
                r_ = stage2(t - LA)
                if r_ is not None:
                    pend.append([3, ep1a, r_])
                    pend.append([7, ep1b, r_])
                    pend.append([13, ep2a, r_])
                    pend.append([15, ep2b, r_])
                    pend.append([18, ep3, r_])
                for pe_ in pend:
                    pe_[0] -= 1
                flush()
        flush(force=True)
        c.emit()


def rms_rstd(c, src, sqb, ones_b, ps, rstd, epsc, n=512):
    c.ACT(sqb[:, :, 0:n], src[:, :, 0:n], AF.Square, [src], [sqb])
    for k in range(8):
        c.MM(ps[:, 0:n], ones_b[:], sqb[:, k, 0:n], k == 0, k == 7, [ones_b, sqb], [ps])
    c.RPOW(rstd[:, 0:n], ps[:, 0:n], [ps], [rstd], -0.5, scale=1.0 / D, bias=epsc[:, 0:1], RB=[epsc])


def phase_D1(nc, c, dr, T):
    NB = T // 512
    with contextlib.ExitStack() as st:
        P = Pool_(nc, st)
        vec = P.sb("vec", [128, NVEC], F32)
        c.dma("sp", vec[:], dr["vecs"], "vec", W=[vec])
        ones_b = P.sb("ones_b", [128, 128], BF16)
        c.op("pool", lambda e: e.memset(ones_b[:], 1.0), W=[ones_b])
        epsc = P.sb("epsc", [128, 1], F32)
        c.op("pool", lambda e: e.memset(epsc[:], NORM_EPS), W=[epsc])
        xs2 = [P.sb("xs", [128, 4096], F32) for _ in range(3)]
        stg = xs2[0:2]
        Wo = P.sb("Wo", [128, 8, D], BF16)
        Wq = P.sb("Wq", [128, 8, D], BF16)
        Wo2 = P.sb("Wo2", [128, 8, D], BF16)
        Wkv = P.sb("Wkv", [128, 8, 2 * D], BF16)
        for k0 in range(0, 8, 4):
            pass
        c.dma("pool", Wo[:], dr["w_out"].rearrange("k p n -> p k n"), "Wo_ld", W=[Wo])
        load_weight_bf16(c, P, dr, "wq_c", 8, D, Wq, stg, VEC_OFF["norm_cross"], vec)
        c.dma("pool", Wo2[:], dr["wo_c"].rearrange("k p n -> p k n"), "Wo2_ld", W=[Wo2])
        load_weight_bf16(c, P, dr, "wkv_c", 8, 2 * D, Wkv, stg, VEC_OFF["norm_mem"], vec)
        pp = [P.ps("pp", [128, 512]) for _ in range(6)]
        ppi = [0]

        def nextp():
            ppi[0] += 1
            return pp[ppi[0] % 6]
        sqb = P.sb("sqb", [128, 8, 512], BF16)
        xb = P.sb("xb", [128, 8, 512], BF16)
        yms = [P.sb("ym", [128, 8, 512], BF16) for _ in range(3)]
        qcb = P.sb("qcb", [128, 8, 512], BF16)
        ocb = P.sb("ocb", [128, 8, 512], BF16)
        rstd = P.sb("rstd", [128, 512], F32)
        rden = P.sb("rden", [128, 512], F32)
        PTc = [P.sb("PTc", [128, 512], BF16) for _ in range(2)]
        Kc = P.sb("Kc", [128, 8, MEM], BF16)
        Vc = P.sb("Vc", [128, 2, D], BF16)
        rst_m = P.sb("rst_m", [128, 2], F32)
        xsT = xs2[0]
        xs = xsT[:].rearrange("p (k t) -> p k t", t=512)
        c.dma("sp", xs[:, :, 0:MEM], dr["memT"].rearrange("k p t -> p k t"), xsT.name, W=[xsT])
        p = nextp()
        c.ACT(sqb[:, :, 0:MEM], xs[:, :, 0:MEM], AF.Square, [xsT], [sqb])
        for k in range(8):
            c.MM(p[:, 0:MEM], ones_b[:], sqb[:, k, 0:MEM], k == 0, k == 7, [ones_b, sqb], [p])
        c.RPOW(rstd[:, 0:MEM], p[:, 0:MEM], [p], [rstd], -0.5, scale=1.0 / D, bias=epsc[:, 0:1], RB=[epsc])
        c.CP("dve", xb[:, :, 0:MEM], xs[:, :, 0:MEM], [xsT], [xb])
        p = nextp()
        for mt in range(2):
            for k in range(8):
                c.MM(p[:, mt:mt + 1], sqb[:, k, mt * 128:(mt + 1) * 128], ones_b[:, 0:1], k == 0, k == 7, [sqb, ones_b], [p])
        c.ACT(rst_m[:], p[:, 0:2], AF.Sqrt, [p], [rst_m], scale=1.0 / D, bias=NORM_EPS)
        c.op("dve", lambda e: e.reciprocal(out=rst_m[:], in_=rst_m[:]), [rst_m], [rst_m])
        for n in range(8):
            p = nextp()
            for k in range(8):
                c.MM(p[:, 0:MEM], Wkv[:, k, n * 128:(n + 1) * 128], xb[:, k, 0:MEM], k == 0, k == 7, [Wkv, xb], [p])
            c.TT("dve", Kc[:, n, :], p[:, 0:MEM], rstd[:, 0:MEM], ALU.mult, [p, rstd], [Kc])
        for mt in range(2):
            for nn in range(2):
                p = nextp()
                for k in range(8):
                    c.MM(p[:, :], xb[:, k, mt * 128:(mt + 1) * 128], Wkv[:, k, D + nn * 512: D + (nn + 1) * 512], k == 0, k == 7, [Wkv, xb], [p])
                c.TS("dve", Vc[:, mt, nn * 512:(nn + 1) * 512], p[:, :], rst_m[:, mt:mt + 1], ALU.mult, [p, rst_m], [Vc])
        def loads(b):
            tsl = slice(b * 512, (b + 1) * 512)
            xT_ = xs2[b % 3]
            c.dma("sp", xT_[:].rearrange("p (k t) -> p k t", t=512), dr["xT"].rearrange("k p t -> p k t")[:, :, tsl], xT_.name, W=[xT_])
            c.dma("sp", yms[b % 3][:], dr["ymix_fm"].rearrange("k p t -> p k t")[:, :, tsl], yms[b % 3].name, W=[yms[b % 3]])

        def wout(b, n0, n1):
            xT_ = xs2[b % 3]
            x3 = xT_[:].rearrange("p (k t) -> p k t", t=512)
            ym_ = yms[b % 3]
            for n in range(n0, n1):
                p = nextp()
                for k in range(8):
                    c.MM(p[:, :], Wo[:, k, n * 128:(n + 1) * 128], ym_[:, k, :], k == 0, k == 7, [Wo, ym_], [p])
                c.TT("dve", x3[:, n, :], p[:, :], x3[:, n, :], ALU.add, [p, xT_], [xT_])

        loads(0)
        if NB > 1:
            loads(1)
        wout(0, 0, 8)
        for b in range(NB):
            tsl = slice(b * 512, (b + 1) * 512)
            xsT = xs2[b % 3]
            xs = xsT[:].rearrange("p (k t) -> p k t", t=512)
            if b + 2 < NB:
                loads(b + 2)
            p = nextp()
            c.ACT(sqb[:], xs, AF.Square, [xsT], [sqb])
            if b + 1 < NB:
                wout(b + 1, 0, 4)
            for k in range(8):
                c.MM(p[:, :], ones_b[:], sqb[:, k, :], k == 0, k == 7, [ones_b, sqb], [p])
            c.RPOW(rstd[:], p[:, :], [p], [rstd], -0.5, scale=1.0 / D, bias=epsc[:, 0:1], RB=[epsc])
            if b + 1 < NB:
                wout(b + 1, 4, 8)
            c.TT("dve", xb[:], xs, rstd[:].unsqueeze(1).to_broadcast([128, 8, 512]), ALU.mult, [xsT, rstd], [xb])
            for n in range(8):
                p = nextp()
                for k in range(8):
                    c.MM(p[:, :], Wq[:, k, n * 128:(n + 1) * 128], xb[:, k, :], k == 0, k == 7, [Wq, xb], [p])
                c.CP("act", qcb[:, n, :], p[:, :], [p], [qcb])
            for hq in range(4):
                for mt in range(2):
                    p = nextp()
                    for dd in range(2):
                        c.MM(p[:, :], Kc[:, 2 * hq + dd, mt * 128:(mt + 1) * 128], qcb[:, 2 * hq + dd, :], dd == 0, dd == 1, [Kc, qcb], [p])
                    c.ACT(PTc[mt][:], p[:, :], AF.Exp, [p], [PTc[mt]], scale=1.0 / 16)
                p = nextp()
                for mt in range(2):
                    c.MM(p[:, :], ones_b[:], PTc[mt][:], mt == 0, mt == 1, [ones_b, PTc[mt]], [p])
                c.RPOW(rden[:], p[:, :], [p], [rden], -1.0)
                for dd in range(2):
                    p = nextp()
                    for mt in range(2):
                        c.MM(p[:, :], Vc[:, mt, hq * 256 + dd * 128: hq * 256 + (dd + 1) * 128], PTc[mt][:], mt == 0, mt == 1, [Vc, PTc[mt]], [p])
                    c.TT("dve", ocb[:, 2 * hq + dd, :], p[:, :], rden[:], ALU.mult, [p, rden], [ocb])
            for n in range(8):
                p = nextp()
                for k in range(8):
                    c.MM(p[:, :], Wo2[:, k, n * 128:(n + 1) * 128], ocb[:, k, :], k == 0, k == 7, [Wo2, ocb], [p])
                c.TT("dve", xs[:, n, :], p[:, :], xs[:, n, :], ALU.add, [p, xsT], [xsT])
            c.dma("sp", dr["x2_fm"].rearrange("k p t -> p k t")[:, :, tsl], xs, "st_" + xsT.name, R=[xsT])
        c.emit()


def phase_D2(nc, c, dr, T):
    NB = T // 512
    with contextlib.ExitStack() as st:
        P = Pool_(nc, st)
        vec = P.sb("vec", [128, NVEC], F32)
        c.dma("sp", vec[:], dr["vecs"], "vec", W=[vec])
        V = lambda n, j=0: vec[:, VEC_OFF[n] + j: VEC_OFF[n] + j + 1]
        ones_b = P.sb("ones_b", [128, 128], BF16)
        c.op("pool", lambda e: e.memset(ones_b[:], 1.0), W=[ones_b])
        epsc = P.sb("epsc", [128, 1], F32)
        c.op("pool", lambda e: e.memset(epsc[:], NORM_EPS), W=[epsc])
        xs2 = [P.sb("xs", [128, 4096], F32) for _ in range(2)]
        stg = xs2
        Wup = P.sb("Wup", [128, 8, 2 * DFF], BF16)
        for k in range(8):
            for half in range(2):
                s_ = stg[(2 * k + half) % 2]
                c.dma("sp", s_[:, 0:DFF], dr["w_up"][k][:, half * DFF:(half + 1) * DFF], s_.name, W=[s_])
                c.TS("dve", Wup[:, k, half * DFF:(half + 1) * DFF], s_[:, 0:DFF], vec[:, VEC_OFF["norm_ffn"] + k: VEC_OFF["norm_ffn"] + k + 1], ALU.mult, [s_, vec], [Wup])
        pp = [P.ps("pp", [128, 512]) for _ in range(6)]
        ppi = [0]

        def nextp():
            ppi[0] += 1
            return pp[ppi[0] % 6]
        sqb = P.sb("sqb", [128, 8, 512], BF16)
        xbs = [P.sb("xb", [128, 8, 512], BF16) for _ in range(2)]
        rstds = [P.sb("rstd", [128, 512], F32) for _ in range(2)]
        G = [P.sb("G", [128, 514], F32) for _ in range(3)]
        H = P.sb("H", [128, 22, 2], F32)
        c.op("pool", lambda e: e.memset(H[:], 0.0), W=[H])
        t1 = [P.sb("t1", [128, 512], F32) for _ in range(3)]
        sl = [P.sb("sl", [128, 512], F32) for _ in range(3)]
        ao = [P.sb("ao", [128, 512], BF16) for _ in range(3)]
        prev = None

        def fin(a_, pv, s_, j, tsl):
            c.TT("dve", a_[:], pv[:, :], s_[:], ALU.mult, [pv, s_], [a_])
            c.dma("sp", dr["a_fm"][j, :, tsl], a_[:], "st_" + a_.name, R=[a_])

        def loadx(b):
            tsl = slice(b * 512, (b + 1) * 512)
            x_ = xs2[b % 2]
            c.dma("sp", x_[:].rearrange("p (k t) -> p k t", t=512), dr["x2_fm"].rearrange("k p t -> p k t")[:, :, tsl], x_.name, W=[x_])

        def prologue(b):
            x_ = xs2[b % 2]
            x3 = x_[:].rearrange("p (k t) -> p k t", t=512)
            p = nextp()
            for k in range(8):
                c.ACT(sqb[:, k, :], x3[:, k, :], AF.Square, [x_], [sqb])
                yield
            for k in range(8):
                c.MM(p[:, :], ones_b[:], sqb[:, k, :], k == 0, k == 7, [ones_b, sqb], [p])
            c.RPOW(rstds[b % 2][:], p[:, :], [p], [rstds[b % 2]], -0.5, scale=1.0 / D, bias=epsc[:, 0:1], RB=[epsc])
            yield
            for k in range(8):
                c.TT("dve", xbs[b % 2][:, k, :], x3[:, k, :], rstds[b % 2][:], ALU.mult, [x_, rstds[b % 2]], [xbs[b % 2]])
                yield

        loadx(0)
        for _ in prologue(0):
            pass
        pro = None
        for b in range(NB):
            tsl = slice(b * 512, (b + 1) * 512)
            xb = xbs[b % 2]
            if b + 1 < NB:
                loadx(b + 1)
            for j in range(22):
                if j == 2 and b + 1 < NB:
                    pro = prologue(b + 1)
                if pro is not None:
                    try:
                        next(pro)
                    except StopIteration:
                        pro = None
                r3 = (b * 22 + j) % 3
                g_, t_, s_, a_ = G[r3], t1[r3], sl[r3], ao[r3]
                pg = nextp()
                for k in range(8):
                    c.MM(pg[:, :], Wup[:, k, j * 128:(j + 1) * 128], xb[:, k, :], k == 0, k == 7, [Wup, xb], [pg])
                pv = nextp()
                for k in range(8):
                    c.MM(pv[:, :], Wup[:, k, DFF + j * 128: DFF + (j + 1) * 128], xb[:, k, :], k == 0, k == 7, [Wup, xb], [pv])
                c.CP("act", g_[:, 0:2], H[:, j, :], [H], [g_])
                c.CP("act", g_[:, 2:514], pg[:, :], [pg], [g_])
                c.CP("act", H[:, j, :], g_[:, 512:514], [g_], [H])
                c.TS("dve", t_[:], g_[:, 0:512], V("conv_w0", j), ALU.mult, [g_, vec], [t_], s2=V("conv_b", j), op1=ALU.add)
                c.STT(t_[:], g_[:, 1:513], V("conv_w1", j), t_[:], ALU.mult, ALU.add, [g_, t_, vec], [t_])
                c.STT(t_[:], g_[:, 2:514], V("conv_w2", j), t_[:], ALU.mult, ALU.add, [g_, t_, vec], [t_])
                c.ACT(s_[:], t_[:], AF.Silu, [t_], [s_])
                if prev is not None:
                    fin(*prev)
                prev = (a_, pv, s_, j, tsl)
        fin(*prev)
        c.emit()


def phase_D3(nc, c, dr, T, outT):
    NB = T // 512
    with contextlib.ExitStack() as st:
        P = Pool_(nc, st)
        vec = P.sb("vec", [128, NVEC], F32)
        c.dma("sp", vec[:], dr["vecs"], "vec", W=[vec])
        V = lambda n, j=0: vec[:, VEC_OFF[n] + j: VEC_OFF[n] + j + 1]
        ones_b = P.sb("ones_b", [128, 128], BF16)
        c.op("pool", lambda e: e.memset(ones_b[:], 1.0), W=[ones_b])
        epsc = P.sb("epsc", [128, 1], F32)
        c.op("pool", lambda e: e.memset(epsc[:], NORM_EPS), W=[epsc])
        Wdt = [P.sb("Wd", [128, 2, D], BF16) for _ in range(11)]
        for jj in range(11):
            c.dma("pool", Wdt[jj][:], dr["w_down"].rearrange("k p n -> p k n")[:, 2 * jj:2 * jj + 2, :], Wdt[jj].name, W=[Wdt[jj]])
        pp = [P.ps("pp", [128, 512]) for _ in range(6)]
        ppi = [0]

        def nextp():
            ppi[0] += 1
            return pp[ppi[0] % 6]
        xs = [P.sb("xs", [128, 8, 512], F32) for _ in range(2)]
        ab = [P.sb("ab", [128, 22, 512], BF16) for _ in range(2)]
        sqb = P.sb("sqb", [128, 8, 512], BF16)
        rstd = P.sb("rstd", [128, 512], F32)
        def loads(b):
            tsl = slice(b * 512, (b + 1) * 512)
            x_, a_ = xs[b % 2], ab[b % 2]
            c.dma("sp", x_[:], dr["x2_fm"].rearrange("k p t -> p k t")[:, :, tsl], x_.name, W=[x_])
            for j0 in range(0, 22, 11):
                c.dma("sp", a_[:, j0:j0 + 11, :], dr["a_fm"].rearrange("k p t -> p k t")[:, j0:j0 + 11, tsl], a_.name + "_%d" % j0, W=[a_])

        loads(0)
        for b in range(NB):
            tsl = slice(b * 512, (b + 1) * 512)
            x_, a_ = xs[b % 2], ab[b % 2]
            if b + 1 < NB:
                loads(b + 1)
            for n in range(8):
                p = nextp()
                for j in range(22):
                    c.MM(p[:, :], Wdt[j // 2][:, j % 2, n * 128:(n + 1) * 128], a_[:, j, :], j == 0, j == 21, [Wdt[j // 2], a_], [p])
                c.TT("dve", x_[:, n, :], p[:, :], x_[:, n, :], ALU.add, [p, x_], [x_])
            p = nextp()
            rms_rstd(c, x_, sqb, ones_b, p, rstd, epsc)
            for n in range(8):
                c.STT(x_[:, n, :], x_[:, n, :], V("norm_final", n), rstd[:], ALU.mult, ALU.mult, [x_, rstd, vec], [x_])
            c.dma("sp", outT.rearrange("k p t -> p k t")[:, :, tsl], x_[:], "st_" + x_.name, R=[x_])
        c.emit()
```

```python
import contextlib
import math
import numpy as np
import concourse.bass as bass
import concourse.mybir as mybir
from concourse.bass_utils import run_bass_kernel_spmd

F32 = mybir.dt.float32
BF16 = mybir.dt.bfloat16
I32 = mybir.dt.int32
AF = mybir.ActivationFunctionType
ALU = mybir.AluOpType
AX = mybir.AxisListType

D = 1024
NIN = 3232
DFF = 2816
MEM = 256
LNX_EPS = 64e-5
SUBLN_EPS = 1e-5
NORM_EPS = 1e-6
LAMBDA_INIT = 0.2
SEQ_ONLY = False


class Buf:
    __slots__ = ("w", "r", "name")

    def __init__(self, name=""):
        self.w = None
        self.r = []
        self.name = name


class Tile:
    def __init__(self, t, name):
        self.t = t
        self.buf = Buf(name)
        self.name = name

    def __getitem__(self, idx):
        return self.t[idx]


def _b(x):
    return x.buf if isinstance(x, Tile) else x


class Ctx:
    ENG = ("pe", "act", "dve", "pool", "sp")

    def __init__(self, nc, stack):
        self.nc = nc
        self.stack = stack
        self.sem = {}
        for e in self.ENG:
            self.sem[e] = stack.enter_context(nc.semaphore("s_" + e))
        self.cnt = {e: 0 for e in self.ENG}
        self.known = {e: {} for e in self.ENG}
        self.ops = {e: [] for e in self.ENG}
        self.dsem = {}
        self.nops = 0

    def _collect(self, e, reads, writes):
        waits = {}

        def add(tok):
            if tok is None:
                return
            k, v = tok
            if waits.get(k, 0) < v:
                waits[k] = v
        for b in reads:
            add(b.w)
        for b in writes:
            add(b.w)
            for t in b.r:
                add(t)
        kn = self.known[e]
        out = []
        for k, v in waits.items():
            if e == "pe" and k == ("E", "pe"):
                continue
            if kn.get(k, 0) >= v:
                continue
            kn[k] = v
            out.append((k, v))
        return out

    def op(self, e, fn, R=(), W=()):
        R = [_b(x) for x in R]
        W = [_b(x) for x in W]
        waits = self._collect(e, R, W)
        self.cnt[e] += 1
        tok = (("E", e), self.cnt[e])
        for b in R:
            b.r.append(tok)
        for b in W:
            b.w = tok
            b.r = []
        self.ops[e].append((waits, fn, None))
        self.nops += 1

    def dma(self, q, out, in_, slot, R=(), W=()):
        R = [_b(x) for x in R]
        W = [_b(x) for x in W]
        if slot not in self.dsem:
            s = self.stack.enter_context(self.nc.semaphore("d_" + slot))
            self.dsem[slot] = [s, 0]
        waits = self._collect(q, R, W)
        self.dsem[slot][1] += 16
        tok = (("D", slot), self.dsem[slot][1])
        for b in R:
            b.r.append(tok)
        for b in W:
            b.w = tok
            b.r = []

        def fn(eng, out=out, in_=in_):
            return eng.dma_start(out=out, in_=in_)
        self.ops[q].append((waits, fn, slot))
        self.nops += 1

    def _semof(self, k):
        return self.sem[k[1]] if k[0] == "E" else self.dsem[k[1]][0]

    def emit(self):
        nc = self.nc
        waits = []
        for name, (s, cn) in self.dsem.items():
            k = ("D", name)
            if cn > 0 and self.known["sp"].get(k, 0) < cn:
                self.known["sp"][k] = cn
                waits.append((k, cn))
        if waits:
            self.ops["sp"].append((waits, None, None))
        ops = self.ops
        self.ops = {e: [] for e in self.ENG}
        with nc.Block() as block:
            def mk(e):
                def body(eng):
                    for waits, fn, slot in ops[e]:
                        for k, v in waits:
                            eng.wait_ge(self._semof(k), v)
                        if fn is None:
                            continue
                        inst = fn(eng)
                        if slot is None:
                            inst.then_inc(self.sem[e], 1)
                        else:
                            inst.then_inc(self.dsem[slot][0], 16)
                return body
            block.tensor(mk("pe"))
            block.scalar(mk("act"))
            block.vector(mk("dve"))
            block.gpsimd(mk("pool"))
            block.sync(mk("sp"))
        for e in self.ENG:
            for e2 in self.ENG:
                self.known[e][("E", e2)] = self.cnt[e2]
            for name, (s, cn) in self.dsem.items():
                self.known[e][("D", name)] = cn

    def ACT(self, out, in_, func, R, W, scale=1.0, bias=None):
        if bias is None:
            self.op("act", lambda e: e.activation(out=out, in_=in_, func=func, scale=scale), R, W)
        else:
            self.op("act", lambda e: e.activation(out=out, in_=in_, func=func, scale=scale, bias=bias), R, W)

    def RPOW(self, out, in_, R, W, power, scale=1.0, bias=None, tmp=None, RB=()):
        t = out if tmp is None else tmp
        self.ACT(t, in_, AF.Ln, list(R) + list(RB), W, scale=scale, bias=bias)
        self.ACT(out, t, AF.Exp, W, W, scale=power)

    def TT(self, eng, out, in0, in1, op, R, W):
        self.op(eng, lambda e: e.tensor_tensor(out=out, in0=in0, in1=in1, op=op), R, W)

    def TS(self, eng, out, in0, s1, op0, R, W, s2=None, op1=None):
        if op1 is None:
            self.op(eng, lambda e: e.tensor_scalar(out=out, in0=in0, scalar1=s1, scalar2=None, op0=op0), R, W)
        else:
            self.op(eng, lambda e: e.tensor_scalar(out=out, in0=in0, scalar1=s1, scalar2=s2, op0=op0, op1=op1), R, W)

    def STT(self, out, in0, scalar, in1, op0, op1, R, W):
        self.op("dve", lambda e: e.scalar_tensor_tensor(out=out, in0=in0, scalar=scalar, in1=in1, op0=op0, op1=op1), R, W)

    def CP(self, eng, out, in_, R, W):
        if eng == "act":
            self.op("act", lambda e: e.activation(out=out, in_=in_, func=AF.Copy), R, W)
        else:
            self.op(eng, lambda e: e.tensor_copy(out=out, in_=in_), R, W)

    def MM(self, out, lhsT, rhs, start, stop, R, W):
        self.op("pe", lambda e: e.matmul(out, lhsT=lhsT, rhs=rhs, start=start, stop=stop), R, W)

    def TR(self, out, in_, ident, R, W):
        self.op("pe", lambda e: e.transpose(out=out, in_=in_, identity=ident), R, W)


class Pool_:
    CNT = [0]

    def __init__(self, nc, st):
        self.nc = nc
        self.st = st

    def sb(self, name, shape, dt):
        Pool_.CNT[0] += 1
        nm = "%s_%d" % (name, Pool_.CNT[0])
        return Tile(self.st.enter_context(self.nc.sbuf_tensor(nm, shape, dt)), nm)

    def ps(self, name, shape, dt=F32):
        Pool_.CNT[0] += 1
        nm = "%s_%d" % (name, Pool_.CNT[0])
        return Tile(self.st.enter_context(self.nc.psum_tensor(nm, shape, dt)), nm)


VEC_SPEC = [
    ("mix_r", 4), ("mix_k", 4), ("mix_v", 4), ("mix_wa", 1), ("mix_g", 1),
    ("w0", 4), ("a0", 4), ("k_k", 4), ("k_a", 4), ("r_k", 4),
    ("norm_mix", 8), ("norm_cross", 8), ("norm_mem", 8), ("norm_ffn", 8), ("norm_final", 8),
    ("conv_w0", 22), ("conv_w1", 22), ("conv_w2", 22), ("conv_b", 22),
    ("inv_freq", 1), ("sin_scale", 1), ("subln", 1),
]
VEC_OFF = {}
_o = 0
for _n, _k in VEC_SPEC:
    VEC_OFF[_n] = _o
    _o += _k
NVEC = _o


def _cols(v, n):
    v = np.asarray(v, np.float32).reshape(-1)
    out = np.zeros((n * 128,), np.float32)
    out[: v.shape[0]] = v
    return np.ascontiguousarray(out.reshape(n, 128).T)


def cb(h):
    return (h % 2) * 4 + h // 2


def build(T, dbg=False):
    NB = T // 512
    NCH = T // 128
    nc = bass.Bass("TRN2", target_bir_lowering=False)
    dr = {}

    def din(name, shape, dt=F32):
        dr[name] = nc.dram_tensor(name, shape, dt, kind="ExternalInput").ap()
        return dr[name]

    def dscr(name, shape, dt):
        dr[name] = nc.dram_tensor(name, shape, dt, kind="ExternalOutput" if dbg else "Internal").ap()
        return dr[name]

    din("xT", [8, 128, T])
    din("memT", [8, 128, MEM])
    din("pos", [128, T], I32)
    din("vecs", [128, NVEC])
    din("w_in", [8, 128, NIN])
    din("lup", [64, 512])
    din("gup", [96, 512])
    din("tmb", [128, 3, 512])
    din("lam", [128, 4, 64])
    din("w_out", [8, 128, D])
    din("wq_c", [8, 128, D])
    din("wkv_c", [8, 128, 2 * D])
    din("wo_c", [8, 128, D])
    din("w_up", [8, 128, 2 * DFF])
    din("w_down", [22, 128, D])
    din("consts", [128, 6, 128])
    din("cmask", [128, 4, 512])
    outT = nc.dram_tensor("outT", [8, 128, T], F32, kind="ExternalOutput").ap()

    for nm in ("kt_fm", "bt_fm", "kk_fm", "rt_fm", "qh_fm", "kh_fm"):
        dscr(nm, [4, 128, T], BF16)
    for nm in ("bt_tm", "kk_tm", "v_tm", "va_tm"):
        dscr(nm, [T, 512], BF16)
    dscr("g_tm", [T, 512], F32)
    dscr("c_tm", [T, 8], F32)
    dscr("dm_fm", [128, 4, NCH], F32)
    dscr("ee_fm", [128, 4, NCH], F32)
    dscr("ymix_fm", [8, 128, T], BF16)
    dscr("x2_fm", [8, 128, T], F32)
    dscr("a_fm", [22, 128, T], BF16)

    with contextlib.ExitStack() as st0:
        c = Ctx(nc, st0)
        phase_A(nc, c, dr, T)
        phase_B(nc, c, dr, T)
        phase_C(nc, c, dr, T)
        phase_D1(nc, c, dr, T)
        phase_D2(nc, c, dr, T)
        phase_D3(nc, c, dr, T, outT)
    return nc


def load_consts(c, P, dr, q="sp"):
    cf = P.sb("cf", [128, 6, 128], F32)
    cbf = P.sb("cbf", [128, 6, 128], BF16)
    c.dma(q, cf[:], dr["consts"], "cf", W=[cf])
    c.CP("dve", cbf[:], cf[:], [cf], [cbf])
    return cf, cbf


def load_weight_bf16(c, P, dr, name, kc, n, wb, stg, gcol=None, vec=None, eng="dve"):
    for k in range(kc):
        s = stg[k % len(stg)]
        c.dma("sp", s[:, 0:n], dr[name][k], s.name, W=[s])
        if gcol is None:
            c.CP("dve" if k % 2 == 0 else "act", wb[:, k, :], s[:, 0:n], [s], [wb])
        else:
            c.TS("dve", wb[:, k, :], s[:, 0:n], vec[:, gcol + k:gcol + k + 1], ALU.mult, [s, vec], [wb])


def phase_A(nc, c, dr, T):
    NB = T // 512
    with contextlib.ExitStack() as st:
        P = Pool_(nc, st)
        cf, cbf = load_consts(c, P, dr)
        ident_b = cbf[:, 0, :]
        bones_f = cf[:, 4, :]
        bones_b = cbf[:, 4, :]
        swap_f = cf[:, 5, :]
        vec = P.sb("vec", [128, NVEC], F32)
        c.dma("sp", vec[:], dr["vecs"], "vec", W=[vec])
        V = lambda n, j=0: vec[:, VEC_OFF[n] + j: VEC_OFF[n] + j + 1]
        ones_f = P.sb("ones_f", [128, 128], BF16)
        c.op("pool", lambda e: e.memset(ones_f[:], 1.0), W=[ones_f])
        epsc = P.sb("epsc", [128, 2], F32)
        c.op("pool", lambda e: e.memset(epsc[:, 0:1], NORM_EPS), W=[epsc])
        c.op("pool", lambda e: e.memset(epsc[:, 1:2], 1e-18), W=[epsc])
        ones512 = P.sb("ones512", [128, 512], F32)
        c.op("pool", lambda e: e.memset(ones512[:], 1.0), W=[ones512])
        Wb = P.sb("Wb", [128, 8, NIN], BF16)
        xs = P.sb("xs", [128, 4096], F32)
        xs3 = xs[:].rearrange("p (k t) -> p k t", t=512)
        stg = [xs]
        load_weight_bf16(c, P, dr, "w_in", 8, NIN, Wb, stg, VEC_OFF["norm_mix"], vec)
        lupf = P.sb("lupf", [64, 512], F32)
        lupb = P.sb("lupb", [64, 512], BF16)
        gupf = P.sb("gupf", [96, 512], F32)
        gupb = P.sb("gupb", [96, 512], BF16)
        c.dma("sp", lupf[:], dr["lup"], "lupf", W=[lupf])
        c.dma("sp", gupf[:], dr["gup"], "gupf", W=[gupf])
        c.CP("dve", lupb[:], lupf[:], [lupf], [lupb])
        c.CP("dve", gupb[:], gupf[:], [gupf], [gupb])

        sq = P.sb("sq", [128, 8, 512], BF16)
        xb = P.sb("xb", [128, 8, 512], BF16)
        rstd = P.sb("rstd", [128, 512], F32)
        pp = [P.ps("pp", [128, 512]) for _ in range(6)]
        ptr = [P.ps("ptr", [128, 512], BF16) for _ in range(2)]
        ppi = [0]

        def nextp():
            ppi[0] += 1
            return pp[ppi[0] % 6]
        tri = [0]

        def nexttr():
            tri[0] += 1
            return ptr[tri[0] % 2]

        Hz = P.sb("Hz", [128, 14], F32)
        c.op("pool", lambda e: e.memset(Hz[:], 0.0), W=[Hz])
        zs_wa = P.sb("zs_wa", [128, 512], F32)
        zs_g = P.sb("zs_g", [128, 512], F32)
        th_b = P.sb("th_b", [64, 512], BF16)
        sg_b = P.sb("sg_b", [96, 512], BF16)
        NSET = 2
        sets = []
        for si in range(NSET):
            sets.append({
                "tmp": [P.sb("tmpA", [128, 512], F32) for _ in range(11)],
                "z": [P.sb("zt", [128, 513], F32) for _ in range(3)],
                "bft": [P.sb("bfA", [128, 512], BF16) for _ in range(6)],
                "trs": [P.sb("trs", [128, 512], BF16) for _ in range(2)],
            })
        tmp = sets[0]["tmp"]
        bft2 = [P.sb("bfQ", [128, 512], BF16) for _ in range(2)]
        trsA = bft2
        cts = P.sb("cts", [128, 4, 8], F32)
        dmt = P.sb("dmt", [128, 4, 4], F32)
        eet = P.sb("eet", [128, 4, 4], F32)
        posi = P.sb("posi", [128, 512], I32)
        ropei = P.sb("ropei", [128, 512], I32)
        ropeT = [P.sb("ropeT", [128, 512], F32) for _ in range(8)]
        cosT, sinT = ropeT[0], ropeT[1]
        gts = zs_wa

        def proj_fm(cols, ncol):
            p = nextp()
            for k in range(8):
                c.MM(p[0:ncol, :], Wb[:, k, cols:cols + ncol], xb[:, k, :], k == 0, k == 7, [Wb, xb], [p])
            return p

        def zproj(cols, ncol, dst, hidx):
            p = proj_fm(cols, ncol)
            c.CP("act", dst[0:ncol, 0:1], Hz[0:ncol, hidx:hidx + 1], [Hz], [dst])
            c.CP("act", dst[0:ncol, 1:513], p[0:ncol, :], [p], [dst])
            c.CP("act", Hz[0:ncol, hidx:hidx + 1], dst[0:ncol, 512:513], [dst], [Hz])

        def shift(dst_t, dst, src, ncol, mixcol, d):
            c.TT("pool", d[0:ncol, :], src[0:ncol, 0:512], src[0:ncol, 1:513], ALU.subtract, [src], [d])
            c.STT(dst, d[0:ncol, :], mixcol, src[0:ncol, 1:513], ALU.mult, ALU.add, [d, src, vec], [dst_t])

        def jchain(j, S, b):
            tsl = slice(b * 512, (b + 1) * 512)
            jsl = slice(j * 128, (j + 1) * 128)
            tmp = S["tmp"]
            zr, zk, zv = S["z"]
            lw, cl, rel, epos, eneg, eprev, av, kpr, t1, t2 = tmp[1:11]
            rsh, vsh, kap = lw, cl, rel
            p = nextp()
            c.MM(p[:, :], lupb[0:32, jsl], th_b[0:32, :], True, True, [lupb, th_b], [p])
            c.ACT(lw[:], p[:, :], AF.Sigmoid, [p, vec], [lw], bias=V("w0", j))
            p = nextp()
            c.MM(p[:, :], lupb[32:64, jsl], th_b[32:64, :], True, True, [lupb, th_b], [p])
            c.ACT(av[:], p[:, :], AF.Sigmoid, [p, vec], [av], bias=V("a0", j))
            yield
            c.TS("dve", lw[:], lw[:], -0.6065306597126334, ALU.mult, [lw], [lw])
            c.op("dve", lambda e, cl=cl, lw=lw: e.tensor_tensor_scan(out=cl[:], data0=ones512[:], data1=lw[:], initial=0.0, op0=ALU.mult, op1=ALU.add), [ones512, lw], [cl])
            yield
            for cc in range(4):
                s_ = slice(cc * 128, (cc + 1) * 128)
                c.TS("dve", rel[:, s_], cl[:, s_], cl[:, cc * 128 + 63: cc * 128 + 64], ALU.subtract, [cl], [rel])
            yield
            c.ACT(epos[:], rel[:], AF.Exp, [rel], [epos])
            c.ACT(eneg[:], rel[:], AF.Exp, [rel], [eneg], scale=-1.0)
            c.TT("dve", t1[:], rel[:], lw[:], ALU.subtract, [rel, lw], [t1])
            yield
            c.ACT(eprev[:], t1[:], AF.Exp, [t1], [eprev])
            c.ACT(dmt[:, j, :], t1[:].rearrange("p (c t) -> p c t", t=128)[:, :, 0], AF.Exp, [t1], [dmt], scale=-1.0)
            c.CP("act", eet[:, j, :], epos[:].rearrange("p (c t) -> p c t", t=128)[:, :, 127], [epos], [eet])
            yield
            zproj(0 + j * 128, 128, zr, 2 + j)
            yield
            shift(rsh, rsh[:], zr, 128, V("mix_r", j), tmp[0])
            yield
            zproj(512 + j * 128, 128, zk, 6 + j)
            yield
            ksh = t2
            shift(ksh, ksh[:], zk, 128, V("mix_k", j), tmp[0])
            yield
            zproj(1024 + j * 128, 128, zv, 10 + j)
            yield
            shift(vsh, vsh[:], zv, 128, V("mix_v", j), tmp[0])
            yield
            kr = tmp[0]
            c.ACT(kr[:], ksh[:], AF.Copy, [ksh, vec], [kr], scale=V("k_k", j))
            sqk = S["bft"][5]
            c.ACT(sqk[:], kr[:], AF.Square, [kr], [sqk])
            yield
            p = nextp()
            c.MM(p[:, :], bones_b, sqk[:], True, True, [cbf, sqk], [p])
            c.RPOW(t1[:], p[:, :], [p], [t1], -0.5, bias=epsc[:, 1:2], RB=[epsc])
            yield
            c.TT("dve", kap[:], kr[:], t1[:], ALU.mult, [kr, t1], [kap])
            yield
            c.TS("dve", t1[:], av[:], -1.0, ALU.add, [av, vec], [t1], s2=V("k_a", j), op1=ALU.mult)
            c.STT(kpr[:], t1[:], 1.0, ksh[:], ALU.add, ALU.mult, [t1, ksh], [kpr])
            yield
            o_kt, o_bt, o_kk, o_rt, o_v, o_rk = S["bft"]
            c.TT("dve", o_kt[:], kap[:], eprev[:], ALU.mult, [kap, eprev], [o_kt])
            c.TT("pool", t1[:], kap[:], av[:], ALU.mult, [kap, av], [t1])
            yield
            c.TT("dve", o_bt[:], t1[:], eneg[:], ALU.mult, [t1, eneg], [o_bt])
            c.TT("dve", o_kk[:], kpr[:], eneg[:], ALU.mult, [kpr, eneg], [o_kk])
            c.TT("pool", o_rt[:], rsh[:], epos[:], ALU.mult, [rsh, epos], [o_rt])
            c.CP("act", o_v[:], vsh[:], [vsh], [o_v])
            yield
            c.TT("pool", t1[:], rsh[:], kpr[:], ALU.mult, [rsh, kpr], [t1])
            yield
            c.ACT(o_rk[:], t1[:], AF.Copy, [t1, vec], [o_rk], scale=V("r_k", j))
            for nm, tl in (("kt_fm", o_kt), ("bt_fm", o_bt), ("kk_fm", o_kk), ("rt_fm", o_rt)):
                c.dma("sp", dr[nm][j, :, tsl], tl[:], "st_" + tl.name, R=[tl])
            yield
            for ti_, (nm, tl) in enumerate((("bt_tm", o_bt), ("kk_tm", o_kk), ("v_tm", o_v))):
                pt = nexttr()
                for tt in range(4):
                    c.TR(pt[:, tt * 128:(tt + 1) * 128], tl[:, tt * 128:(tt + 1) * 128], ident_b, [tl, cbf], [pt])
                ts_ = S["trs"][ti_ % 2]
                c.CP("act" if ti_ % 2 == 0 else "dve", ts_[:], pt[:, :], [pt], [ts_])
                c.dma("sp", dr[nm].rearrange("(n p) f -> p n f", p=128)[:, b * 4:(b + 1) * 4, jsl],
                      ts_[:].rearrange("p (n f) -> p n f", f=128), "st_" + ts_.name, R=[ts_])
                yield
            p = nextp()
            for tt in range(4):
                c.MM(p[:, tt * 2:tt * 2 + 2], o_rk[:, tt * 128:(tt + 1) * 128],
                     cbf[:, 4, :].rearrange("p (i k) -> p i k", k=64)[:, :, 0], True, True, [o_rk, cbf], [p])
            c.CP("act", cts[:, :, 2 * j:2 * j + 2], p[:, 0:8].rearrange("p (t i) -> p t i", i=2), [p], [cts])
            yield


        def achain(b):
            tsl = slice(b * 512, (b + 1) * 512)
            c.dma("sp", posi[:], dr["pos"][:, tsl], "posi", W=[posi])
            u, uc, kf, f1 = ropeT[2:6]
            c.CP("dve", u[:], posi[:], [posi], [u])
            c.TS("dve", u[:], u[:], V("inv_freq"), ALU.mult, [u, vec], [u], s2=1.0 / (2 * math.pi), op1=ALU.mult)
            yield
            for which, dst in ((0, sinT), (1, cosT)):
                if which == 1:
                    c.TS("dve", uc[:], u[:], 0.25, ALU.add, [u], [uc])
                    src = uc
                else:
                    src = u
                c.CP("dve", ropei[:], src[:], [src], [ropei])
                yield
                c.CP("dve", kf[:], ropei[:], [ropei], [kf])
                yield
                c.TT("dve", f1[:], src[:], kf[:], ALU.subtract, [src, kf], [f1])
                yield
                c.STT(kf[:], f1[:], 0.5, f1[:], ALU.is_gt, ALU.subtract, [f1], [kf])
                yield
                if which == 0:
                    c.ACT(dst[:], kf[:], AF.Sin, [kf, vec], [dst], scale=V("sin_scale"))
                else:
                    c.ACT(dst[:], kf[:], AF.Sin, [kf], [dst], scale=-6.283185)
                yield
            qfs = [ropeT[2], ropeT[3]]
            t1s = [ropeT[4], ropeT[5]]
            t2s = [ropeT[6], ropeT[7]]
            it = 0
            for which, (c0, nm) in enumerate(((1696, "qh_fm"), (2208, "kh_fm"))):
                for j in range(4):
                    qf_, t1, t2 = qfs[it % 2], t1s[it % 2], t2s[it % 2]
                    p = proj_fm(c0 + j * 128, 128)
                    c.CP("act", qf_[:], p[:, :], [p], [qf_])
                    yield
                    p2 = nextp()
                    c.MM(p2[:, :], swap_f, qf_[:], True, True, [cf, qf_], [p2])
                    c.TT("dve", t2[:], p2[:, :], sinT[:], ALU.mult, [p2, sinT], [t2])
                    c.TT("pool", t1[:], qf_[:], cosT[:], ALU.mult, [qf_, cosT], [t1])
                    yield
                    ob = bft2[it % 2]
                    c.TT("dve", ob[:], t1[:], t2[:], ALU.add, [t1, t2], [ob])
                    c.dma("sp", dr[nm][j, :, tsl], ob[:], "st_" + ob.name, R=[ob])
                    it += 1
                    yield
            for tt in range(4):
                p = nextp()
                for k in range(8):
                    c.MM(p[:, :], xb[:, k, tt * 128:(tt + 1) * 128], Wb[:, k, 2720:3232], k == 0, k == 7, [xb, Wb], [p])
                ts_ = trsA[tt % 2]
                c.CP("act" if tt % 2 == 0 else "dve", ts_[:], p[:, :], [p], [ts_])
                c.dma("sp", dr["va_tm"][b * 512 + tt * 128: b * 512 + (tt + 1) * 128, :], ts_[:], "st_" + ts_.name, R=[ts_])
                yield

        def achain_head(gen, n):
            for _ in range(n):
                try:
                    next(gen)
                except StopIteration:
                    return
                yield

        def run_slots(slots):
            cur = [None] * len(slots)
            live = True
            while live:
                live = False
                for si, sl_ in enumerate(slots):
                    while True:
                        if cur[si] is None:
                            if not sl_:
                                break
                            cur[si] = sl_.pop(0)
                        try:
                            next(cur[si])
                            live = True
                            break
                        except StopIteration:
                            cur[si] = None

        for b in range(NB):
            tsl = slice(b * 512, (b + 1) * 512)
            if b == 0:
                c.dma("sp", xs3, dr["xT"].rearrange("k p t -> p k t")[:, :, tsl], "xs", W=[xs])
            c.ACT(sq[:], xs3, AF.Square, [xs], [sq])
            p = nextp()
            for k in range(8):
                c.MM(p[:, :], ones_f[:], sq[:, k, :], k == 0, k == 7, [ones_f, sq], [p])
            c.RPOW(rstd[:], p[:, :], [p], [rstd], -0.5, scale=1.0 / D, bias=epsc[:, 0:1], RB=[epsc])
            c.TT("dve", xb[:], xs3, rstd[:].unsqueeze(1).to_broadcast([128, 8, 512]), ALU.mult, [xs, rstd], [xb])
            if b + 1 < NB:
                c.dma("sp", xs3, dr["xT"].rearrange("k p t -> p k t")[:, :, slice((b + 1) * 512, (b + 2) * 512)], "xs", W=[xs])
            lora_done = [False]

            def lora(b=b):
                zw_ = sets[0]["z"][0]
                zproj(1536, 64, zw_, 0)
                yield
                shift(zs_wa, zs_wa[0:64, :], zw_, 64, V("mix_wa")[0:64, :], tmp[0])
                yield
                c.ACT(th_b[0:32, :], zs_wa[0:32, :], AF.Tanh, [zs_wa], [th_b])
                c.CP("act", th_b[32:64, :], zs_wa[32:64, :], [zs_wa], [th_b])
                yield
                zg_ = sets[1]["z"][0]
                zproj(1600, 96, zg_, 1)
                yield
                shift(zs_g, zs_g[0:96, :], zg_, 96, V("mix_g")[0:96, :], sets[1]["tmp"][0])
                yield
                c.ACT(sg_b[:, :], zs_g[0:96, :], AF.Sigmoid, [zs_g], [sg_b])
                yield
                for tt in range(4):
                    p = nextp()
                    c.MM(p[:, :], sg_b[:, tt * 128:(tt + 1) * 128], gupb[:, :], True, True, [sg_b, gupb], [p])
                    c.CP("dve", gts[:], p[:, :], [p], [gts])
                    c.dma("sp", dr["g_tm"][b * 512 + tt * 128: b * 512 + (tt + 1) * 128, :], gts[:], "st_gts", R=[gts])
                    yield
                lora_done[0] = True

            def guard(gen):
                assert lora_done[0], "jchain started before lora finished"
                yield from gen

            ach = achain(b)
            run_slots([[lora(), guard(jchain(0, sets[0], b)), guard(jchain(2, sets[0], b))],
                       [achain_head(ach, 12), guard(jchain(1, sets[1], b)), guard(jchain(3, sets[1], b))],
                       [ach]])
            c.dma("sp", dr["c_tm"].rearrange("(n p) h -> p n h", p=128)[:, b * 4:(b + 1) * 4, :], cts[:], "st_cts", R=[cts])
            c.dma("sp", dr["dm_fm"][:, :, b * 4:(b + 1) * 4], dmt[:], "st_dmt", R=[dmt])
            c.dma("sp", dr["ee_fm"][:, :, b * 4:(b + 1) * 4], eet[:], "st_eet", R=[eet])

        c.emit()


def make_consts():
    cm = np.zeros((128, 6, 128), np.float32)
    p = np.arange(128)[:, None]
    m = np.arange(128)[None, :]
    cm[:, 0, :] = (p == m)
    cm[:, 1, :] = (p < m)
    cm[:, 2, :] = (p <= m)
    cm[:, 3, :] = (p > m)
    cm[:, 4, :] = (p // 64 == m // 64)
    partner = np.where((np.arange(128) % 64) < 32, np.arange(128) + 32, np.arange(128) - 32)
    sw = np.zeros((128, 128), np.float32)
    sw[partner, np.arange(128)] = 1.0
    cm[:, 5, :] = sw
    cmask = np.zeros((128, 4, 512), np.float32)
    q = np.arange(512)[None, :]
    for i in range(4):
        cmask[:, i, :] = (128 * i + p <= q)
    return cm, cmask


def pack_shared(inp):
    g = lambda n: np.asarray(inp[n], np.float32)
    vec = np.zeros((128, NVEC), np.float32)

    def put(name, v, n):
        vec[:, VEC_OFF[name]:VEC_OFF[name] + n] = _cols(v, n)
    sm = g("shift_mix")[0]
    put("mix_r", sm[0:512], 4)
    put("mix_k", sm[512:1024], 4)
    put("mix_v", sm[1024:1536], 4)
    put("mix_wa", sm[1536:1600], 1)
    put("mix_g", sm[1600:1696], 1)
    for n in ("w0", "a0", "k_k", "k_a", "r_k"):
        put(n, g(n)[0].reshape(-1), 4)
    for n in ("norm_mix", "norm_cross", "norm_mem", "norm_ffn"):
        put(n, g(n)[0], 8)
    put("norm_final", g("norm_final"), 8)
    cw = g("conv_w")[0]
    for j in range(3):
        put("conv_w%d" % j, cw[j], 22)
    put("conv_b", g("conv_b")[0], 22)
    pidx = np.arange(128) % 32
    inv = (10000.0 ** (-(2.0 * pidx.astype(np.float32)) / 64.0)).astype(np.float32)
    vec[:, VEC_OFF["inv_freq"]] = inv
    sgn = np.where((np.arange(128) % 64) < 32, -1.0, 1.0).astype(np.float32)
    vec[:, VEC_OFF["sin_scale"]] = -6.283185 * sgn
    vec[:, VEC_OFF["subln"]] = g("subln_gain")[0]
    tmb = np.zeros((128, 3, 512), np.float32)
    tmb[:, 0, :] = g("lnx_gain")[0][None, :]
    tmb[:, 1, :] = g("lnx_bias")[0][None, :]
    tmb[:, 2, :] = np.tile(g("subln_gain")[0], 4)[None, :]
    lam = np.zeros((128, 4, 64), np.float32)
    for i, n in enumerate(("lam_q1", "lam_k1", "lam_q2", "lam_k2")):
        lam[:, i, :] = g(n)[0][None, :]
    cm, cmask = make_consts()
    sh = {
        "vecs": vec, "tmb": tmb, "lam": lam, "consts": cm, "cmask": cmask,
        "w_in": np.ascontiguousarray(g("w_in")[0].reshape(8, 128, NIN)),
        "lup": np.ascontiguousarray(np.concatenate([g("w_lora_up")[0], g("a_lora_up")[0]], 0)),
        "gup": np.ascontiguousarray(g("g_lora_up")[0]),
        "w_out": np.ascontiguousarray(g("w_out")[0].reshape(8, 128, D)),
        "wq_c": np.ascontiguousarray(g("wq_c")[0].reshape(8, 128, D)),
        "wkv_c": np.ascontiguousarray(g("wkv_c")[0].reshape(8, 128, 2 * D)),
        "wo_c": np.ascontiguousarray(g("wo_c")[0].reshape(8, 128, D)),
        "w_up": np.ascontiguousarray(g("w_up")[0].reshape(8, 128, 2 * DFF)),
        "w_down": np.ascontiguousarray(g("w_down")[0].reshape(22, 128, D)),
    }
    return sh


def pack_core(inp, b, T):
    x = np.asarray(inp["x"], np.float32)[b]
    mem = np.asarray(inp["mem"], np.float32)[b]
    pos = np.asarray(inp["positions"], np.int32)[b]
    return {
        "xT": np.ascontiguousarray(x.T.reshape(8, 128, T)),
        "memT": np.ascontiguousarray(mem.T.reshape(8, 128, MEM)),
        "pos": np.ascontiguousarray(np.broadcast_to(pos[None, :], (128, T))),
    }


_NC_CACHE = {}


def kernel(**inputs):
    x = np.asarray(inputs["x"])
    B, T, _ = x.shape
    if T not in _NC_CACHE:
        _NC_CACHE[T] = build(T)
    nc = _NC_CACHE[T]
    sh = pack_shared(inputs)
    in_maps = []
    for b in range(B):
        m = dict(sh)
        m.update(pack_core(inputs, b, T))
        in_maps.append(m)
    res = run_bass_kernel_spmd(nc, in_maps, core_ids=list(range(B)))
    out = np.stack([np.ascontiguousarray(res.results[b]["outT"].reshape(D, T).T) for b in range(B)], 0)
    return out.astype(np.float32)


def phase_B(nc, c, dr, T):
    NCH = T // 128
    with contextlib.ExitStack() as st:
        P = Pool_(nc, st)
        cf, cbf = load_consts(c, P, dr)
        ident_b = cbf[:, 0, :]
        mSU = P.sb("mSU", [128, 8, 128], F32)
        mUI = P.sb("mUI", [128, 8, 128], F32)
        mSL = P.sb("mSL", [128, 8, 128], F32)
        idr = P.sb("idr", [128, 8, 128], BF16)
        for h in range(8):
            c.CP("pool", mSU[:, h, :], cf[:, 1, :], [cf], [mSU])
            c.CP("pool", mUI[:, h, :], cf[:, 2, :], [cf], [mUI])
            c.CP("pool", mSL[:, h, :], cf[:, 3, :], [cf], [mSL])
            c.CP("pool", idr[:, h, :], cf[:, 0, :], [cf], [idr])
        tmb = P.sb("tmb", [128, 3, 512], F32)
        c.dma("sp", tmb[:], dr["tmb"], "tmb", W=[tmb])
        dm = P.sb("dm", [128, 4, NCH], F32)
        ee = P.sb("ee", [128, 4, NCH], F32)
        c.dma("sp", dm[:], dr["dm_fm"], "dm", W=[dm])
        c.dma("sp", ee[:], dr["ee_fm"], "ee", W=[ee])
        S0 = P.sb("S0", [128, 4, 64], F32)
        c.op("pool", lambda e: e.memset(S0[:], 0.0), W=[S0])
        Sm = P.sb("Sm", [128, 4, 64], F32)
        Smb = P.sb("Smb", [128, 4, 64], BF16)
        NBUF = 2
        fm = {n: [P.sb(n, [128, 4, 128], BF16) for _ in range(NBUF)] for n in ("kt", "bt", "kk", "rt")}
        tm = {n: [P.sb(n, [128, 512], BF16) for _ in range(NBUF)] for n in ("btT", "kkT", "vT")}
        gt = [P.sb("gt", [128, 512], F32) for _ in range(NBUF)]
        ct = [P.sb("ct", [128, 8], F32) for _ in range(NBUF)]
        Zb = P.sb("Zb", [128, 2, 256], BF16)
        nU = P.sb("nU", [128, 512], BF16)
        y32 = P.sb("y32", [128, 512], F32)
        ysq = P.sb("ysq", [128, 512], F32)
        vf = P.sb("vf", [128, 512], F32)
        st8 = [P.sb("st8", [128, 8], F32) for _ in range(4)]
        yo = P.sb("yo", [128, 512], BF16)
        yT = P.sb("yT", [128, 4, 128], BF16)
        QA = P.ps("QA", [128, 1024])
        SP = P.ps("SPs", [128, 512])
        TP = P.ps("TPs", [128, 512], BF16)
        dpi = [0]

        def nextd():
            dpi[0] += 1
            return DP[dpi[0] % 2]

        def HP(i):
            return slice(i * 64, (i + 1) * 64)

        def mk(name, n=2):
            return [[P.sb(name, [128, 4, 128], BF16) for _ in range(2)] for _ in range(n)]
        AakH, ArbH, ArkH, XTfH = mk("AakH"), mk("ArbH"), mk("ArkH"), mk("XTfH")
        NbH, MbH, XTH = mk("NbH"), mk("MbH"), mk("XTH")
        DPh = [[P.ps("DPh", [128, 512]) for _ in range(2)] for _ in range(2)]
        dph_i = [0, 0]

        def nexth(half):
            dph_i[half] += 1
            return DPh[half][dph_i[half] % 2]

        def gen_load(ch):
            bi = ch % NBUF
            tsl = slice(ch * 128, (ch + 1) * 128)
            for n, src in (("kt", "kt_fm"), ("bt", "bt_fm"), ("kk", "kk_fm"), ("rt", "rt_fm")):
                t_ = fm[n][bi]
                c.dma("sp", t_[:], dr[src].rearrange("j p t -> p j t")[:, :, tsl], t_.name, W=[t_])
            for n, src in (("btT", "bt_tm"), ("kkT", "kk_tm"), ("vT", "v_tm")):
                t_ = tm[n][bi]
                c.dma("sp", t_[:], dr[src][tsl, :], t_.name, W=[t_])
            c.dma("sp", gt[bi][:], dr["g_tm"][tsl, :], gt[bi].name, W=[gt[bi]])
            c.dma("sp", ct[bi][:], dr["c_tm"][tsl, :], ct[bi].name, W=[ct[bi]])

        def gen_pre(ch, half):
            bi = ch % NBUF
            kt, bt, kk, rt = fm["kt"][bi], fm["bt"][bi], fm["kk"][bi], fm["rt"][bi]
            Aak, Arb, Ark = AakH[ch % 2][half], ArbH[ch % 2][half], ArkH[ch % 2][half]
            Nb, Mb, XT = NbH[half], MbH[half], XTH[half]
            hp = slice(half * 64, (half + 1) * 64)
            fl = lambda t_: t_[:].rearrange("p h t -> p (h t)")
            mk4 = lambda m_: m_[:, 0:4, :].rearrange("p h t -> p (h t)")

            def headmm(dst, A, Bm):
                for j in range(4):
                    c.MM(dst[:, j * 128:(j + 1) * 128], A[hp, j, :], Bm[hp, j, :], True, True, [A, Bm], [dst])

            d = nexth(half); headmm(d, bt, kt)
            N0 = Nb[0]
            c.STT(fl(N0), d[:, :], -1.0, mk4(mSU), ALU.mult, ALU.mult, [d, mSU], [N0])
            yield
            d = nexth(half); headmm(d, kt, bt)
            M0 = Mb[0]
            c.STT(fl(M0), d[:, :], -1.0, mk4(mSL), ALU.mult, ALU.mult, [d, mSL], [M0])
            yield
            d = nexth(half); headmm(d, kk, kt)
            c.TT("dve", fl(Aak), d[:, :], mk4(mSU), ALU.mult, [d, mSU], [Aak])
            yield
            d = nexth(half); headmm(d, bt, rt)
            c.TT("dve", fl(Arb), d[:, :], mk4(mUI), ALU.mult, [d, mUI], [Arb])
            yield
            d = nexth(half); headmm(d, kk, rt)
            c.TT("dve", fl(Ark), d[:, :], mk4(mUI), ALU.mult, [d, mUI], [Ark])
            yield
            xc = XT[0]
            c.TT("dve", fl(xc), fl(N0), idr[:, 0:4, :].rearrange("p h t -> p (h t)"), ALU.add, [N0, idr], [xc])
            yield
            Nc, Mc = N0, M0
            for lv in range(1, 7):
                Nn, Mn = Nb[lv % 2], Mb[lv % 2]
                dM = nexth(half)
                for j in range(4):
                    c.MM(dM[:, j * 128:(j + 1) * 128], Nc[:, j, :], Mc[:, j, :], True, True, [Nc, Mc], [dM])
                c.CP("act", fl(Mn), dM[:, :], [dM], [Mn])
                yield
                if lv < 6:
                    dN = nexth(half)
                    for j in range(4):
                        c.MM(dN[:, j * 128:(j + 1) * 128], Mc[:, j, :], Nc[:, j, :], True, True, [Nc, Mc], [dN])
                    c.CP("act", fl(Nn), dN[:, :], [dN], [Nn])
                    yield
                dX = nexth(half)
                for j in range(4):
                    c.MM(dX[:, j * 128:(j + 1) * 128], Mn[:, j, :], xc[:, j, :], True, True, [Mn, xc], [dX])
                xn = XT[lv % 2] if lv < 6 else XTfH[ch % 2][half]
                c.TT("dve", fl(xn), dX[:, :], fl(xc), ALU.add, [dX, xc], [xn])
                yield
                xc, Nc, Mc = xn, Nn, Mn

        def gen_seq(ch):
            bi = ch % NBUF
            tsl = slice(ch * 128, (ch + 1) * 128)
            AakQ, ArbQ, ArkQ, XTQ = AakH[ch % 2], ArbH[ch % 2], ArkH[ch % 2], XTfH[ch % 2]
            kt, bt, kk, rt = fm["kt"][bi], fm["bt"][bi], fm["kk"][bi], fm["rt"][bi]
            btT, kkT, vT = tm["btT"][bi], tm["kkT"][bi], tm["vT"][bi]
            for j in range(4):
                c.TS("dve", Sm[:, j, :], S0[:, j, :], dm[:, j, ch:ch + 1], ALU.mult, [S0, dm], [Sm])
                yield
            c.CP("act", Smb[:], Sm[:], [Sm], [Smb])
            yield
            dZ = QA
            for h in range(8):
                j, i = h // 2, h % 2
                o = dZ[:, i * 512 + j * 64: i * 512 + (j + 1) * 64]
                c.MM(o, kt[HP(i), j, :], Smb[HP(i), j, :], True, False, [kt, Smb], [dZ])
                c.MM(o, AakQ[i][:, j, :], vT[:, h * 64:(h + 1) * 64], False, True, [AakQ[i], vT], [dZ])
            c.CP("act", Zb[:], dZ[:, :].rearrange("p (i x) -> p i x", i=2)[:, :, 0:256], [dZ], [Zb])
            yield
            for h in range(8):
                j, i = h // 2, h % 2
                c.MM(SP[:, h * 64:(h + 1) * 64], XTQ[i][:, j, :], Zb[:, i, j * 64:(j + 1) * 64], True, True, [XTQ[i], Zb], [SP])
            c.ACT(nU[:], SP[:, :], AF.Copy, [SP], [nU], scale=-1.0)
            yield
            dY = QA
            for h in range(8):
                j, i = h // 2, h % 2
                o = dY[:, i * 512 + j * 64: i * 512 + (j + 1) * 64]
                hs = slice(h * 64, (h + 1) * 64)
                c.MM(o, rt[HP(i), j, :], Smb[HP(i), j, :], True, False, [rt, Smb], [dY])
                c.MM(o, ArbQ[i][:, j, :], nU[:, hs], False, False, [ArbQ[i], nU], [dY])
                c.MM(o, ArkQ[i][:, j, :], vT[:, hs], False, True, [ArkQ[i], vT], [dY])
            dS = SP
            for h in range(8):
                j, i = h // 2, h % 2
                hs = slice(h * 64, (h + 1) * 64)
                o = dS[HP(i), j * 64:(j + 1) * 64]
                c.MM(o, btT[:, hs], nU[:, hs], True, False, [btT, nU], [dS])
                c.MM(o, kkT[:, hs], vT[:, hs], False, True, [kkT, vT], [dS])
            c.TT("dve", Sm[:].rearrange("p j v -> p (j v)"), dS[:, 0:256], Sm[:].rearrange("p j v -> p (j v)"), ALU.add, [dS, Sm], [Sm])
            yield
            for j in range(4):
                c.TS("dve", S0[:, j, :], Sm[:, j, :], ee[:, j, ch:ch + 1], ALU.mult, [Sm, ee], [S0])
                yield
            y4 = y32[:].rearrange("p (j i v) -> p i j v", i=2, v=64)
            for i in range(2):
                c.CP("act", y4[:, i, :, :], dY[:, i * 512:i * 512 + 256].rearrange("p (j v) -> p j v", v=64), [dY], [y32])
                yield
            y3 = y32[:].rearrange("p (h v) -> p h v", v=64)
            s1, s2, mean, rs = st8
            c.op("dve", lambda e, s1=s1, y3=y3: e.tensor_reduce(out=s1[:], in_=y3, axis=AX.X, op=ALU.add), [y32], [s1])
            c.ACT(ysq[:], y32[:], AF.Square, [y32], [ysq])
            yield
            c.op("dve", lambda e, s2=s2: e.tensor_reduce(out=s2[:], in_=ysq[:].rearrange("p (h v) -> p h v", v=64), axis=AX.X, op=ALU.add), [ysq], [s2])
            c.TS("dve", mean[:], s1[:], 1.0 / 64, ALU.mult, [s1], [mean])
            yield
            c.TT("dve", s1[:], mean[:], mean[:], ALU.mult, [mean], [s1])
            yield
            c.STT(s2[:], s2[:], 1.0 / 64, s1[:], ALU.mult, ALU.subtract, [s2, s1], [s2])
            yield
            c.ACT(rs[:], s2[:], AF.Sqrt, [s2], [rs], bias=LNX_EPS)
            yield
            c.op("dve", lambda e, rs=rs: e.reciprocal(out=rs[:], in_=rs[:]), [rs], [rs])
            c.TT("dve", y3, y3, mean[:].unsqueeze(2).to_broadcast([128, 8, 64]), ALU.subtract, [y32, mean], [y32])
            yield
            c.TT("dve", y3, y3, rs[:].unsqueeze(2).to_broadcast([128, 8, 64]), ALU.mult, [y32, rs], [y32])
            yield
            c.TT("pool", y32[:], y32[:], tmb[:, 0, :], ALU.mult, [y32, tmb], [y32])
            yield
            c.TT("pool", y32[:], y32[:], tmb[:, 1, :], ALU.add, [y32, tmb], [y32])
            yield
            c.CP("act", vf[:], vT[:], [vT], [vf])
            yield
            c.TT("dve", vf[:].rearrange("p (h v) -> p h v", v=64), vf[:].rearrange("p (h v) -> p h v", v=64),
                 ct[bi][:].unsqueeze(2).to_broadcast([128, 8, 64]), ALU.mult, [vf, ct[bi]], [vf])
            c.TT("pool", y32[:], y32[:], vf[:], ALU.add, [y32, vf], [y32])
            yield
            c.TT("dve", yo[:], y32[:], gt[bi][:], ALU.mult, [y32, gt[bi]], [yo])
            yield
            for q in range(4):
                c.TR(TP[:, q * 128:(q + 1) * 128], yo[:, q * 128:(q + 1) * 128], ident_b, [yo, cbf], [TP])
            c.CP("act", yT[:].rearrange("p q t -> p (q t)"), TP[:, :], [TP], [yT])
            yield
            c.dma("sp", dr["ymix_fm"].rearrange("k p t -> p k t")[:, 0:4, tsl], yT[:], "st_yT", R=[yT])
            yield

        for ch in range(NCH + 1):
            gens = []
            if ch < NCH:
                gen_load(ch)
            if ch >= 1:
                gens.append(gen_seq(ch - 1))
            if ch < NCH:
                gens.append(gen_pre(ch, 0))
                gens.append(gen_pre(ch, 1))
            while gens:
                for g_ in list(gens):
                    try:
                        next(g_)
                    except StopIteration:
                        gens.remove(g_)
        c.emit()


def phase_C(nc, c, dr, T):
    NCH = T // 128
    NB = T // 512
    with contextlib.ExitStack() as st:
        P = Pool_(nc, st)
        vec = P.sb("vec", [128, NVEC], F32)
        c.dma("sp", vec[:], dr["vecs"], "vec", W=[vec])
        cmf = P.sb("cmf", [128, 4, 512], F32)
        cmb = P.sb("cmb", [128, 4, 512], BF16)
        c.dma("sp", cmf[:], dr["cmask"], "cmf", W=[cmf])
        c.TS("dve", cmb[:], cmf[:], 30000.0, ALU.mult, [cmf], [cmb], s2=-30000.0, op1=ALU.add)
        idf = P.sb("idf", [128, 128], F32)
        idb = P.sb("idb", [128, 128], BF16)
        c.dma("sp", idf[:], dr["consts"][:, 0, :], "idf", W=[idf])
        c.CP("dve", idb[:], idf[:], [idf], [idb])
        lam = P.sb("lam", [128, 4, 64], F32)
        c.dma("sp", lam[:], dr["lam"], "lam", W=[lam])
        lp = P.sb("lp", [128, 2, 64], F32)
        ls = P.sb("ls", [128, 2], F32)
        nlam = P.sb("nlam", [128, 1], F32)
        c.TT("dve", lp[:, 0, :], lam[:, 0, :], lam[:, 1, :], ALU.mult, [lam], [lp])
        c.TT("dve", lp[:, 1, :], lam[:, 2, :], lam[:, 3, :], ALU.mult, [lam], [lp])
        c.op("dve", lambda e: e.tensor_reduce(out=ls[:], in_=lp[:], axis=AX.X, op=ALU.add), [lp], [ls])
        c.ACT(ls[:], ls[:], AF.Exp, [ls], [ls])
        c.TT("dve", nlam[:], ls[:, 1:2], ls[:, 0:1], ALU.subtract, [ls], [nlam])
        c.TS("dve", nlam[:], nlam[:], -LAMBDA_INIT, ALU.add, [nlam], [nlam])
        ones_f = P.sb("ones_f", [128, 128], F32)
        c.op("pool", lambda e: e.memset(ones_f[:], 1.0), W=[ones_f])
        ones_bb = P.sb("ones_bb", [128, 128], BF16)
        c.op("pool", lambda e: e.memset(ones_bb[:], 1.0), W=[ones_bb])
        osqb = P.sb("osqb", [128, 512], BF16)
        Kt = P.sb("Kt", [128, 4, T], BF16)
        Va = P.sb("Va", [128, NCH, 512], BF16)
        for h in range(4):
            c.dma("sp", Kt[:, h, :], dr["kh_fm"][h], "Kt%d" % h, W=[Kt])
        vsrc = dr["va_tm"].rearrange("(n p) f -> p n f", p=128)
        for n0 in range(0, NCH, 8):
            n1 = min(NCH, n0 + 8)
            c.dma("sp", Va[:, n0:n1, :], vsrc[:, n0:n1, :], "Va%d" % (n0 % 16), W=[Va])
        Qt = [P.sb("Qt", [128, 4, 512], BF16) for _ in range(2)]
        NPT = 4
        PT = [P.sb("PT", [128, 2, 512], BF16) for _ in range(NPT)]
        PS = [P.ps("PS", [128, 1024]) for _ in range(2)]
        ACC = [P.ps("ACC", [128, 512]) for _ in range(2)]
        EPt = P.ps("EPt", [128, 1024])
        EPs = EPt
        Pacc = [P.sb("Pacc", [128, 2, 512], F32) for _ in range(2)]
        oraw = [P.sb("oraw", [128, 2, 512], F32) for _ in range(2)]
        rden = P.sb("rden", [128, 2, 512], F32)
        o1 = P.sb("o1", [128, 512], F32)
        o2 = P.sb("o2", [128, 512], F32)
        osq = P.sb("osq", [128, 512], F32)
        rs = P.sb("rs", [128, 512], F32)
        yb = [P.sb("yb", [128, 512], BF16) for _ in range(2)]
        epsc = P.sb("epsc", [128, 1], F32)
        c.op("pool", lambda e: e.memset(epsc[:], SUBLN_EPS), W=[epsc])
        nlam8 = P.sb("nlam8", [128, 1], F32)

        tiles = []
        for qb in range(NB):
            for h in range(4):
                nkt = 4 * (qb + 1)
                for kt in range(nkt):
                    tiles.append((qb, h, kt, nkt))
        NT = len(tiles)

        def stage1(t):
            qb, h, kt, nkt = tiles[t]
            if qb == 0 and h == 0 and kt == 0:
                qt = Qt[0]
                c.dma("sp", qt[:], dr["qh_fm"].rearrange("j p t -> p j t")[:, :, 0:512], qt.name, W=[qt])
            if h == 3 and kt == 0 and qb + 1 < NB:
                qn = Qt[(qb + 1) % 2]
                c.dma("sp", qn[:], dr["qh_fm"].rearrange("j p t -> p j t")[:, :, (qb + 1) * 512:(qb + 2) * 512], qn.name, W=[qn])
            qt = Qt[qb % 2]
            i = kt - 4 * qb
            q0 = 128 * i if i > 0 else 0
            ps = PS[t % 2]
            pt = PT[t % NPT]
            for cm in range(2):
                hp = slice(cm * 64, (cm + 1) * 64)
                c.MM(ps[:, cm * 512 + q0:(cm + 1) * 512], Kt[hp, h, kt * 128:(kt + 1) * 128], qt[hp, h, q0:512], True, i < 0, [Kt, qt], [ps])
            if i >= 0:
                for cm in range(2):
                    c.MM(ps[:, cm * 512 + q0:(cm + 1) * 512], idb[:], cmb[:, i, q0:512], False, True, [idb, cmb], [ps])
            psv = ps[:, :].rearrange("p (m q) -> p m q", m=2)[:, :, q0:512]
            c.ACT(pt[:, :, q0:512], psv, AF.Exp, [ps], [pt], scale=0.125)

        def stage2(t):
            qb, h, kt, nkt = tiles[t]
            g = qb * 4 + h
            i = kt - 4 * qb
            q0 = 128 * i if i > 0 else 0
            pt = PT[t % NPT]
            pa = Pacc[g % 2]
            for cm in range(2):
                c.MM(ACC[cm][:, q0:512], Va[:, kt, h * 128:(h + 1) * 128], pt[:, cm, q0:512], kt == 0, kt == nkt - 1, [Va, pt], [ACC[cm]])
            if kt == 0:
                c.CP("dve", pa[:], pt[:], [pt], [pa])
            else:
                c.TT("dve", pa[:, :, q0:512], pa[:, :, q0:512], pt[:, :, q0:512], ALU.add, [pa, pt], [pa])
            if kt == nkt - 1:
                orw = oraw[g % 2]
                c.CP("act", orw[:, 0, :], ACC[0][:, :], [ACC[0]], [orw])
                c.CP("dve", orw[:, 1, :], ACC[1][:, :], [ACC[1]], [orw])
                return (qb, h, g)
            return None

        def ep1a(qb, h, g):
            pa = Pacc[g % 2]
            for cm in range(2):
                c.MM(EPt[:, cm * 512:(cm + 1) * 512], ones_f[:], pa[:, cm, :], True, True, [ones_f, pa], [EPt])

        def ep1b(qb, h, g):
            orw = oraw[g % 2]
            for cm in range(2):
                c.RPOW(rden[:, cm, :], EPt[:, cm * 512:(cm + 1) * 512], [EPt], [rden], -1.0)
            c.TT("pool", o1[:], orw[:, 0, :], rden[:, 0, :], ALU.mult, [orw, rden], [o1])
            c.TT("pool", o2[:], orw[:, 1, :], rden[:, 1, :], ALU.mult, [orw, rden], [o2])
            c.TS("pool", o2[:], o2[:], nlam[:, 0:1], ALU.mult, [o2, nlam], [o2], s2=0.0, op1=ALU.add)
            c.TT("pool", o1[:], o1[:], o2[:], ALU.add, [o1, o2], [o1])

        def ep2a(qb, h, g):
            c.ACT(osqb[:], o1[:], AF.Square, [o1], [osqb])

        def ep2b(qb, h, g):
            c.MM(EPt[:, 0:512], ones_bb[:], osqb[:], True, True, [ones_bb, osqb], [EPt])

        def ep3(qb, h, g):
            c.RPOW(rs[:], EPt[:, 0:512], [EPt], [rs], -0.5, scale=1.0 / 128, bias=epsc[:, 0:1], RB=[epsc])
            c.TT("pool", o1[:], o1[:], rs[:], ALU.mult, [o1, rs], [o1])
            y_ = yb[g % 2]
            c.TS("pool", y_[:], o1[:], vec[:, VEC_OFF["subln"]:VEC_OFF["subln"] + 1], ALU.mult, [o1, vec], [y_], s2=1.0 - LAMBDA_INIT, op1=ALU.mult)
            c.dma("sp", dr["ymix_fm"][4 + h, :, qb * 512:(qb + 1) * 512], y_[:], "st_" + y_.name, R=[y_])

        LA = 1
        pend = []

        def flush(upto_g=None, force=False):
            while pend:
                cd, fn_, r_ = pend[0]
                if force or cd <= 0 or (upto_g is not None and r_[2] <= upto_g):
                    pend.pop(0)
                    fn_(*r_)
                else:
                    break

        for t in range(NT + LA):
            if t < NT:
                stage1(t)
            if t - LA >= 0:
                qb_, h_, kt_, nkt_ = tiles[t - LA]
                if kt_ == 0:
                    flush(upto_g=qb_ * 4 + h_ - 2)
                r_ = stage2(t - LA)
                if r_ is not None:
                    pend.append([8, ep1a, r_])
                    pend.append([16, ep1b, r_])
                    pend.append([28, ep2a, r_])
                    pend.append([33, ep2b, r_])
                    pend.append([40, ep3, r_])
                for pe_ in pend:
                    pe_[0] -= 1
                flush()
        flush(force=True)
        c.emit()


def rms_rstd(c, src, sqb, ones_b, ps, rstd, epsc, n=512):
    c.ACT(sqb[:, :, 0:n], src[:, :, 0:n], AF.Square, [src], [sqb])
    for k in range(8):
        c.MM(ps[:, 0:n], ones_b[:], sqb[:, k, 0:n], k == 0, k == 7, [ones_b, sqb], [ps])
    c.RPOW(rstd[:, 0:n], ps[:, 0:n], [ps], [rstd], -0.5, scale=1.0 / D, bias=epsc[:, 0:1], RB=[epsc])


def phase_D1(nc, c, dr, T):
    NB = T // 512
    with contextlib.ExitStack() as st:
        P = Pool_(nc, st)
        vec = P.sb("vec", [128, NVEC], F32)
        c.dma("sp", vec[:], dr["vecs"], "vec", W=[vec])
        ones_b = P.sb("ones_b", [128, 128], BF16)
        c.op("pool", lambda e: e.memset(ones_b[:], 1.0), W=[ones_b])
        epsc = P.sb("epsc", [128, 1], F32)
        c.op("pool", lambda e: e.memset(epsc[:], NORM_EPS), W=[epsc])
        xs2 = [P.sb("xs", [128, 4096], F32) for _ in range(3)]
        stg = xs2[0:2]
        Wo = P.sb("Wo", [128, 8, D], BF16)
        Wq = P.sb("Wq", [128, 8, D], BF16)
        Wo2 = P.sb("Wo2", [128, 8, D], BF16)
        Wkv = P.sb("Wkv", [128, 8, 2 * D], BF16)
        for k0 in range(0, 8, 4):
            pass
        c.dma("pool", Wo[:], dr["w_out"].rearrange("k p n -> p k n"), "Wo_ld", W=[Wo])
        load_weight_bf16(c, P, dr, "wq_c", 8, D, Wq, stg, VEC_OFF["norm_cross"], vec)
        c.dma("pool", Wo2[:], dr["wo_c"].rearrange("k p n -> p k n"), "Wo2_ld", W=[Wo2])
        load_weight_bf16(c, P, dr, "wkv_c", 8, 2 * D, Wkv, stg, VEC_OFF["norm_mem"], vec)
        pp = [P.ps("pp", [128, 512]) for _ in range(6)]
        ppi = [0]

        def nextp():
            ppi[0] += 1
            return pp[ppi[0] % 6]
        sqb = P.sb("sqb", [128, 8, 512], BF16)
        xb = P.sb("xb", [128, 8, 512], BF16)
        yms = [P.sb("ym", [128, 8, 512], BF16) for _ in range(3)]
        qcb = P.sb("qcb", [128, 8, 512], BF16)
        ocb = P.sb("ocb", [128, 8, 512], BF16)
        rstd = P.sb("rstd", [128, 512], F32)
        rden = P.sb("rden", [128, 512], F32)
        PTc = [P.sb("PTc", [128, 512], BF16) for _ in range(2)]
        Kc = P.sb("Kc", [128, 8, MEM], BF16)
        Vc = P.sb("Vc", [128, 2, D], BF16)
        rst_m = P.sb("rst_m", [128, 2], F32)
        xsT = xs2[0]
        xs = xsT[:].rearrange("p (k t) -> p k t", t=512)
        c.dma("sp", xs[:, :, 0:MEM], dr["memT"].rearrange("k p t -> p k t"), xsT.name, W=[xsT])
        p = nextp()
        c.ACT(sqb[:, :, 0:MEM], xs[:, :, 0:MEM], AF.Square, [xsT], [sqb])
        for k in range(8):
            c.MM(p[:, 0:MEM], ones_b[:], sqb[:, k, 0:MEM], k == 0, k == 7, [ones_b, sqb], [p])
        c.RPOW(rstd[:, 0:MEM], p[:, 0:MEM], [p], [rstd], -0.5, scale=1.0 / D, bias=epsc[:, 0:1], RB=[epsc])
        c.CP("dve", xb[:, :, 0:MEM], xs[:, :, 0:MEM], [xsT], [xb])
        p = nextp()
        for mt in range(2):
            for k in range(8):
                c.MM(p[:, mt:mt + 1], sqb[:, k, mt * 128:(mt + 1) * 128], ones_b[:, 0:1], k == 0, k == 7, [sqb, ones_b], [p])
        c.ACT(rst_m[:], p[:, 0:2], AF.Sqrt, [p], [rst_m], scale=1.0 / D, bias=NORM_EPS)
        c.op("dve", lambda e: e.reciprocal(out=rst_m[:], in_=rst_m[:]), [rst_m], [rst_m])
        for n in range(8):
            p = nextp()
            for k in range(8):
                c.MM(p[:, 0:MEM], Wkv[:, k, n * 128:(n + 1) * 128], xb[:, k, 0:MEM], k == 0, k == 7, [Wkv, xb], [p])
            c.TT("dve", Kc[:, n, :], p[:, 0:MEM], rstd[:, 0:MEM], ALU.mult, [p, rstd], [Kc])
        for mt in range(2):
            for nn in range(2):
                p = nextp()
                for k in range(8):
                    c.MM(p[:, :], xb[:, k, mt * 128:(mt + 1) * 128], Wkv[:, k, D + nn * 512: D + (nn + 1) * 512], k == 0, k == 7, [Wkv, xb], [p])
                c.TS("dve", Vc[:, mt, nn * 512:(nn + 1) * 512], p[:, :], rst_m[:, mt:mt + 1], ALU.mult, [p, rst_m], [Vc])
        def loads(b):
            tsl = slice(b * 512, (b + 1) * 512)
            xT_ = xs2[b % 3]
            c.dma("sp", xT_[:].rearrange("p (k t) -> p k t", t=512), dr["xT"].rearrange("k p t -> p k t")[:, :, tsl], xT_.name, W=[xT_])
            c.dma("sp", yms[b % 3][:], dr["ymix_fm"].rearrange("k p t -> p k t")[:, :, tsl], yms[b % 3].name, W=[yms[b % 3]])

        def wout(b, n0, n1):
            xT_ = xs2[b % 3]
            x3 = xT_[:].rearrange("p (k t) -> p k t", t=512)
            ym_ = yms[b % 3]
            for n in range(n0, n1):
                p = nextp()
                for k in range(8):
                    c.MM(p[:, :], Wo[:, k, n * 128:(n + 1) * 128], ym_[:, k, :], k == 0, k == 7, [Wo, ym_], [p])
                c.TT("dve", x3[:, n, :], p[:, :], x3[:, n, :], ALU.add, [p, xT_], [xT_])

        loads(0)
        if NB > 1:
            loads(1)
        wout(0, 0, 8)
        for b in range(NB):
            tsl = slice(b * 512, (b + 1) * 512)
            xsT = xs2[b % 3]
            xs = xsT[:].rearrange("p (k t) -> p k t", t=512)
            if b + 2 < NB:
                loads(b + 2)
            p = nextp()
            c.ACT(sqb[:], xs, AF.Square, [xsT], [sqb])
            if b + 1 < NB:
                wout(b + 1, 0, 4)
            for k in range(8):
                c.MM(p[:, :], ones_b[:], sqb[:, k, :], k == 0, k == 7, [ones_b, sqb], [p])
            c.RPOW(rstd[:], p[:, :], [p], [rstd], -0.5, scale=1.0 / D, bias=epsc[:, 0:1], RB=[epsc])
            if b + 1 < NB:
                wout(b + 1, 4, 8)
            c.TT("dve", xb[:], xs, rstd[:].unsqueeze(1).to_broadcast([128, 8, 512]), ALU.mult, [xsT, rstd], [xb])
            for n in range(8):
                p = nextp()
                for k in range(8):
                    c.MM(p[:, :], Wq[:, k, n * 128:(n + 1) * 128], xb[:, k, :], k == 0, k == 7, [Wq, xb], [p])
                c.CP("act", qcb[:, n, :], p[:, :], [p], [qcb])
            for hq in range(4):
                for mt in range(2):
                    p = nextp()
                    for dd in range(2):
                        c.MM(p[:, :], Kc[:, 2 * hq + dd, mt * 128:(mt + 1) * 128], qcb[:, 2 * hq + dd, :], dd == 0, dd == 1, [Kc, qcb], [p])
                    c.ACT(PTc[mt][:], p[:, :], AF.Exp, [p], [PTc[mt]], scale=1.0 / 16)
                p = nextp()
                for mt in range(2):
                    c.MM(p[:, :], ones_b[:], PTc[mt][:], mt == 0, mt == 1, [ones_b, PTc[mt]], [p])
                c.RPOW(rden[:], p[:, :], [p], [rden], -1.0)
                for dd in range(2):
                    p = nextp()
                    for mt in range(2):
                        c.MM(p[:, :], Vc[:, mt, hq * 256 + dd * 128: hq * 256 + (dd + 1) * 128], PTc[mt][:], mt == 0, mt == 1, [Vc, PTc[mt]], [p])
                    c.TT("dve", ocb[:, 2 * hq + dd, :], p[:, :], rden[:], ALU.mult, [p, rden], [ocb])
            for n in range(8):
                p = nextp()
                for k in range(8):
                    c.MM(p[:, :], Wo2[:, k, n * 128:(n + 1) * 128], ocb[:, k, :], k == 0, k == 7, [Wo2, ocb], [p])
                c.TT("dve", xs[:, n, :], p[:, :], xs[:, n, :], ALU.add, [p, xsT], [xsT])
            c.dma("sp", dr["x2_fm"].rearrange("k p t -> p k t")[:, :, tsl], xs, "st_" + xsT.name, R=[xsT])
        c.emit()


def phase_D2(nc, c, dr, T):
    NB = T // 512
    with contextlib.ExitStack() as st:
        P = Pool_(nc, st)
        vec = P.sb("vec", [128, NVEC], F32)
        c.dma("sp", vec[:], dr["vecs"], "vec", W=[vec])
        V = lambda n, j=0: vec[:, VEC_OFF[n] + j: VEC_OFF[n] + j + 1]
        ones_b = P.sb("ones_b", [128, 128], BF16)
        c.op("pool", lambda e: e.memset(ones_b[:], 1.0), W=[ones_b])
        epsc = P.sb("epsc", [128, 1], F32)
        c.op("pool", lambda e: e.memset(epsc[:], NORM_EPS), W=[epsc])
        xs2 = [P.sb("xs", [128, 4096], F32) for _ in range(2)]
        stg = xs2
        Wup = P.sb("Wup", [128, 8, 2 * DFF], BF16)
        for k in range(8):
            for half in range(2):
                s_ = stg[(2 * k + half) % 2]
                c.dma("sp", s_[:, 0:DFF], dr["w_up"][k][:, half * DFF:(half + 1) * DFF], s_.name, W=[s_])
                c.TS("dve", Wup[:, k, half * DFF:(half + 1) * DFF], s_[:, 0:DFF], vec[:, VEC_OFF["norm_ffn"] + k: VEC_OFF["norm_ffn"] + k + 1], ALU.mult, [s_, vec], [Wup])
        pp = [P.ps("pp", [128, 512]) for _ in range(6)]
        ppi = [0]

        def nextp():
            ppi[0] += 1
            return pp[ppi[0] % 6]
        sqb = P.sb("sqb", [128, 8, 512], BF16)
        xbs = [P.sb("xb", [128, 8, 512], BF16) for _ in range(2)]
        rstds = [P.sb("rstd", [128, 512], F32) for _ in range(2)]
        G = [P.sb("G", [128, 514], F32) for _ in range(3)]
        H = P.sb("H", [128, 22, 2], F32)
        c.op("pool", lambda e: e.memset(H[:], 0.0), W=[H])
        t1 = [P.sb("t1", [128, 512], F32) for _ in range(3)]
        sl = [P.sb("sl", [128, 512], F32) for _ in range(3)]
        ao = [P.sb("ao", [128, 512], BF16) for _ in range(3)]
        prev = None

        def fin(a_, pv, s_, j, tsl):
            c.TT("dve", a_[:], pv[:, :], s_[:], ALU.mult, [pv, s_], [a_])
            c.dma("sp", dr["a_fm"][j, :, tsl], a_[:], "st_" + a_.name, R=[a_])

        def loadx(b):
            tsl = slice(b * 512, (b + 1) * 512)
            x_ = xs2[b % 2]
            c.dma("sp", x_[:].rearrange("p (k t) -> p k t", t=512), dr["x2_fm"].rearrange("k p t -> p k t")[:, :, tsl], x_.name, W=[x_])

        def prologue(b):
            x_ = xs2[b % 2]
            x3 = x_[:].rearrange("p (k t) -> p k t", t=512)
            p = nextp()
            for k in range(8):
                c.ACT(sqb[:, k, :], x3[:, k, :], AF.Square, [x_], [sqb])
                yield
            for k in range(8):
                c.MM(p[:, :], ones_b[:], sqb[:, k, :], k == 0, k == 7, [ones_b, sqb], [p])
            c.RPOW(rstds[b % 2][:], p[:, :], [p], [rstds[b % 2]], -0.5, scale=1.0 / D, bias=epsc[:, 0:1], RB=[epsc])
            yield
            for k in range(8):
                c.TT("dve", xbs[b % 2][:, k, :], x3[:, k, :], rstds[b % 2][:], ALU.mult, [x_, rstds[b % 2]], [xbs[b % 2]])
                yield

        loadx(0)
        for _ in prologue(0):
            pass
        pro = None
        for b in range(NB):
            tsl = slice(b * 512, (b + 1) * 512)
            xb = xbs[b % 2]
            if b + 1 < NB:
                loadx(b + 1)
            for j in range(22):
                if j == 2 and b + 1 < NB:
                    pro = prologue(b + 1)
                if pro is not None:
                    try:
                        next(pro)
                    except StopIteration:
                        pro = None
                r3 = (b * 22 + j) % 3
                g_, t_, s_, a_ = G[r3], t1[r3], sl[r3], ao[r3]
                pg = nextp()
                for k in range(8):
                    c.MM(pg[:, :], Wup[:, k, j * 128:(j + 1) * 128], xb[:, k, :], k == 0, k == 7, [Wup, xb], [pg])
                pv = nextp()
                for k in range(8):
                    c.MM(pv[:, :], Wup[:, k, DFF + j * 128: DFF + (j + 1) * 128], xb[:, k, :], k == 0, k == 7, [Wup, xb], [pv])
                c.CP("act", g_[:, 0:2], H[:, j, :], [H], [g_])
                c.CP("act", g_[:, 2:514], pg[:, :], [pg], [g_])
                c.CP("act", H[:, j, :], g_[:, 512:514], [g_], [H])
                c.TS("dve", t_[:], g_[:, 0:512], V("conv_w0", j), ALU.mult, [g_, vec], [t_], s2=V("conv_b", j), op1=ALU.add)
                c.STT(t_[:], g_[:, 1:513], V("conv_w1", j), t_[:], ALU.mult, ALU.add, [g_, t_, vec], [t_])
                c.STT(t_[:], g_[:, 2:514], V("conv_w2", j), t_[:], ALU.mult, ALU.add, [g_, t_, vec], [t_])
                c.ACT(s_[:], t_[:], AF.Silu, [t_], [s_])
                if prev is not None:
                    fin(*prev)
                prev = (a_, pv, s_, j, tsl)
        fin(*prev)
        c.emit()


def phase_D3(nc, c, dr, T, outT):
    NB = T // 512
    with contextlib.ExitStack() as st:
        P = Pool_(nc, st)
        vec = P.sb("vec", [128, NVEC], F32)
        c.dma("sp", vec[:], dr["vecs"], "vec", W=[vec])
        V = lambda n, j=0: vec[:, VEC_OFF[n] + j: VEC_OFF[n] + j + 1]
        ones_b = P.sb("ones_b", [128, 128], BF16)
        c.op("pool", lambda e: e.memset(ones_b[:], 1.0), W=[ones_b])
        epsc = P.sb("epsc", [128, 1], F32)
        c.op("pool", lambda e: e.memset(epsc[:], NORM_EPS), W=[epsc])
        Wdt = [P.sb("Wd", [128, 2, D], BF16) for _ in range(11)]
        for jj in range(11):
            c.dma("pool", Wdt[jj][:], dr["w_down"].rearrange("k p n -> p k n")[:, 2 * jj:2 * jj + 2, :], Wdt[jj].name, W=[Wdt[jj]])
        pp = [P.ps("pp", [128, 512]) for _ in range(6)]
        ppi = [0]

        def nextp():
            ppi[0] += 1
            return pp[ppi[0] % 6]
        xs = [P.sb("xs", [128, 8, 512], F32) for _ in range(2)]
        ab = [P.sb("ab", [128, 22, 512], BF16) for _ in range(2)]
        sqb = P.sb("sqb", [128, 8, 512], BF16)
        rstd = P.sb("rstd", [128, 512], F32)
        def loads(b):
            tsl = slice(b * 512, (b + 1) * 512)
            x_, a_ = xs[b % 2], ab[b % 2]
            c.dma("sp", x_[:], dr["x2_fm"].rearrange("k p t -> p k t")[:, :, tsl], x_.name, W=[x_])
            for j0 in range(0, 22, 11):
                c.dma("sp", a_[:, j0:j0 + 11, :], dr["a_fm"].rearrange("k p t -> p k t")[:, j0:j0 + 11, tsl], a_.name + "_%d" % j0, W=[a_])

        loads(0)
        for b in range(NB):
            tsl = slice(b * 512, (b + 1) * 512)
            x_, a_ = xs[b % 2], ab[b % 2]
            if b + 1 < NB:
                loads(b + 1)
            for n in range(8):
                p = nextp()
                for j in range(22):
                    c.MM(p[:, :], Wdt[j // 2][:, j % 2, n * 128:(n + 1) * 128], a_[:, j, :], j == 0, j == 21, [Wdt[j // 2], a_], [p])
                c.TT("dve", x_[:, n, :], p[:, :], x_[:, n, :], ALU.add, [p, x_], [x_])
            p = nextp()
            rms_rstd(c, x_, sqb, ones_b, p, rstd, epsc)
            for n in range(8):
                c.STT(x_[:, n, :], x_[:, n, :], V("norm_final", n), rstd[:], ALU.mult, ALU.mult, [x_, rstd, vec], [x_])
            c.dma("sp", outT.rearrange("k p t -> p k t")[:, :, tsl], x_[:], "st_" + x_.name, R=[x_])
        c.emit()
```

```python
import contextlib
import math
import numpy as np
import concourse.bass as bass
import concourse.mybir as mybir
from concourse.bass_utils import run_bass_kernel_spmd

F32 = mybir.dt.float32
BF16 = mybir.dt.bfloat16
I32 = mybir.dt.int32
AF = mybir.ActivationFunctionType
ALU = mybir.AluOpType
AX = mybir.AxisListType

D = 1024
NIN = 3232
DFF = 2816
MEM = 256
LNX_EPS = 64e-5
SUBLN_EPS = 1e-5
NORM_EPS = 1e-6
LAMBDA_INIT = 0.2
SEQ_ONLY = False


class Buf:
    __slots__ = ("w", "r", "name")

    def __init__(self, name=""):
        self.w = None
        self.r = []
        self.name = name


class Tile:
    def __init__(self, t, name):
        self.t = t
        self.buf = Buf(name)
        self.name = name

    def __getitem__(self, idx):
        return self.t[idx]


def _b(x):
    return x.buf if isinstance(x, Tile) else x


class Ctx:
    ENG = ("pe", "act", "dve", "pool", "sp")

    def __init__(self, nc, stack):
        self.nc = nc
        self.stack = stack
        self.sem = {}
        for e in self.ENG:
            self.sem[e] = stack.enter_context(nc.semaphore("s_" + e))
        self.cnt = {e: 0 for e in self.ENG}
        self.known = {e: {} for e in self.ENG}
        self.ops = {e: [] for e in self.ENG}
        self.dsem = {}
        self.nops = 0

    def _collect(self, e, reads, writes):
        waits = {}

        def add(tok):
            if tok is None:
                return
            k, v = tok
            if waits.get(k, 0) < v:
                waits[k] = v
        for b in reads:
            add(b.w)
        for b in writes:
            add(b.w)
            for t in b.r:
                add(t)
        kn = self.known[e]
        out = []
        for k, v in waits.items():
            if e == "pe" and k == ("E", "pe"):
                continue
            if kn.get(k, 0) >= v:
                continue
            kn[k] = v
            out.append((k, v))
        return out

    def op(self, e, fn, R=(), W=()):
        R = [_b(x) for x in R]
        W = [_b(x) for x in W]
        waits = self._collect(e, R, W)
        self.cnt[e] += 1
        tok = (("E", e), self.cnt[e])
        for b in R:
            b.r.append(tok)
        for b in W:
            b.w = tok
            b.r = []
        self.ops[e].append((waits, fn, None))
        self.nops += 1

    def dma(self, q, out, in_, slot, R=(), W=()):
        R = [_b(x) for x in R]
        W = [_b(x) for x in W]
        if slot not in self.dsem:
            s = self.stack.enter_context(self.nc.semaphore("d_" + slot))
            self.dsem[slot] = [s, 0]
        waits = self._collect(q, R, W)
        self.dsem[slot][1] += 16
        tok = (("D", slot), self.dsem[slot][1])
        for b in R:
            b.r.append(tok)
        for b in W:
            b.w = tok
            b.r = []

        def fn(eng, out=out, in_=in_):
            return eng.dma_start(out=out, in_=in_)
        self.ops[q].append((waits, fn, slot))
        self.nops += 1

    def _semof(self, k):
        return self.sem[k[1]] if k[0] == "E" else self.dsem[k[1]][0]

    def emit(self):
        nc = self.nc
        waits = []
        for name, (s, cn) in self.dsem.items():
            k = ("D", name)
            if cn > 0 and self.known["sp"].get(k, 0) < cn:
                self.known["sp"][k] = cn
                waits.append((k, cn))
        if waits:
            self.ops["sp"].append((waits, None, None))
        ops = self.ops
        self.ops = {e: [] for e in self.ENG}
        with nc.Block() as block:
            def mk(e):
                def body(eng):
                    for waits, fn, slot in ops[e]:
                        for k, v in waits:
                            eng.wait_ge(self._semof(k), v)
                        if fn is None:
                            continue
                        inst = fn(eng)
                        if slot is None:
                            inst.then_inc(self.sem[e], 1)
                        else:
                            inst.then_inc(self.dsem[slot][0], 16)
                return body
            block.tensor(mk("pe"))
            block.scalar(mk("act"))
            block.vector(mk("dve"))
            block.gpsimd(mk("pool"))
            block.sync(mk("sp"))
        for e in self.ENG:
            for e2 in self.ENG:
                self.known[e][("E", e2)] = self.cnt[e2]
            for name, (s, cn) in self.dsem.items():
                self.known[e][("D", name)] = cn

    def ACT(self, out, in_, func, R, W, scale=1.0, bias=None):
        if bias is None:
            self.op("act", lambda e: e.activation(out=out, in_=in_, func=func, scale=scale), R, W)
        else:
            self.op("act", lambda e: e.activation(out=out, in_=in_, func=func, scale=scale, bias=bias), R, W)

    def RPOW(self, out, in_, R, W, power, scale=1.0, bias=None, tmp=None, RB=()):
        t = out if tmp is None else tmp
        self.ACT(t, in_, AF.Ln, list(R) + list(RB), W, scale=scale, bias=bias)
        self.ACT(out, t, AF.Exp, W, W, scale=power)

    def TT(self, eng, out, in0, in1, op, R, W):
        self.op(eng, lambda e: e.tensor_tensor(out=out, in0=in0, in1=in1, op=op), R, W)

    def TS(self, eng, out, in0, s1, op0, R, W, s2=None, op1=None):
        if op1 is None:
            self.op(eng, lambda e: e.tensor_scalar(out=out, in0=in0, scalar1=s1, scalar2=None, op0=op0), R, W)
        else:
            self.op(eng, lambda e: e.tensor_scalar(out=out, in0=in0, scalar1=s1, scalar2=s2, op0=op0, op1=op1), R, W)

    def STT(self, out, in0, scalar, in1, op0, op1, R, W):
        self.op("dve", lambda e: e.scalar_tensor_tensor(out=out, in0=in0, scalar=scalar, in1=in1, op0=op0, op1=op1), R, W)

    def CP(self, eng, out, in_, R, W):
        if eng == "act":
            self.op("act", lambda e: e.activation(out=out, in_=in_, func=AF.Copy), R, W)
        else:
            self.op(eng, lambda e: e.tensor_copy(out=out, in_=in_), R, W)

    def MM(self, out, lhsT, rhs, start, stop, R, W):
        self.op("pe", lambda e: e.matmul(out, lhsT=lhsT, rhs=rhs, start=start, stop=stop), R, W)

    def TR(self, out, in_, ident, R, W):
        self.op("pe", lambda e: e.transpose(out=out, in_=in_, identity=ident), R, W)


class Pool_:
    CNT = [0]

    def __init__(self, nc, st):
        self.nc = nc
        self.st = st

    def sb(self, name, shape, dt):
        Pool_.CNT[0] += 1
        nm = "%s_%d" % (name, Pool_.CNT[0])
        return Tile(self.st.enter_context(self.nc.sbuf_tensor(nm, shape, dt)), nm)

    def ps(self, name, shape, dt=F32):
        Pool_.CNT[0] += 1
        nm = "%s_%d" % (name, Pool_.CNT[0])
        return Tile(self.st.enter_context(self.nc.psum_tensor(nm, shape, dt)), nm)


VEC_SPEC = [
    ("mix_r", 4), ("mix_k", 4), ("mix_v", 4), ("mix_wa", 1), ("mix_g", 1),
    ("w0", 4), ("a0", 4), ("k_k", 4), ("k_a", 4), ("r_k", 4),
    ("norm_mix", 8), ("norm_cross", 8), ("norm_mem", 8), ("norm_ffn", 8), ("norm_final", 8),
    ("conv_w0", 22), ("conv_w1", 22), ("conv_w2", 22), ("conv_b", 22),
    ("inv_freq", 1), ("sin_scale", 1), ("subln", 1),
]
VEC_OFF = {}
_o = 0
for _n, _k in VEC_SPEC:
    VEC_OFF[_n] = _o
    _o += _k
NVEC = _o


def _cols(v, n):
    v = np.asarray(v, np.float32).reshape(-1)
    out = np.zeros((n * 128,), np.float32)
    out[: v.shape[0]] = v
    return np.ascontiguousarray(out.reshape(n, 128).T)


def cb(h):
    return (h % 2) * 4 + h // 2


def build(T, dbg=False):
    NB = T // 512
    NCH = T // 128
    nc = bass.Bass("TRN2", target_bir_lowering=False)
    dr = {}

    def din(name, shape, dt=F32):
        dr[name] = nc.dram_tensor(name, shape, dt, kind="ExternalInput").ap()
        return dr[name]

    def dscr(name, shape, dt):
        dr[name] = nc.dram_tensor(name, shape, dt, kind="ExternalOutput" if dbg else "Internal").ap()
        return dr[name]

    din("xT", [8, 128, T])
    din("memT", [8, 128, MEM])
    din("pos", [128, T], I32)
    din("vecs", [128, NVEC])
    din("w_in", [8, 128, NIN])
    din("lup", [64, 512])
    din("gup", [96, 512])
    din("tmb", [128, 3, 512])
    din("lam", [128, 4, 64])
    din("w_out", [8, 128, D])
    din("wq_c", [8, 128, D])
    din("wkv_c", [8, 128, 2 * D])
    din("wo_c", [8, 128, D])
    din("w_up", [8, 128, 2 * DFF])
    din("w_down", [22, 128, D])
    din("consts", [128, 6, 128])
    din("cmask", [128, 4, 512])
    outT = nc.dram_tensor("outT", [8, 128, T], F32, kind="ExternalOutput").ap()

    for nm in ("kt_fm", "bt_fm", "kk_fm", "rt_fm", "qh_fm", "kh_fm"):
        dscr(nm, [4, 128, T], BF16)
    for nm in ("bt_tm", "kk_tm", "v_tm", "va_tm"):
        dscr(nm, [T, 512], BF16)
    dscr("g_tm", [T, 512], F32)
    dscr("c_tm", [T, 8], F32)
    dscr("dm_fm", [128, 4, NCH], F32)
    dscr("ee_fm", [128, 4, NCH], F32)
    dscr("ymix_fm", [8, 128, T], BF16)
    dscr("x2_fm", [8, 128, T], F32)
    dscr("a_fm", [22, 128, T], BF16)

    with contextlib.ExitStack() as st0:
        c = Ctx(nc, st0)
        phase_A(nc, c, dr, T)
        phase_B(nc, c, dr, T)
        phase_C(nc, c, dr, T)
        phase_D1(nc, c, dr, T)
        phase_D2(nc, c, dr, T)
        phase_D3(nc, c, dr, T, outT)
    return nc


def load_consts(c, P, dr, q="sp"):
    cf = P.sb("cf", [128, 6, 128], F32)
    cbf = P.sb("cbf", [128, 6, 128], BF16)
    c.dma(q, cf[:], dr["consts"], "cf", W=[cf])
    c.CP("dve", cbf[:], cf[:], [cf], [cbf])
    return cf, cbf


def load_weight_bf16(c, P, dr, name, kc, n, wb, stg, gcol=None, vec=None, eng="dve"):
    for k in range(kc):
        s = stg[k % len(stg)]
        c.dma("sp", s[:, 0:n], dr[name][k], s.name, W=[s])
        if gcol is None:
            c.CP("dve" if k % 2 == 0 else "act", wb[:, k, :], s[:, 0:n], [s], [wb])
        else:
            c.TS("dve", wb[:, k, :], s[:, 0:n], vec[:, gcol + k:gcol + k + 1], ALU.mult, [s, vec], [wb])


def phase_A(nc, c, dr, T):
    NB = T // 512
    with contextlib.ExitStack() as st:
        P = Pool_(nc, st)
        cf, cbf = load_consts(c, P, dr)
        ident_b = cbf[:, 0, :]
        bones_f = cf[:, 4, :]
        bones_b = cbf[:, 4, :]
        swap_f = cf[:, 5, :]
        vec = P.sb("vec", [128, NVEC], F32)
        c.dma("sp", vec[:], dr["vecs"], "vec", W=[vec])
        V = lambda n, j=0: vec[:, VEC_OFF[n] + j: VEC_OFF[n] + j + 1]
        ones_f = P.sb("ones_f", [128, 128], BF16)
        c.op("pool", lambda e: e.memset(ones_f[:], 1.0), W=[ones_f])
        epsc = P.sb("epsc", [128, 2], F32)
        c.op("pool", lambda e: e.memset(epsc[:, 0:1], NORM_EPS), W=[epsc])
        c.op("pool", lambda e: e.memset(epsc[:, 1:2], 1e-18), W=[epsc])
        ones512 = P.sb("ones512", [128, 512], F32)
        c.op("pool", lambda e: e.memset(ones512[:], 1.0), W=[ones512])
        Wb = P.sb("Wb", [128, 8, NIN], BF16)
        xs = P.sb("xs", [128, 4096], F32)
        xs3 = xs[:].rearrange("p (k t) -> p k t", t=512)
        stg = [xs]
        load_weight_bf16(c, P, dr, "w_in", 8, NIN, Wb, stg, VEC_OFF["norm_mix"], vec)
        lupf = P.sb("lupf", [64, 512], F32)
        lupb = P.sb("lupb", [64, 512], BF16)
        gupf = P.sb("gupf", [96, 512], F32)
        gupb = P.sb("gupb", [96, 512], BF16)
        c.dma("sp", lupf[:], dr["lup"], "lupf", W=[lupf])
        c.dma("sp", gupf[:], dr["gup"], "gupf", W=[gupf])
        c.CP("dve", lupb[:], lupf[:], [lupf], [lupb])
        c.CP("dve", gupb[:], gupf[:], [gupf], [gupb])

        sq = P.sb("sq", [128, 8, 512], BF16)
        xb = P.sb("xb", [128, 8, 512], BF16)
        rstd = P.sb("rstd", [128, 512], F32)
        pp = [P.ps("pp", [128, 512]) for _ in range(6)]
        ptr = [P.ps("ptr", [128, 512], BF16) for _ in range(2)]
        ppi = [0]

        def nextp():
            ppi[0] += 1
            return pp[ppi[0] % 6]
        tri = [0]

        def nexttr():
            tri[0] += 1
            return ptr[tri[0] % 2]

        Hz = P.sb("Hz", [128, 14], F32)
        c.op("pool", lambda e: e.memset(Hz[:], 0.0), W=[Hz])
        zs_wa = P.sb("zs_wa", [128, 512], F32)
        zs_g = P.sb("zs_g", [128, 512], F32)
        th_b = P.sb("th_b", [64, 512], BF16)
        sg_b = P.sb("sg_b", [96, 512], BF16)
        NSET = 2
        sets = []
        for si in range(NSET):
            sets.append({
                "tmp": [P.sb("tmpA", [128, 512], F32) for _ in range(11)],
                "z": [P.sb("zt", [128, 513], F32) for _ in range(3)],
                "bft": [P.sb("bfA", [128, 512], BF16) for _ in range(6)],
                "trs": [P.sb("trs", [128, 512], BF16) for _ in range(2)],
            })
        tmp = sets[0]["tmp"]
        bft2 = [P.sb("bfQ", [128, 512], BF16) for _ in range(2)]
        trsA = bft2
        cts = P.sb("cts", [128, 4, 8], F32)
        dmt = P.sb("dmt", [128, 4, 4], F32)
        eet = P.sb("eet", [128, 4, 4], F32)
        posi = P.sb("posi", [128, 512], I32)
        ropei = P.sb("ropei", [128, 512], I32)
        ropeT = [P.sb("ropeT", [128, 512], F32) for _ in range(8)]
        cosT, sinT = ropeT[0], ropeT[1]
        gts = zs_wa

        def proj_fm(cols, ncol):
            p = nextp()
            for k in range(8):
                c.MM(p[0:ncol, :], Wb[:, k, cols:cols + ncol], xb[:, k, :], k == 0, k == 7, [Wb, xb], [p])
            return p

        def zproj(cols, ncol, dst, hidx):
            p = proj_fm(cols, ncol)
            c.CP("act", dst[0:ncol, 0:1], Hz[0:ncol, hidx:hidx + 1], [Hz], [dst])
            c.CP("act", dst[0:ncol, 1:513], p[0:ncol, :], [p], [dst])
            c.CP("act", Hz[0:ncol, hidx:hidx + 1], dst[0:ncol, 512:513], [dst], [Hz])

        def shift(dst_t, dst, src, ncol, mixcol, d):
            c.TT("pool", d[0:ncol, :], src[0:ncol, 0:512], src[0:ncol, 1:513], ALU.subtract, [src], [d])
            c.STT(dst, d[0:ncol, :], mixcol, src[0:ncol, 1:513], ALU.mult, ALU.add, [d, src, vec], [dst_t])

        def jchain(j, S, b):
            tsl = slice(b * 512, (b + 1) * 512)
            jsl = slice(j * 128, (j + 1) * 128)
            tmp = S["tmp"]
            zr, zk, zv = S["z"]
            lw, cl, rel, epos, eneg, eprev, av, kpr, t1, t2 = tmp[1:11]
            rsh, vsh, kap = lw, cl, rel
            p = nextp()
            c.MM(p[:, :], lupb[0:32, jsl], th_b[0:32, :], True, True, [lupb, th_b], [p])
            c.ACT(lw[:], p[:, :], AF.Sigmoid, [p, vec], [lw], bias=V("w0", j))
            p = nextp()
            c.MM(p[:, :], lupb[32:64, jsl], th_b[32:64, :], True, True, [lupb, th_b], [p])
            c.ACT(av[:], p[:, :], AF.Sigmoid, [p, vec], [av], bias=V("a0", j))
            yield
            c.TS("dve", lw[:], lw[:], -0.6065306597126334, ALU.mult, [lw], [lw])
            c.op("dve", lambda e, cl=cl, lw=lw: e.tensor_tensor_scan(out=cl[:], data0=ones512[:], data1=lw[:], initial=0.0, op0=ALU.mult, op1=ALU.add), [ones512, lw], [cl])
            yield
            for cc in range(4):
                s_ = slice(cc * 128, (cc + 1) * 128)
                c.TS("dve", rel[:, s_], cl[:, s_], cl[:, cc * 128 + 63: cc * 128 + 64], ALU.subtract, [cl], [rel])
            yield
            c.ACT(epos[:], rel[:], AF.Exp, [rel], [epos])
            c.ACT(eneg[:], rel[:], AF.Exp, [rel], [eneg], scale=-1.0)
            c.TT("dve", t1[:], rel[:], lw[:], ALU.subtract, [rel, lw], [t1])
            yield
            c.ACT(eprev[:], t1[:], AF.Exp, [t1], [eprev])
            c.ACT(dmt[:, j, :], t1[:].rearrange("p (c t) -> p c t", t=128)[:, :, 0], AF.Exp, [t1], [dmt], scale=-1.0)
            c.CP("act", eet[:, j, :], epos[:].rearrange("p (c t) -> p c t", t=128)[:, :, 127], [epos], [eet])
            yield
            zproj(0 + j * 128, 128, zr, 2 + j)
            yield
            shift(rsh, rsh[:], zr, 128, V("mix_r", j), tmp[0])
            yield
            zproj(512 + j * 128, 128, zk, 6 + j)
            yield
            ksh = t2
            shift(ksh, ksh[:], zk, 128, V("mix_k", j), tmp[0])
            yield
            zproj(1024 + j * 128, 128, zv, 10 + j)
            yield
            shift(vsh, vsh[:], zv, 128, V("mix_v", j), tmp[0])
            yield
            kr = tmp[0]
            c.ACT(kr[:], ksh[:], AF.Copy, [ksh, vec], [kr], scale=V("k_k", j))
            sqk = S["bft"][5]
            c.ACT(sqk[:], kr[:], AF.Square, [kr], [sqk])
            yield
            p = nextp()
            c.MM(p[:, :], bones_b, sqk[:], True, True, [cbf, sqk], [p])
            c.RPOW(t1[:], p[:, :], [p], [t1], -0.5, bias=epsc[:, 1:2], RB=[epsc])
            yield
            c.TT("dve", kap[:], kr[:], t1[:], ALU.mult, [kr, t1], [kap])
            yield
            c.TS("dve", t1[:], av[:], -1.0, ALU.add, [av, vec], [t1], s2=V("k_a", j), op1=ALU.mult)
            c.STT(kpr[:], t1[:], 1.0, ksh[:], ALU.add, ALU.mult, [t1, ksh], [kpr])
            yield
            o_kt, o_bt, o_kk, o_rt, o_v, o_rk = S["bft"]
            c.TT("dve", o_kt[:], kap[:], eprev[:], ALU.mult, [kap, eprev], [o_kt])
            c.TT("pool", t1[:], kap[:], av[:], ALU.mult, [kap, av], [t1])
            yield
            c.TT("dve", o_bt[:], t1[:], eneg[:], ALU.mult, [t1, eneg], [o_bt])
            c.TT("dve", o_kk[:], kpr[:], eneg[:], ALU.mult, [kpr, eneg], [o_kk])
            c.TT("pool", o_rt[:], rsh[:], epos[:], ALU.mult, [rsh, epos], [o_rt])
            c.CP("act", o_v[:], vsh[:], [vsh], [o_v])
            yield
            c.TT("pool", t1[:], rsh[:], kpr[:], ALU.mult, [rsh, kpr], [t1])
            yield
            c.ACT(o_rk[:], t1[:], AF.Copy, [t1, vec], [o_rk], scale=V("r_k", j))
            for nm, tl in (("kt_fm", o_kt), ("bt_fm", o_bt), ("kk_fm", o_kk), ("rt_fm", o_rt)):
                c.dma("sp", dr[nm][j, :, tsl], tl[:], "st_" + tl.name, R=[tl])
            yield
            for ti_, (nm, tl) in enumerate((("bt_tm", o_bt), ("kk_tm", o_kk), ("v_tm", o_v))):
                pt = nexttr()
                for tt in range(4):
                    c.TR(pt[:, tt * 128:(tt + 1) * 128], tl[:, tt * 128:(tt + 1) * 128], ident_b, [tl, cbf], [pt])
                ts_ = S["trs"][ti_ % 2]
                c.CP("act" if ti_ % 2 == 0 else "dve", ts_[:], pt[:, :], [pt], [ts_])
                c.dma("sp", dr[nm].rearrange("(n p) f -> p n f", p=128)[:, b * 4:(b + 1) * 4, jsl],
                      ts_[:].rearrange("p (n f) -> p n f", f=128), "st_" + ts_.name, R=[ts_])
                yield
            p = nextp()
            for tt in range(4):
                c.MM(p[:, tt * 2:tt * 2 + 2], o_rk[:, tt * 128:(tt + 1) * 128],
                     cbf[:, 4, :].rearrange("p (i k) -> p i k", k=64)[:, :, 0], True, True, [o_rk, cbf], [p])
            c.CP("act", cts[:, :, 2 * j:2 * j + 2], p[:, 0:8].rearrange("p (t i) -> p t i", i=2), [p], [cts])
            yield


        def achain(b):
            tsl = slice(b * 512, (b + 1) * 512)
            c.dma("sp", posi[:], dr["pos"][:, tsl], "posi", W=[posi])
            u, uc, kf, f1 = ropeT[2:6]
            c.CP("dve", u[:], posi[:], [posi], [u])
            c.TS("dve", u[:], u[:], V("inv_freq"), ALU.mult, [u, vec], [u], s2=1.0 / (2 * math.pi), op1=ALU.mult)
            yield
            for which, dst in ((0, sinT), (1, cosT)):
                if which == 1:
                    c.TS("dve", uc[:], u[:], 0.25, ALU.add, [u], [uc])
                    src = uc
                else:
                    src = u
                c.CP("dve", ropei[:], src[:], [src], [ropei])
                yield
                c.CP("dve", kf[:], ropei[:], [ropei], [kf])
                yield
                c.TT("dve", f1[:], src[:], kf[:], ALU.subtract, [src, kf], [f1])
                yield
                c.STT(kf[:], f1[:], 0.5, f1[:], ALU.is_gt, ALU.subtract, [f1], [kf])
                yield
                if which == 0:
                    c.ACT(dst[:], kf[:], AF.Sin, [kf, vec], [dst], scale=V("sin_scale"))
                else:
                    c.ACT(dst[:], kf[:], AF.Sin, [kf], [dst], scale=-6.283185)
                yield
            qfs = [ropeT[2], ropeT[3]]
            t1s = [ropeT[4], ropeT[5]]
            t2s = [ropeT[6], ropeT[7]]
            it = 0
            for which, (c0, nm) in enumerate(((1696, "qh_fm"), (2208, "kh_fm"))):
                for j in range(4):
                    qf_, t1, t2 = qfs[it % 2], t1s[it % 2], t2s[it % 2]
                    p = proj_fm(c0 + j * 128, 128)
                    c.CP("act", qf_[:], p[:, :], [p], [qf_])
                    yield
                    p2 = nextp()
                    c.MM(p2[:, :], swap_f, qf_[:], True, True, [cf, qf_], [p2])
                    c.TT("dve", t2[:], p2[:, :], sinT[:], ALU.mult, [p2, sinT], [t2])
                    c.TT("pool", t1[:], qf_[:], cosT[:], ALU.mult, [qf_, cosT], [t1])
                    yield
                    ob = bft2[it % 2]
                    c.TT("dve", ob[:], t1[:], t2[:], ALU.add, [t1, t2], [ob])
                    c.dma("sp", dr[nm][j, :, tsl], ob[:], "st_" + ob.name, R=[ob])
                    it += 1
                    yield
            for tt in range(4):
                p = nextp()
                for k in range(8):
                    c.MM(p[:, :], xb[:, k, tt * 128:(tt + 1) * 128], Wb[:, k, 2720:3232], k == 0, k == 7, [xb, Wb], [p])
                ts_ = trsA[tt % 2]
                c.CP("act" if tt % 2 == 0 else "dve", ts_[:], p[:, :], [p], [ts_])
                c.dma("sp", dr["va_tm"][b * 512 + tt * 128: b * 512 + (tt + 1) * 128, :], ts_[:], "st_" + ts_.name, R=[ts_])
                yield

        def achain_head(gen, n):
            for _ in range(n):
                try:
                    next(gen)
                except StopIteration:
                    return
                yield

        def run_slots(slots):
            cur = [None] * len(slots)
            live = True
            while live:
                live = False
                for si, sl_ in enumerate(slots):
                    while True:
                        if cur[si] is None:
                            if not sl_:
                                break
                            cur[si] = sl_.pop(0)
                        try:
                            next(cur[si])
                            live = True
                            break
                        except StopIteration:
                            cur[si] = None

        for b in range(NB):
            tsl = slice(b * 512, (b + 1) * 512)
            if b == 0:
                c.dma("sp", xs3, dr["xT"].rearrange("k p t -> p k t")[:, :, tsl], "xs", W=[xs])
            c.ACT(sq[:], xs3, AF.Square, [xs], [sq])
            p = nextp()
            for k in range(8):
                c.MM(p[:, :], ones_f[:], sq[:, k, :], k == 0, k == 7, [ones_f, sq], [p])
            c.RPOW(rstd[:], p[:, :], [p], [rstd], -0.5, scale=1.0 / D, bias=epsc[:, 0:1], RB=[epsc])
            c.TT("dve", xb[:], xs3, rstd[:].unsqueeze(1).to_broadcast([128, 8, 512]), ALU.mult, [xs, rstd], [xb])
            if b + 1 < NB:
                c.dma("sp", xs3, dr["xT"].rearrange("k p t -> p k t")[:, :, slice((b + 1) * 512, (b + 2) * 512)], "xs", W=[xs])
            lora_done = [False]

            def lora(b=b):
                zw_ = sets[0]["z"][0]
                zproj(1536, 64, zw_, 0)
                yield
                shift(zs_wa, zs_wa[0:64, :], zw_, 64, V("mix_wa")[0:64, :], tmp[0])
                yield
                c.ACT(th_b[0:32, :], zs_wa[0:32, :], AF.Tanh, [zs_wa], [th_b])
                c.CP("act", th_b[32:64, :], zs_wa[32:64, :], [zs_wa], [th_b])
                yield
                zg_ = sets[1]["z"][0]
                zproj(1600, 96, zg_, 1)
                yield
                shift(zs_g, zs_g[0:96, :], zg_, 96, V("mix_g")[0:96, :], sets[1]["tmp"][0])
                yield
                c.ACT(sg_b[:, :], zs_g[0:96, :], AF.Sigmoid, [zs_g], [sg_b])
                yield
                for tt in range(4):
                    p = nextp()
                    c.MM(p[:, :], sg_b[:, tt * 128:(tt + 1) * 128], gupb[:, :], True, True, [sg_b, gupb], [p])
                    c.CP("dve", gts[:], p[:, :], [p], [gts])
                    c.dma("sp", dr["g_tm"][b * 512 + tt * 128: b * 512 + (tt + 1) * 128, :], gts[:], "st_gts", R=[gts])
                    yield
                lora_done[0] = True

            def guard(gen):
                assert lora_done[0], "jchain started before lora finished"
                yield from gen

            ach = achain(b)
            run_slots([[lora(), guard(jchain(0, sets[0], b)), guard(jchain(2, sets[0], b))],
                       [achain_head(ach, 12), guard(jchain(1, sets[1], b)), guard(jchain(3, sets[1], b))],
                       [ach]])
            c.dma("sp", dr["c_tm"].rearrange("(n p) h -> p n h", p=128)[:, b * 4:(b + 1) * 4, :], cts[:], "st_cts", R=[cts])
            c.dma("sp", dr["dm_fm"][:, :, b * 4:(b + 1) * 4], dmt[:], "st_dmt", R=[dmt])
            c.dma("sp", dr["ee_fm"][:, :, b * 4:(b + 1) * 4], eet[:], "st_eet", R=[eet])

        c.emit()


def make_consts():
    cm = np.zeros((128, 6, 128), np.float32)
    p = np.arange(128)[:, None]
    m = np.arange(128)[None, :]
    cm[:, 0, :] = (p == m)
    cm[:, 1, :] = (p < m)
    cm[:, 2, :] = (p <= m)
    cm[:, 3, :] = (p > m)
    cm[:, 4, :] = (p // 64 == m // 64)
    partner = np.where((np.arange(128) % 64) < 32, np.arange(128) + 32, np.arange(128) - 32)
    sw = np.zeros((128, 128), np.float32)
    sw[partner, np.arange(128)] = 1.0
    cm[:, 5, :] = sw
    cmask = np.zeros((128, 4, 512), np.float32)
    q = np.arange(512)[None, :]
    for i in range(4):
        cmask[:, i, :] = (128 * i + p <= q)
    return cm, cmask


def pack_shared(inp):
    g = lambda n: np.asarray(inp[n], np.float32)
    vec = np.zeros((128, NVEC), np.float32)

    def put(name, v, n):
        vec[:, VEC_OFF[name]:VEC_OFF[name] + n] = _cols(v, n)
    sm = g("shift_mix")[0]
    put("mix_r", sm[0:512], 4)
    put("mix_k", sm[512:1024], 4)
    put("mix_v", sm[1024:1536], 4)
    put("mix_wa", sm[1536:1600], 1)
    put("mix_g", sm[1600:1696], 1)
    for n in ("w0", "a0", "k_k", "k_a", "r_k"):
        put(n, g(n)[0].reshape(-1), 4)
    for n in ("norm_mix", "norm_cross", "norm_mem", "norm_ffn"):
        put(n, g(n)[0], 8)
    put("norm_final", g("norm_final"), 8)
    cw = g("conv_w")[0]
    for j in range(3):
        put("conv_w%d" % j, cw[j], 22)
    put("conv_b", g("conv_b")[0], 22)
    pidx = np.arange(128) % 32
    inv = (10000.0 ** (-(2.0 * pidx.astype(np.float32)) / 64.0)).astype(np.float32)
    vec[:, VEC_OFF["inv_freq"]] = inv
    sgn = np.where((np.arange(128) % 64) < 32, -1.0, 1.0).astype(np.float32)
    vec[:, VEC_OFF["sin_scale"]] = -6.283185 * sgn
    vec[:, VEC_OFF["subln"]] = g("subln_gain")[0]
    tmb = np.zeros((128, 3, 512), np.float32)
    tmb[:, 0, :] = g("lnx_gain")[0][None, :]
    tmb[:, 1, :] = g("lnx_bias")[0][None, :]
    tmb[:, 2, :] = np.tile(g("subln_gain")[0], 4)[None, :]
    lam = np.zeros((128, 4, 64), np.float32)
    for i, n in enumerate(("lam_q1", "lam_k1", "lam_q2", "lam_k2")):
        lam[:, i, :] = g(n)[0][None, :]
    cm, cmask = make_consts()
    sh = {
        "vecs": vec, "tmb": tmb, "lam": lam, "consts": cm, "cmask": cmask,
        "w_in": np.ascontiguousarray(g("w_in")[0].reshape(8, 128, NIN)),
        "lup": np.ascontiguousarray(np.concatenate([g("w_lora_up")[0], g("a_lora_up")[0]], 0)),
        "gup": np.ascontiguousarray(g("g_lora_up")[0]),
        "w_out": np.ascontiguousarray(g("w_out")[0].reshape(8, 128, D)),
        "wq_c": np.ascontiguousarray(g("wq_c")[0].reshape(8, 128, D)),
        "wkv_c": np.ascontiguousarray(g("wkv_c")[0].reshape(8, 128, 2 * D)),
        "wo_c": np.ascontiguousarray(g("wo_c")[0].reshape(8, 128, D)),
        "w_up": np.ascontiguousarray(g("w_up")[0].reshape(8, 128, 2 * DFF)),
        "w_down": np.ascontiguousarray(g("w_down")[0].reshape(22, 128, D)),
    }
    return sh


def pack_core(inp, b, T):
    x = np.asarray(inp["x"], np.float32)[b]
    mem = np.asarray(inp["mem"], np.float32)[b]
    pos = np.asarray(inp["positions"], np.int32)[b]
    return {
        "xT": np.ascontiguousarray(x.T.reshape(8, 128, T)),
        "memT": np.ascontiguousarray(mem.T.reshape(8, 128, MEM)),
        "pos": np.ascontiguousarray(np.broadcast_to(pos[None, :], (128, T))),
    }


_NC_CACHE = {}


def kernel(**inputs):
    x = np.asarray(inputs["x"])
    B, T, _ = x.shape
    if T not in _NC_CACHE:
        _NC_CACHE[T] = build(T)
    nc = _NC_CACHE[T]
    sh = pack_shared(inputs)
    in_maps = []
    for b in range(B):
        m = dict(sh)
        m.update(pack_core(inputs, b, T))
        in_maps.append(m)
    res = run_bass_kernel_spmd(nc, in_maps, core_ids=list(range(B)))
    out = np.stack([np.ascontiguousarray(res.results[b]["outT"].reshape(D, T).T) for b in range(B)], 0)
    return out.astype(np.float32)


def phase_B(nc, c, dr, T):
    NCH = T // 128
    with contextlib.ExitStack() as st:
        P = Pool_(nc, st)
        cf, cbf = load_consts(c, P, dr)
        ident_b = cbf[:, 0, :]
        mSU = P.sb("mSU", [128, 8, 128], F32)
        mUI = P.sb("mUI", [128, 8, 128], F32)
        mSL = P.sb("mSL", [128, 8, 128], F32)
        idr = P.sb("idr", [128, 8, 128], BF16)
        for h in range(8):
            c.CP("pool", mSU[:, h, :], cf[:, 1, :], [cf], [mSU])
            c.CP("pool", mUI[:, h, :], cf[:, 2, :], [cf], [mUI])
            c.CP("pool", mSL[:, h, :], cf[:, 3, :], [cf], [mSL])
            c.CP("pool", idr[:, h, :], cf[:, 0, :], [cf], [idr])
        tmb = P.sb("tmb", [128, 3, 512], F32)
        c.dma("sp", tmb[:], dr["tmb"], "tmb", W=[tmb])
        dm = P.sb("dm", [128, 4, NCH], F32)
        ee = P.sb("ee", [128, 4, NCH], F32)
        c.dma("sp", dm[:], dr["dm_fm"], "dm", W=[dm])
        c.dma("sp", ee[:], dr["ee_fm"], "ee", W=[ee])
        S0 = P.sb("S0", [128, 4, 64], F32)
        c.op("pool", lambda e: e.memset(S0[:], 0.0), W=[S0])
        Sm = P.sb("Sm", [128, 4, 64], F32)
        Smb = P.sb("Smb", [128, 4, 64], BF16)
        NBUF = 2
        fm = {n: [P.sb(n, [128, 4, 128], BF16) for _ in range(NBUF)] for n in ("kt", "bt", "kk", "rt")}
        tm = {n: [P.sb(n, [128, 512], BF16) for _ in range(NBUF)] for n in ("btT", "kkT", "vT")}
        gt = [P.sb("gt", [128, 512], F32) for _ in range(NBUF)]
        ct = [P.sb("ct", [128, 8], F32) for _ in range(NBUF)]
        Zb = P.sb("Zb", [128, 2, 256], BF16)
        nU = P.sb("nU", [128, 512], BF16)
        y32 = P.sb("y32", [128, 512], F32)
        ysq = P.sb("ysq", [128, 512], F32)
        vf = P.sb("vf", [128, 512], F32)
        st8 = [P.sb("st8", [128, 8], F32) for _ in range(4)]
        yo = P.sb("yo", [128, 512], BF16)
        yT = P.sb("yT", [128, 4, 128], BF16)
        QA = P.ps("QA", [128, 1024])
        SP = P.ps("SPs", [128, 512])
        TP = P.ps("TPs", [128, 512], BF16)
        dpi = [0]

        def nextd():
            dpi[0] += 1
            return DP[dpi[0] % 2]

        def HP(i):
            return slice(i * 64, (i + 1) * 64)

        def mk(name, n=2):
            return [[P.sb(name, [128, 4, 128], BF16) for _ in range(2)] for _ in range(n)]
        AakH, ArbH, ArkH, XTfH = mk("AakH"), mk("ArbH"), mk("ArkH"), mk("XTfH")
        NbH, MbH, XTH = mk("NbH"), mk("MbH"), mk("XTH")
        DPh = [[P.ps("DPh", [128, 512]) for _ in range(2)] for _ in range(2)]
        dph_i = [0, 0]

        def nexth(half):
            dph_i[half] += 1
            return DPh[half][dph_i[half] % 2]

        def gen_load(ch):
            bi = ch % NBUF
            tsl = slice(ch * 128, (ch + 1) * 128)
            for n, src in (("kt", "kt_fm"), ("bt", "bt_fm"), ("kk", "kk_fm"), ("rt", "rt_fm")):
                t_ = fm[n][bi]
                c.dma("sp", t_[:], dr[src].rearrange("j p t -> p j t")[:, :, tsl], t_.name, W=[t_])
            for n, src in (("btT", "bt_tm"), ("kkT", "kk_tm"), ("vT", "v_tm")):
                t_ = tm[n][bi]
                c.dma("sp", t_[:], dr[src][tsl, :], t_.name, W=[t_])
            c.dma("sp", gt[bi][:], dr["g_tm"][tsl, :], gt[bi].name, W=[gt[bi]])
            c.dma("sp", ct[bi][:], dr["c_tm"][tsl, :], ct[bi].name, W=[ct[bi]])

        def gen_pre(ch, half):
            bi = ch % NBUF
            kt, bt, kk, rt = fm["kt"][bi], fm["bt"][bi], fm["kk"][bi], fm["rt"][bi]
            Aak, Arb, Ark = AakH[ch % 2][half], ArbH[ch % 2][half], ArkH[ch % 2][half]
            Nb, Mb, XT = NbH[half], MbH[half], XTH[half]
            hp = slice(half * 64, (half + 1) * 64)
            fl = lambda t_: t_[:].rearrange("p h t -> p (h t)")
            mk4 = lambda m_: m_[:, 0:4, :].rearrange("p h t -> p (h t)")

            def headmm(dst, A, Bm):
                for j in range(4):
                    c.MM(dst[:, j * 128:(j + 1) * 128], A[hp, j, :], Bm[hp, j, :], True, True, [A, Bm], [dst])

            d = nexth(half); headmm(d, bt, kt)
            N0 = Nb[0]
            c.STT(fl(N0), d[:, :], -1.0, mk4(mSU), ALU.mult, ALU.mult, [d, mSU], [N0])
            yield
            d = nexth(half); headmm(d, kt, bt)
            M0 = Mb[0]
            c.STT(fl(M0), d[:, :], -1.0, mk4(mSL), ALU.mult, ALU.mult, [d, mSL], [M0])
            yield
            d = nexth(half); headmm(d, kk, kt)
            c.TT("dve", fl(Aak), d[:, :], mk4(mSU), ALU.mult, [d, mSU], [Aak])
            yield
            d = nexth(half); headmm(d, bt, rt)
            c.TT("dve", fl(Arb), d[:, :], mk4(mUI), ALU.mult, [d, mUI], [Arb])
            yield
            d = nexth(half); headmm(d, kk, rt)
            c.TT("dve", fl(Ark), d[:, :], mk4(mUI), ALU.mult, [d, mUI], [Ark])
            yield
            xc = XT[0]
            c.TT("dve", fl(xc), fl(N0), idr[:, 0:4, :].rearrange("p h t -> p (h t)"), ALU.add, [N0, idr], [xc])
            yield
            Nc, Mc = N0, M0
            for lv in range(1, 7):
                Nn, Mn = Nb[lv % 2], Mb[lv % 2]
                dM = nexth(half)
                for j in range(4):
                    c.MM(dM[:, j * 128:(j + 1) * 128], Nc[:, j, :], Mc[:, j, :], True, True, [Nc, Mc], [dM])
                c.CP("act", fl(Mn), dM[:, :], [dM], [Mn])
                yield
                if lv < 6:
                    dN = nexth(half)
                    for j in range(4):
                        c.MM(dN[:, j * 128:(j + 1) * 128], Mc[:, j, :], Nc[:, j, :], True, True, [Nc, Mc], [dN])
                    c.CP("act", fl(Nn), dN[:, :], [dN], [Nn])
                    yield
                dX = nexth(half)
                for j in range(4):
                    c.MM(dX[:, j * 128:(j + 1) * 128], Mn[:, j, :], xc[:, j, :], True, True, [Mn, xc], [dX])
                xn = XT[lv % 2] if lv < 6 else XTfH[ch % 2][half]
                c.TT("dve", fl(xn), dX[:, :], fl(xc), ALU.add, [dX, xc], [xn])
                yield
                xc, Nc, Mc = xn, Nn, Mn

        def gen_seq(ch):
            bi = ch % NBUF
            tsl = slice(ch * 128, (ch + 1) * 128)
            AakQ, ArbQ, ArkQ, XTQ = AakH[ch % 2], ArbH[ch % 2], ArkH[ch % 2], XTfH[ch % 2]
            kt, bt, kk, rt = fm["kt"][bi], fm["bt"][bi], fm["kk"][bi], fm["rt"][bi]
            btT, kkT, vT = tm["btT"][bi], tm["kkT"][bi], tm["vT"][bi]
            for j in range(4):
                c.TS("dve", Sm[:, j, :], S0[:, j, :], dm[:, j, ch:ch + 1], ALU.mult, [S0, dm], [Sm])
                yield
            c.CP("act", Smb[:], Sm[:], [Sm], [Smb])
            yield
            dZ = QA
            for h in range(8):
                j, i = h // 2, h % 2
                o = dZ[:, i * 512 + j * 64: i * 512 + (j + 1) * 64]
                c.MM(o, kt[HP(i), j, :], Smb[HP(i), j, :], True, False, [kt, Smb], [dZ])
                c.MM(o, AakQ[i][:, j, :], vT[:, h * 64:(h + 1) * 64], False, True, [AakQ[i], vT], [dZ])
            c.CP("act", Zb[:], dZ[:, :].rearrange("p (i x) -> p i x", i=2)[:, :, 0:256], [dZ], [Zb])
            yield
            for h in range(8):
                j, i = h // 2, h % 2
                c.MM(SP[:, h * 64:(h + 1) * 64], XTQ[i][:, j, :], Zb[:, i, j * 64:(j + 1) * 64], True, True, [XTQ[i], Zb], [SP])
            c.ACT(nU[:], SP[:, :], AF.Copy, [SP], [nU], scale=-1.0)
            yield
            dY = QA
            for h in range(8):
                j, i = h // 2, h % 2
                o = dY[:, i * 512 + j * 64: i * 512 + (j + 1) * 64]
                hs = slice(h * 64, (h + 1) * 64)
                c.MM(o, rt[HP(i), j, :], Smb[HP(i), j, :], True, False, [rt, Smb], [dY])
                c.MM(o, ArbQ[i][:, j, :], nU[:, hs], False, False, [ArbQ[i], nU], [dY])
                c.MM(o, ArkQ[i][:, j, :], vT[:, hs], False, True, [ArkQ[i], vT], [dY])
            dS = SP
            for h in range(8):
                j, i = h // 2, h % 2
                hs = slice(h * 64, (h + 1) * 64)
                o = dS[HP(i), j * 64:(j + 1) * 64]
                c.MM(o, btT[:, hs], nU[:, hs], True, False, [btT, nU], [dS])
                c.MM(o, kkT[:, hs], vT[:, hs], False, True, [kkT, vT], [dS])
            c.TT("dve", Sm[:].rearrange("p j v -> p (j v)"), dS[:, 0:256], Sm[:].rearrange("p j v -> p (j v)"), ALU.add, [dS, Sm], [Sm])
            yield
            for j in range(4):
                c.TS("dve", S0[:, j, :], Sm[:, j, :], ee[:, j, ch:ch + 1], ALU.mult, [Sm, ee], [S0])
                yield
            y4 = y32[:].rearrange("p (j i v) -> p i j v", i=2, v=64)
            for i in range(2):
                c.CP("act", y4[:, i, :, :], dY[:, i * 512:i * 512 + 256].rearrange("p (j v) -> p j v", v=64), [dY], [y32])
                yield
            y3 = y32[:].rearrange("p (h v) -> p h v", v=64)
            s1, s2, mean, rs = st8
            c.op("dve", lambda e, s1=s1, y3=y3: e.tensor_reduce(out=s1[:], in_=y3, axis=AX.X, op=ALU.add), [y32], [s1])
            c.ACT(ysq[:], y32[:], AF.Square, [y32], [ysq])
            yield
            c.op("dve", lambda e, s2=s2: e.tensor_reduce(out=s2[:], in_=ysq[:].rearrange("p (h v) -> p h v", v=64), axis=AX.X, op=ALU.add), [ysq], [s2])
            c.TS("dve", mean[:], s1[:], 1.0 / 64, ALU.mult, [s1], [mean])
            yield
            c.TT("dve", s1[:], mean[:], mean[:], ALU.mult, [mean], [s1])
            yield
            c.STT(s2[:], s2[:], 1.0 / 64, s1[:], ALU.mult, ALU.subtract, [s2, s1], [s2])
            yield
            c.ACT(rs[:], s2[:], AF.Sqrt, [s2], [rs], bias=LNX_EPS)
            yield
            c.op("dve", lambda e, rs=rs: e.reciprocal(out=rs[:], in_=rs[:]), [rs], [rs])
            c.TT("dve", y3, y3, mean[:].unsqueeze(2).to_broadcast([128, 8, 64]), ALU.subtract, [y32, mean], [y32])
            yield
            c.TT("dve", y3, y3, rs[:].unsqueeze(2).to_broadcast([128, 8, 64]), ALU.mult, [y32, rs], [y32])
            yield
            c.TT("pool", y32[:], y32[:], tmb[:, 0, :], ALU.mult, [y32, tmb], [y32])
            yield
            c.TT("pool", y32[:], y32[:], tmb[:, 1, :], ALU.add, [y32, tmb], [y32])
            yield
            c.CP("act", vf[:], vT[:], [vT], [vf])
            yield
            c.TT("dve", vf[:].rearrange("p (h v) -> p h v", v=64), vf[:].rearrange("p (h v) -> p h v", v=64),
                 ct[bi][:].unsqueeze(2).to_broadcast([128, 8, 64]), ALU.mult, [vf, ct[bi]], [vf])
            c.TT("pool", y32[:], y32[:], vf[:], ALU.add, [y32, vf], [y32])
            yield
            c.TT("dve", yo[:], y32[:], gt[bi][:], ALU.mult, [y32, gt[bi]], [yo])
            yield
            for q in range(4):
                c.TR(TP[:, q * 128:(q + 1) * 128], yo[:, q * 128:(q + 1) * 128], ident_b, [yo, cbf], [TP])
            c.CP("act", yT[:].rearrange("p q t -> p (q t)"), TP[:, :], [TP], [yT])
            yield
            c.dma("sp", dr["ymix_fm"].rearrange("k p t -> p k t")[:, 0:4, tsl], yT[:], "st_yT", R=[yT])
            yield

        for ch in range(NCH + 1):
            gens = []
            if ch < NCH:
                gen_load(ch)
            if ch >= 1:
                gens.append(gen_seq(ch - 1))
            if ch < NCH:
                gens.append(gen_pre(ch, 0))
                gens.append(gen_pre(ch, 1))
            while gens:
                for g_ in list(gens):
                    try:
                        next(g_)
                    except StopIteration:
                        gens.remove(g_)
        c.emit()


def phase_C(nc, c, dr, T):
    NCH = T // 128
    NB = T // 512
    with contextlib.ExitStack() as st:
        P = Pool_(nc, st)
        vec = P.sb("vec", [128, NVEC], F32)
        c.dma("sp", vec[:], dr["vecs"], "vec", W=[vec])
        cmf = P.sb("cmf", [128, 4, 512], F32)
        cmb = P.sb("cmb", [128, 4, 512], BF16)
        c.dma("sp", cmf[:], dr["cmask"], "cmf", W=[cmf])
        c.TS("dve", cmb[:], cmf[:], 30000.0, ALU.mult, [cmf], [cmb], s2=-30000.0, op1=ALU.add)
        idf = P.sb("idf", [128, 128], F32)
        idb = P.sb("idb", [128, 128], BF16)
        c.dma("sp", idf[:], dr["consts"][:, 0, :], "idf", W=[idf])
        c.CP("dve", idb[:], idf[:], [idf], [idb])
        lam = P.sb("lam", [128, 4, 64], F32)
        c.dma("sp", lam[:], dr["lam"], "lam", W=[lam])
        lp = P.sb("lp", [128, 2, 64], F32)
        ls = P.sb("ls", [128, 2], F32)
        nlam = P.sb("nlam", [128, 1], F32)
        c.TT("dve", lp[:, 0, :], lam[:, 0, :], lam[:, 1, :], ALU.mult, [lam], [lp])
        c.TT("dve", lp[:, 1, :], lam[:, 2, :], lam[:, 3, :], ALU.mult, [lam], [lp])
        c.op("dve", lambda e: e.tensor_reduce(out=ls[:], in_=lp[:], axis=AX.X, op=ALU.add), [lp], [ls])
        c.ACT(ls[:], ls[:], AF.Exp, [ls], [ls])
        c.TT("dve", nlam[:], ls[:, 1:2], ls[:, 0:1], ALU.subtract, [ls], [nlam])
        c.TS("dve", nlam[:], nlam[:], -LAMBDA_INIT, ALU.add, [nlam], [nlam])
        ones_f = P.sb("ones_f", [128, 128], F32)
        c.op("pool", lambda e: e.memset(ones_f[:], 1.0), W=[ones_f])
        ones_bb = P.sb("ones_bb", [128, 128], BF16)
        c.op("pool", lambda e: e.memset(ones_bb[:], 1.0), W=[ones_bb])
        osqb = P.sb("osqb", [128, 512], BF16)
        Kth = [P.sb("Kt", [128, T], BF16) for _ in range(4)]
        NVG = (NCH + 7) // 8
        Vag = [P.sb("Va", [128, 8, 512], BF16) for _ in range(NVG)]
        vsrc = dr["va_tm"].rearrange("(n p) f -> p n f", p=128)
        for h in range(4):
            c.dma("sp" if h % 2 == 0 else "act", Kth[h][:], dr["kh_fm"][h], Kth[h].name, W=[Kth[h]])
        for gi in range(NVG):
            n0 = gi * 8
            n1 = min(NCH, n0 + 8)
            c.dma("sp" if gi % 2 == 0 else "act", Vag[gi][:, 0:n1 - n0, :], vsrc[:, n0:n1, :], "VaS" if gi % 2 == 0 else "VaA", W=[Vag[gi]])
        for gi in range(NVG):
            sl_ = "VaS" if gi % 2 == 0 else "VaA"
            Vag[gi].buf.w = (("D", sl_), c.dsem[sl_][1])
        Qt = [P.sb("Qt", [128, 4, 512], BF16) for _ in range(2)]
        NPT = 4
        PT = [P.sb("PT", [128, 2, 512], BF16) for _ in range(NPT)]
        PS = [P.ps("PS", [128, 1024]) for _ in range(2)]
        ACC = [P.ps("ACC", [128, 512]) for _ in range(2)]
        EPt = P.ps("EPt", [128, 1024])
        EPs = EPt
        Pacc = [P.sb("Pacc", [128, 2, 512], F32) for _ in range(2)]
        oraw = [P.sb("oraw", [128, 2, 512], F32) for _ in range(2)]
        rden = P.sb("rden", [128, 2, 512], F32)
        o1 = P.sb("o1", [128, 512], F32)
        o2 = P.sb("o2", [128, 512], F32)
        osq = P.sb("osq", [128, 512], F32)
        rs = P.sb("rs", [128, 512], F32)
        yb = [P.sb("yb", [128, 512], BF16) for _ in range(2)]
        epsc = P.sb("epsc", [128, 1], F32)
        c.op("pool", lambda e: e.memset(epsc[:], SUBLN_EPS), W=[epsc])
        nlam8 = P.sb("nlam8", [128, 1], F32)

        tiles = []
        for qb in range(NB):
            for h in range(4):
                nkt = 4 * (qb + 1)
                for kt in range(nkt):
                    tiles.append((qb, h, kt, nkt))
        NT = len(tiles)

        def stage1(t):
            qb, h, kt, nkt = tiles[t]
            if qb == 0 and h == 0 and kt == 0:
                qt = Qt[0]
                c.dma("sp", qt[:], dr["qh_fm"].rearrange("j p t -> p j t")[:, :, 0:512], qt.name, W=[qt])
            if h == 3 and kt == 0 and qb + 1 < NB:
                qn = Qt[(qb + 1) % 2]
                c.dma("sp", qn[:], dr["qh_fm"].rearrange("j p t -> p j t")[:, :, (qb + 1) * 512:(qb + 2) * 512], qn.name, W=[qn])
            qt = Qt[qb % 2]
            i = kt - 4 * qb
            q0 = 128 * i if i > 0 else 0
            ps = PS[t % 2]
            pt = PT[t % NPT]
            for cm in range(2):
                hp = slice(cm * 64, (cm + 1) * 64)
                c.MM(ps[:, cm * 512 + q0:(cm + 1) * 512], Kth[h][hp, kt * 128:(kt + 1) * 128], qt[hp, h, q0:512], True, i < 0, [Kth[h], qt], [ps])
            if i >= 0:
                for cm in range(2):
                    c.MM(ps[:, cm * 512 + q0:(cm + 1) * 512], idb[:], cmb[:, i, q0:512], False, True, [idb, cmb], [ps])
            psv = ps[:, :].rearrange("p (m q) -> p m q", m=2)[:, :, q0:512]
            c.ACT(pt[:, :, q0:512], psv, AF.Exp, [ps], [pt], scale=0.125)

        def stage2(t):
            qb, h, kt, nkt = tiles[t]
            g = qb * 4 + h
            i = kt - 4 * qb
            q0 = 128 * i if i > 0 else 0
            pt = PT[t % NPT]
            pa = Pacc[g % 2]
            for cm in range(2):
                c.MM(ACC[cm][:, q0:512], Vag[kt // 8][:, kt % 8, h * 128:(h + 1) * 128], pt[:, cm, q0:512], kt == 0, kt == nkt - 1, [Vag[kt // 8], pt], [ACC[cm]])
            if kt == 0:
                c.CP("dve", pa[:], pt[:], [pt], [pa])
            else:
                c.TT("dve", pa[:, :, q0:512], pa[:, :, q0:512], pt[:, :, q0:512], ALU.add, [pa, pt], [pa])
            if kt == nkt - 1:
                orw = oraw[g % 2]
                c.CP("act", orw[:, 0, :], ACC[0][:, :], [ACC[0]], [orw])
                c.CP("dve", orw[:, 1, :], ACC[1][:, :], [ACC[1]], [orw])
                return (qb, h, g)
            return None

        def ep1a(qb, h, g):
            pa = Pacc[g % 2]
            for cm in range(2):
                c.MM(EPt[:, cm * 512:(cm + 1) * 512], ones_f[:], pa[:, cm, :], True, True, [ones_f, pa], [EPt])

        def ep1b(qb, h, g):
            orw = oraw[g % 2]
            for cm in range(2):
                c.RPOW(rden[:, cm, :], EPt[:, cm * 512:(cm + 1) * 512], [EPt], [rden], -1.0)
            c.TT("pool", o1[:], orw[:, 0, :], rden[:, 0, :], ALU.mult, [orw, rden], [o1])
            c.TT("pool", o2[:], orw[:, 1, :], rden[:, 1, :], ALU.mult, [orw, rden], [o2])
            c.TS("pool", o2[:], o2[:], nlam[:, 0:1], ALU.mult, [o2, nlam], [o2], s2=0.0, op1=ALU.add)
            c.TT("pool", o1[:], o1[:], o2[:], ALU.add, [o1, o2], [o1])

        def ep2a(qb, h, g):
            c.ACT(osqb[:], o1[:], AF.Square, [o1], [osqb])

        def ep2b(qb, h, g):
            c.MM(EPt[:, 0:512], ones_bb[:], osqb[:], True, True, [ones_bb, osqb], [EPt])

        def ep3(qb, h, g):
            c.RPOW(rs[:], EPt[:, 0:512], [EPt], [rs], -0.5, scale=1.0 / 128, bias=epsc[:, 0:1], RB=[epsc])
            c.TT("pool", o1[:], o1[:], rs[:], ALU.mult, [o1, rs], [o1])
            y_ = yb[g % 2]
            c.TS("pool", y_[:], o1[:], vec[:, VEC_OFF["subln"]:VEC_OFF["subln"] + 1], ALU.mult, [o1, vec], [y_], s2=1.0 - LAMBDA_INIT, op1=ALU.mult)
            c.dma("sp", dr["ymix_fm"][4 + h, :, qb * 512:(qb + 1) * 512], y_[:], "st_" + y_.name, R=[y_])

        LA = 1
        pend = []

        def flush(upto_g=None, force=False):
            while pend:
                cd, fn_, r_ = pend[0]
                if force or cd <= 0 or (upto_g is not None and r_[2] <= upto_g):
                    pend.pop(0)
                    fn_(*r_)
                else:
                    break

        for t in range(NT + LA):
            if t < NT:
                stage1(t)
            if t - LA >= 0:
                qb_, h_, kt_, nkt_ = tiles[t - LA]
                if kt_ == 0:
                    flush(upto_g=qb_ * 4 + h_ - 2)
                r_ = stage2(t - LA)
                if r_ is not None:
                    pend.append([8, ep1a, r_])
                    pend.append([16, ep1b, r_])
                    pend.append([28, ep2a, r_])
                    pend.append([33, ep2b, r_])
                    pend.append([40, ep3, r_])
                for pe_ in pend:
                    pe_[0] -= 1
                flush()
        flush(force=True)
        c.emit()


def rms_rstd(c, src, sqb, ones_b, ps, rstd, epsc, n=512):
    c.ACT(sqb[:, :, 0:n], src[:, :, 0:n], AF.Square, [src], [sqb])
    for k in range(8):
        c.MM(ps[:, 0:n], ones_b[:], sqb[:, k, 0:n], k == 0, k == 7, [ones_b, sqb], [ps])
    c.RPOW(rstd[:, 0:n], ps[:, 0:n], [ps], [rstd], -0.5, scale=1.0 / D, bias=epsc[:, 0:1], RB=[epsc])


def phase_D1(nc, c, dr, T):
    NB = T // 512
    with contextlib.ExitStack() as st:
        P = Pool_(nc, st)
        vec = P.sb("vec", [128, NVEC], F32)
        c.dma("sp", vec[:], dr["vecs"], "vec", W=[vec])
        ones_b = P.sb("ones_b", [128, 128], BF16)
        c.op("pool", lambda e: e.memset(ones_b[:], 1.0), W=[ones_b])
        epsc = P.sb("epsc", [128, 1], F32)
        c.op("pool", lambda e: e.memset(epsc[:], NORM_EPS), W=[epsc])
        xs2 = [P.sb("xs", [128, 4096], F32) for _ in range(3)]
        stg = xs2[0:2]
        Wo = P.sb("Wo", [128, 8, D], BF16)
        Wq = P.sb("Wq", [128, 8, D], BF16)
        Wo2 = P.sb("Wo2", [128, 8, D], BF16)
        Wkv = P.sb("Wkv", [128, 8, 2 * D], BF16)
        for k0 in range(0, 8, 4):
            pass
        c.dma("pool", Wo[:], dr["w_out"].rearrange("k p n -> p k n"), "Wo_ld", W=[Wo])
        load_weight_bf16(c, P, dr, "wq_c", 8, D, Wq, stg, VEC_OFF["norm_cross"], vec)
        c.dma("pool", Wo2[:], dr["wo_c"].rearrange("k p n -> p k n"), "Wo2_ld", W=[Wo2])
        load_weight_bf16(c, P, dr, "wkv_c", 8, 2 * D, Wkv, stg, VEC_OFF["norm_mem"], vec)
        pp = [P.ps("pp", [128, 512]) for _ in range(6)]
        ppi = [0]

        def nextp():
            ppi[0] += 1
            return pp[ppi[0] % 6]
        sqb = P.sb("sqb", [128, 8, 512], BF16)
        xb = P.sb("xb", [128, 8, 512], BF16)
        yms = [P.sb("ym", [128, 8, 512], BF16) for _ in range(3)]
        qcb = P.sb("qcb", [128, 8, 512], BF16)
        ocb = P.sb("ocb", [128, 8, 512], BF16)
        rstd = P.sb("rstd", [128, 512], F32)
        rden = P.sb("rden", [128, 512], F32)
        PTc = [P.sb("PTc", [128, 512], BF16) for _ in range(2)]
        Kc = P.sb("Kc", [128, 8, MEM], BF16)
        Vc = P.sb("Vc", [128, 2, D], BF16)
        rst_m = P.sb("rst_m", [128, 2], F32)
        xsT = xs2[0]
        xs = xsT[:].rearrange("p (k t) -> p k t", t=512)
        c.dma("sp", xs[:, :, 0:MEM], dr["memT"].rearrange("k p t -> p k t"), xsT.name, W=[xsT])
        p = nextp()
        c.ACT(sqb[:, :, 0:MEM], xs[:, :, 0:MEM], AF.Square, [xsT], [sqb])
        for k in range(8):
            c.MM(p[:, 0:MEM], ones_b[:], sqb[:, k, 0:MEM], k == 0, k == 7, [ones_b, sqb], [p])
        c.RPOW(rstd[:, 0:MEM], p[:, 0:MEM], [p], [rstd], -0.5, scale=1.0 / D, bias=epsc[:, 0:1], RB=[epsc])
        c.CP("dve", xb[:, :, 0:MEM], xs[:, :, 0:MEM], [xsT], [xb])
        p = nextp()
        for mt in range(2):
            for k in range(8):
                c.MM(p[:, mt:mt + 1], sqb[:, k, mt * 128:(mt + 1) * 128], ones_b[:, 0:1], k == 0, k == 7, [sqb, ones_b], [p])
        c.ACT(rst_m[:], p[:, 0:2], AF.Sqrt, [p], [rst_m], scale=1.0 / D, bias=NORM_EPS)
        c.op("dve", lambda e: e.reciprocal(out=rst_m[:], in_=rst_m[:]), [rst_m], [rst_m])
        for n in range(8):
            p = nextp()
            for k in range(8):
                c.MM(p[:, 0:MEM], Wkv[:, k, n * 128:(n + 1) * 128], xb[:, k, 0:MEM], k == 0, k == 7, [Wkv, xb], [p])
            c.TT("dve", Kc[:, n, :], p[:, 0:MEM], rstd[:, 0:MEM], ALU.mult, [p, rstd], [Kc])
        for mt in range(2):
            for nn in range(2):
                p = nextp()
                for k in range(8):
                    c.MM(p[:, :], xb[:, k, mt * 128:(mt + 1) * 128], Wkv[:, k, D + nn * 512: D + (nn + 1) * 512], k == 0, k == 7, [Wkv, xb], [p])
                c.TS("dve", Vc[:, mt, nn * 512:(nn + 1) * 512], p[:, :], rst_m[:, mt:mt + 1], ALU.mult, [p, rst_m], [Vc])
        def loads(b):
            tsl = slice(b * 512, (b + 1) * 512)
            xT_ = xs2[b % 3]
            c.dma("sp", xT_[:].rearrange("p (k t) -> p k t", t=512), dr["xT"].rearrange("k p t -> p k t")[:, :, tsl], xT_.name, W=[xT_])
            c.dma("sp", yms[b % 3][:], dr["ymix_fm"].rearrange("k p t -> p k t")[:, :, tsl], yms[b % 3].name, W=[yms[b % 3]])

        def wout(b, n0, n1):
            xT_ = xs2[b % 3]
            x3 = xT_[:].rearrange("p (k t) -> p k t", t=512)
            ym_ = yms[b % 3]
            for n in range(n0, n1):
                p = nextp()
                for k in range(8):
                    c.MM(p[:, :], Wo[:, k, n * 128:(n + 1) * 128], ym_[:, k, :], k == 0, k == 7, [Wo, ym_], [p])
                c.TT("dve", x3[:, n, :], p[:, :], x3[:, n, :], ALU.add, [p, xT_], [xT_])

        loads(0)
        if NB > 1:
            loads(1)
        wout(0, 0, 8)
        for b in range(NB):
            tsl = slice(b * 512, (b + 1) * 512)
            xsT = xs2[b % 3]
            xs = xsT[:].rearrange("p (k t) -> p k t", t=512)
            if b + 2 < NB:
                loads(b + 2)
            p = nextp()
            c.ACT(sqb[:], xs, AF.Square, [xsT], [sqb])
            if b + 1 < NB:
                wout(b + 1, 0, 4)
            for k in range(8):
                c.MM(p[:, :], ones_b[:], sqb[:, k, :], k == 0, k == 7, [ones_b, sqb], [p])
            c.RPOW(rstd[:], p[:, :], [p], [rstd], -0.5, scale=1.0 / D, bias=epsc[:, 0:1], RB=[epsc])
            if b + 1 < NB:
                wout(b + 1, 4, 8)
            c.TT("dve", xb[:], xs, rstd[:].unsqueeze(1).to_broadcast([128, 8, 512]), ALU.mult, [xsT, rstd], [xb])
            for n in range(8):
                p = nextp()
                for k in range(8):
                    c.MM(p[:, :], Wq[:, k, n * 128:(n + 1) * 128], xb[:, k, :], k == 0, k == 7, [Wq, xb], [p])
                c.CP("act", qcb[:, n, :], p[:, :], [p], [qcb])
            for hq in range(4):
                for mt in range(2):
                    p = nextp()
                    for dd in range(2):
                        c.MM(p[:, :], Kc[:, 2 * hq + dd, mt * 128:(mt + 1) * 128], qcb[:, 2 * hq + dd, :], dd == 0, dd == 1, [Kc, qcb], [p])
                    c.ACT(PTc[mt][:], p[:, :], AF.Exp, [p], [PTc[mt]], scale=1.0 / 16)
                p = nextp()
                for mt in range(2):
                    c.MM(p[:, :], ones_b[:], PTc[mt][:], mt == 0, mt == 1, [ones_b, PTc[mt]], [p])
                c.RPOW(rden[:], p[:, :], [p], [rden], -1.0)
                for dd in range(2):
                    p = nextp()
                    for mt in range(2):
                        c.MM(p[:, :], Vc[:, mt, hq * 256 + dd * 128: hq * 256 + (dd + 1) * 128], PTc[mt][:], mt == 0, mt == 1, [Vc, PTc[mt]], [p])
                    c.TT("dve", ocb[:, 2 * hq + dd, :], p[:, :], rden[:], ALU.mult, [p, rden], [ocb])
            for n in range(8):
                p = nextp()
                for k in range(8):
                    c.MM(p[:, :], Wo2[:, k, n * 128:(n + 1) * 128], ocb[:, k, :], k == 0, k == 7, [Wo2, ocb], [p])
                c.TT("dve", xs[:, n, :], p[:, :], xs[:, n, :], ALU.add, [p, xsT], [xsT])
            c.dma("sp", dr["x2_fm"].rearrange("k p t -> p k t")[:, :, tsl], xs, "st_" + xsT.name, R=[xsT])
        c.emit()


def phase_D2(nc, c, dr, T):
    NB = T // 512
    with contextlib.ExitStack() as st:
        P = Pool_(nc, st)
        vec = P.sb("vec", [128, NVEC], F32)
        c.dma("sp", vec[:], dr["vecs"], "vec", W=[vec])
        V = lambda n, j=0: vec[:, VEC_OFF[n] + j: VEC_OFF[n] + j + 1]
        ones_b = P.sb("ones_b", [128, 128], BF16)
        c.op("pool", lambda e: e.memset(ones_b[:], 1.0), W=[ones_b])
        epsc = P.sb("epsc", [128, 1], F32)
        c.op("pool", lambda e: e.memset(epsc[:], NORM_EPS), W=[epsc])
        xs2 = [P.sb("xs", [128, 4096], F32) for _ in range(2)]
        stg = xs2
        Wup = P.sb("Wup", [128, 8, 2 * DFF], BF16)
        for k in range(8):
            for half in range(2):
                s_ = stg[(2 * k + half) % 2]
                c.dma("sp", s_[:, 0:DFF], dr["w_up"][k][:, half * DFF:(half + 1) * DFF], s_.name, W=[s_])
                c.TS("dve", Wup[:, k, half * DFF:(half + 1) * DFF], s_[:, 0:DFF], vec[:, VEC_OFF["norm_ffn"] + k: VEC_OFF["norm_ffn"] + k + 1], ALU.mult, [s_, vec], [Wup])
        pp = [P.ps("pp", [128, 512]) for _ in range(6)]
        ppi = [0]

        def nextp():
            ppi[0] += 1
            return pp[ppi[0] % 6]
        sqb = P.sb("sqb", [128, 8, 512], BF16)
        xbs = [P.sb("xb", [128, 8, 512], BF16) for _ in range(2)]
        rstds = [P.sb("rstd", [128, 512], F32) for _ in range(2)]
        G = [P.sb("G", [128, 514], F32) for _ in range(3)]
        H = P.sb("H", [128, 22, 2], F32)
        c.op("pool", lambda e: e.memset(H[:], 0.0), W=[H])
        t1 = [P.sb("t1", [128, 512], F32) for _ in range(3)]
        sl = [P.sb("sl", [128, 512], F32) for _ in range(3)]
        ao = [P.sb("ao", [128, 512], BF16) for _ in range(3)]
        prev = None

        def fin(a_, pv, s_, j, tsl):
            c.TT("dve", a_[:], pv[:, :], s_[:], ALU.mult, [pv, s_], [a_])
            c.dma("sp", dr["a_fm"][j, :, tsl], a_[:], "st_" + a_.name, R=[a_])

        def loadx(b):
            tsl = slice(b * 512, (b + 1) * 512)
            x_ = xs2[b % 2]
            c.dma("sp", x_[:].rearrange("p (k t) -> p k t", t=512), dr["x2_fm"].rearrange("k p t -> p k t")[:, :, tsl], x_.name, W=[x_])

        def prologue(b):
            x_ = xs2[b % 2]
            x3 = x_[:].rearrange("p (k t) -> p k t", t=512)
            p = nextp()
            for k in range(8):
                c.ACT(sqb[:, k, :], x3[:, k, :], AF.Square, [x_], [sqb])
                yield
            for k in range(8):
                c.MM(p[:, :], ones_b[:], sqb[:, k, :], k == 0, k == 7, [ones_b, sqb], [p])
            c.RPOW(rstds[b % 2][:], p[:, :], [p], [rstds[b % 2]], -0.5, scale=1.0 / D, bias=epsc[:, 0:1], RB=[epsc])
            yield
            for k in range(8):
                c.TT("dve", xbs[b % 2][:, k, :], x3[:, k, :], rstds[b % 2][:], ALU.mult, [x_, rstds[b % 2]], [xbs[b % 2]])
                yield

        loadx(0)
        for _ in prologue(0):
            pass
        pro = None
        for b in range(NB):
            tsl = slice(b * 512, (b + 1) * 512)
            xb = xbs[b % 2]
            if b + 1 < NB:
                loadx(b + 1)
            for j in range(22):
                if j == 4 and b + 1 < NB:
                    pro = prologue(b + 1)
                if pro is not None:
                    try:
                        next(pro)
                    except StopIteration:
                        pro = None
                r3 = (b * 22 + j) % 3
                g_, t_, s_, a_ = G[r3], t1[r3], sl[r3], ao[r3]
                pg = nextp()
                for k in range(8):
                    c.MM(pg[:, :], Wup[:, k, j * 128:(j + 1) * 128], xb[:, k, :], k == 0, k == 7, [Wup, xb], [pg])
                pv = nextp()
                for k in range(8):
                    c.MM(pv[:, :], Wup[:, k, DFF + j * 128: DFF + (j + 1) * 128], xb[:, k, :], k == 0, k == 7, [Wup, xb], [pv])
                c.CP("act", g_[:, 0:2], H[:, j, :], [H], [g_])
                c.CP("act", g_[:, 2:514], pg[:, :], [pg], [g_])
                c.CP("act", H[:, j, :], g_[:, 512:514], [g_], [H])
                c.TS("dve", t_[:], g_[:, 0:512], V("conv_w0", j), ALU.mult, [g_, vec], [t_], s2=V("conv_b", j), op1=ALU.add)
                c.STT(t_[:], g_[:, 1:513], V("conv_w1", j), t_[:], ALU.mult, ALU.add, [g_, t_, vec], [t_])
                c.STT(t_[:], g_[:, 2:514], V("conv_w2", j), t_[:], ALU.mult, ALU.add, [g_, t_, vec], [t_])
                c.ACT(s_[:], t_[:], AF.Silu, [t_], [s_])
                if prev is not None:
                    fin(*prev)
                prev = (a_, pv, s_, j, tsl)
        fin(*prev)
        c.emit()


def phase_D3(nc, c, dr, T, outT):
    NB = T // 512
    with contextlib.ExitStack() as st:
        P = Pool_(nc, st)
        vec = P.sb("vec", [128, NVEC], F32)
        c.dma("sp", vec[:], dr["vecs"], "vec", W=[vec])
        V = lambda n, j=0: vec[:, VEC_OFF[n] + j: VEC_OFF[n] + j + 1]
        ones_b = P.sb("ones_b", [128, 128], BF16)
        c.op("pool", lambda e: e.memset(ones_b[:], 1.0), W=[ones_b])
        epsc = P.sb("epsc", [128, 1], F32)
        c.op("pool", lambda e: e.memset(epsc[:], NORM_EPS), W=[epsc])
        Wdt = [P.sb("Wd", [128, 2, D], BF16) for _ in range(11)]
        for jj in range(11):
            c.dma("pool", Wdt[jj][:], dr["w_down"].rearrange("k p n -> p k n")[:, 2 * jj:2 * jj + 2, :], Wdt[jj].name, W=[Wdt[jj]])
        pp = [P.ps("pp", [128, 512]) for _ in range(6)]
        ppi = [0]

        def nextp():
            ppi[0] += 1
            return pp[ppi[0] % 6]
        xs = [P.sb("xs", [128, 8, 512], F32) for _ in range(2)]
        ab = [P.sb("ab", [128, 22, 512], BF16) for _ in range(2)]
        sqb = P.sb("sqb", [128, 8, 512], BF16)
        rstd = P.sb("rstd", [128, 512], F32)
        def loads(b):
            tsl = slice(b * 512, (b + 1) * 512)
            x_, a_ = xs[b % 2], ab[b % 2]
            c.dma("sp", x_[:], dr["x2_fm"].rearrange("k p t -> p k t")[:, :, tsl], x_.name, W=[x_])
            for j0 in range(0, 22, 11):
                c.dma("sp", a_[:, j0:j0 + 11, :], dr["a_fm"].rearrange("k p t -> p k t")[:, j0:j0 + 11, tsl], a_.name + "_%d" % j0, W=[a_])

        loads(0)
        for b in range(NB):
            tsl = slice(b * 512, (b + 1) * 512)
            x_, a_ = xs[b % 2], ab[b % 2]
            if b + 1 < NB:
                loads(b + 1)
            for n in range(8):
                p = nextp()
                for j in range(22):
                    c.MM(p[:, :], Wdt[j // 2][:, j % 2, n * 128:(n + 1) * 128], a_[:, j, :], j == 0, j == 21, [Wdt[j // 2], a_], [p])
                c.TT("dve", x_[:, n, :], p[:, :], x_[:, n, :], ALU.add, [p, x_], [x_])
            p = nextp()
            rms_rstd(c, x_, sqb, ones_b, p, rstd, epsc)
            for n in range(8):
                c.STT(x_[:, n, :], x_[:, n, :], V("norm_final", n), rstd[:], ALU.mult, ALU.mult, [x_, rstd, vec], [x_])
            c.dma("sp", outT.rearrange("k p t -> p k t")[:, :, tsl], x_[:], "st_" + x_.name, R=[x_])
        c.emit()
```
